# Optimizing a Trainium2 kernel written in Bass

```python
import jax, jax.numpy as jnp
from jax import lax
import numpy as np

D_MODEL = 2048
BATCH = 2
SEQ = 4096
DEPTH = 2
DEC_BATCH = 128
DEC_SEQ = 8
PAST_LEN = 8192
PAGE_SIZE = 128

N_HEADS = 16
N_KV_HEADS = 4
HEAD_DIM = 64
ATTN_WIDTH = N_HEADS * HEAD_DIM
KV_WIDTH = N_KV_HEADS * HEAD_DIM
WINDOW = 128
BLOCK = 128
CONV_CH = D_MODEL // 2
CONV_WIDTH = 31
IN_COLS = ATTN_WIDTH + 2 * KV_WIDTH + 2 * CONV_CH
MIX_OUT = ATTN_WIDTH + CONV_CH
POOL_WINDOWS = (2, 4, 8, 16)
N_POOL_GROUPS = 4
POOL_GROUP = D_MODEL // N_POOL_GROUPS
POOL_STATE = max(POOL_WINDOWS) - 1
D_FF = -(-8 * D_MODEL // (3 * 256)) * 256
N_EVEN = (DEPTH + 1) // 2
N_ODD = DEPTH // 2
RMS_EPS = 1e-6
LN_EPS = 1e-5
NEG = -1e30

kernel_name = 'swa_sink_conformer_pool_hybrid_step'


def _rmsnorm(x, g):
    xf = x.astype(jnp.float32)
    y = xf * lax.rsqrt(jnp.mean(xf * xf, axis=-1, keepdims=True) + RMS_EPS)
    return (y * g.astype(jnp.float32)).astype(x.dtype)


def _layernorm(x, g, b):
    xf = x.astype(jnp.float32)
    xc = xf - jnp.mean(xf, axis=-1, keepdims=True)
    var = jnp.mean(xc * xc, axis=-1, keepdims=True)
    return (xc * lax.rsqrt(var + LN_EPS) * g.astype(jnp.float32) + b.astype(jnp.float32)).astype(x.dtype)


def _swiglu(h, w_gate, w_up, w_down):
    return (jax.nn.silu(h @ w_gate) * (h @ w_up)) @ w_down


def _sink_window_attention(q, k, v, qpos, kpos, sinks):
    n, tq = q.shape[:2]
    g = N_HEADS // N_KV_HEADS
    qg = q.reshape(n, tq, N_KV_HEADS, g, HEAD_DIM)
    s = jnp.einsum('ntkgd,nskd->nkgts', qg, k, preferred_element_type=jnp.float32) * (HEAD_DIM ** -0.5)
    diff = qpos[:, :, None] - kpos[:, None, :]
    ok = (diff >= 0) & (diff < WINDOW) & (kpos[:, None, :] >= 0)
    s = jnp.where(ok[:, None, None], s, NEG)
    sink = sinks.astype(jnp.float32).reshape(1, N_KV_HEADS, g, 1, 1)
    m = jnp.maximum(jnp.max(s, axis=-1, keepdims=True), sink)
    p = jnp.exp(s - m)
    denom = jnp.sum(p, axis=-1, keepdims=True) + jnp.exp(sink - m)
    o = jnp.einsum('nkgts,nskd->ntkgd', p / denom, v.astype(jnp.float32))
    return o.reshape(n, tq, ATTN_WIDTH).astype(q.dtype)


def _band_blocks(t):
    b, s = t.shape[:2]
    nb = s // BLOCK
    cur = t.reshape(b, nb, BLOCK, *t.shape[2:])
    prev = jnp.concatenate([jnp.zeros_like(cur[:, :1]), cur[:, :-1]], axis=1)
    return jnp.concatenate([prev, cur], axis=2).reshape(b * nb, 2 * BLOCK, *t.shape[2:])


def _prompt_positions(b, s):
    nb = s // BLOCK
    qpos = jnp.arange(s, dtype=jnp.int32).reshape(nb, BLOCK)
    kpos = jnp.arange(nb, dtype=jnp.int32)[:, None] * BLOCK - BLOCK + jnp.arange(2 * BLOCK, dtype=jnp.int32)[None, :]
    qpos = jnp.broadcast_to(qpos[None], (b, nb, BLOCK)).reshape(b * nb, BLOCK)
    kpos = jnp.broadcast_to(kpos[None], (b, nb, 2 * BLOCK)).reshape(b * nb, 2 * BLOCK)
    return qpos, kpos


def _depthwise_causal_conv(xp, w, b):
    y = lax.conv_general_dilated(xp, w.astype(xp.dtype)[:, None, :], window_strides=(1,), padding='VALID',
                                 dimension_numbers=('NWC', 'WIO', 'NWC'), feature_group_count=xp.shape[-1])
    return y + b.astype(y.dtype)


def _even_mixer(h, w_in, q_gain, k_gain, sinks, conv_w, conv_b, ln_g, ln_b, w_out, k_past, v_past, conv_past, past_len):
    n, t, _ = h.shape
    z = h @ w_in
    q, k, v, u = jnp.split(z, [ATTN_WIDTH, ATTN_WIDTH + KV_WIDTH, ATTN_WIDTH + 2 * KV_WIDTH], axis=-1)
    q = _rmsnorm(q.reshape(n, t, N_HEADS, HEAD_DIM), q_gain)
    k = _rmsnorm(k.reshape(n, t, N_KV_HEADS, HEAD_DIM), k_gain)
    v = v.reshape(n, t, N_KV_HEADS, HEAD_DIM)
    if k_past is None:
        nb = t // BLOCK
        qpos, kpos = _prompt_positions(n, t)
        attn = _sink_window_attention(q.reshape(n * nb, BLOCK, N_HEADS, HEAD_DIM), _band_blocks(k), _band_blocks(v), qpos, kpos, sinks)
        attn = attn.reshape(n, t, ATTN_WIDTH)
        k_all, v_all = k, v
        conv_past = jnp.zeros((n, CONV_WIDTH - 1, CONV_CH), h.dtype)
    else:
        k_all = jnp.concatenate([k_past.astype(k.dtype), k], axis=1)
        v_all = jnp.concatenate([v_past.astype(v.dtype), v], axis=1)
        qpos = jnp.broadcast_to(past_len + jnp.arange(t, dtype=jnp.int32), (n, t))
        kpos = jnp.broadcast_to(past_len - WINDOW + jnp.arange(WINDOW + t, dtype=jnp.int32), (n, WINDOW + t))
        attn = _sink_window_attention(q, k_all, v_all, qpos, kpos, sinks)
    new_k = k_all[:, -WINDOW:]
    new_v = v_all[:, -WINDOW:]
    a, gate = jnp.split(u, 2, axis=-1)
    glu = a * jax.nn.sigmoid(gate)
    xc = jnp.concatenate([conv_past.astype(glu.dtype), glu], axis=1)
    c = jax.nn.silu(_layernorm(_depthwise_causal_conv(xc, conv_w, conv_b), ln_g, ln_b))
    out = jnp.concatenate([attn, c], axis=-1) @ w_out
    return out, new_k, new_v, xc[:, -(CONV_WIDTH - 1):]


def _pool_mixer(h, pool_w, pool_scale, pool_past, start_pos):
    n, t, d = h.shape
    if pool_past is None:
        pool_past = jnp.zeros((n, POOL_STATE, d), h.dtype)
    xp = jnp.concatenate([pool_past.astype(h.dtype), h], axis=1)
    cs = jnp.cumsum(xp.astype(jnp.float32), axis=1)
    cs = jnp.concatenate([jnp.zeros((n, 1, d), jnp.float32), cs], axis=1)
    end = cs[:, POOL_STATE + 1:]
    pos = start_pos + jnp.arange(t, dtype=jnp.int32)
    hf = h.astype(jnp.float32)
    groups = []
    for gi, w in enumerate(POOL_WINDOWS):
        sl = slice(gi * POOL_GROUP, (gi + 1) * POOL_GROUP)
        start = cs[:, POOL_STATE + 1 - w: POOL_STATE + 1 - w + t, sl]
        count = jnp.minimum(pos + 1, w).astype(jnp.float32)[:, None]
        groups.append((end[..., sl] - start) / count - hf[..., sl])
    dpool = jnp.stack(groups, axis=2).astype(h.dtype)
    y = jnp.einsum('ntgc,gce->ntge', dpool, pool_w).reshape(n, t, d)
    return y * pool_scale.astype(y.dtype), xp[:, -POOL_STATE:]


def setup_inputs(seed: int = 0) -> dict:
    key = jax.random.key(seed)
    ks = jax.random.split(key, 22)
    nrm = lambda k, shape, s: jax.random.normal(k, shape, jnp.float32) * s
    return {
        'x_prompt': nrm(ks[0], (BATCH, SEQ, D_MODEL), 1.0),
        'x_sample': nrm(ks[1], (DEC_BATCH, DEC_SEQ, D_MODEL), 1.0),
        'cache_k': nrm(ks[2], (N_EVEN, DEC_BATCH, WINDOW, N_KV_HEADS, HEAD_DIM), 1.0),
        'cache_v': nrm(ks[3], (N_EVEN, DEC_BATCH, WINDOW, N_KV_HEADS, HEAD_DIM), 1.0),
        'state_conv': nrm(ks[4], (N_EVEN, DEC_BATCH, CONV_WIDTH - 1, CONV_CH), 0.5),
        'state_pool': nrm(ks[5], (N_ODD, DEC_BATCH, POOL_STATE, D_MODEL), 1.0),
        'norm_mix': 1.0 + nrm(ks[6], (DEPTH, D_MODEL), 0.1),
        'w_in': nrm(ks[7], (N_EVEN, D_MODEL, IN_COLS), D_MODEL ** -0.5),
        'q_norm': 1.0 + nrm(ks[8], (N_EVEN, HEAD_DIM), 0.1),
        'k_norm': 1.0 + nrm(ks[9], (N_EVEN, HEAD_DIM), 0.1),
        'sinks': nrm(ks[10], (N_EVEN, N_HEADS), 0.5),
        'conv_w': nrm(ks[11], (N_EVEN, CONV_WIDTH, CONV_CH), CONV_WIDTH ** -0.5),
        'conv_b': nrm(ks[12], (N_EVEN, CONV_CH), 0.02),
        'conv_ln_g': 1.0 + nrm(ks[13], (N_EVEN, CONV_CH), 0.1),
        'conv_ln_b': nrm(ks[14], (N_EVEN, CONV_CH), 0.02),
        'w_out': nrm(ks[15], (N_EVEN, MIX_OUT, D_MODEL), MIX_OUT ** -0.5),
        'pool_w': nrm(ks[16], (N_ODD, N_POOL_GROUPS, POOL_GROUP, POOL_GROUP), POOL_GROUP ** -0.5),
        'pool_scale': 1.0 + nrm(ks[17], (N_ODD, D_MODEL), 0.1),
        'norm_ffn': 1.0 + nrm(ks[18], (DEPTH, D_MODEL), 0.1),
        'w_gate': nrm(ks[19], (DEPTH, D_MODEL, D_FF), D_MODEL ** -0.5),
        'w_up': nrm(ks[20], (DEPTH, D_MODEL, D_FF), D_MODEL ** -0.5),
        'w_down': nrm(ks[21], (DEPTH, D_FF, D_MODEL), D_FF ** -0.5),
    }


def reference(x_prompt, x_sample, cache_k, cache_v, state_conv, state_pool, norm_mix, w_in, q_norm, k_norm,
              sinks, conv_w, conv_b, conv_ln_g, conv_ln_b, w_out, pool_w, pool_scale, norm_ffn, w_gate, w_up, w_down):
    xp, xs = x_prompt, x_sample
    kp_l, vp_l, cp_l, pp_l = [], [], [], []
    ks_l, vs_l, cs_l, ps_l = [], [], [], []
    for layer in range(DEPTH):
        i = layer // 2
        hp = _rmsnorm(xp, norm_mix[layer])
        hs = _rmsnorm(xs, norm_mix[layer])
        if layer % 2 == 0:
            params = (w_in[i], q_norm[i], k_norm[i], sinks[i], conv_w[i], conv_b[i], conv_ln_g[i], conv_ln_b[i], w_out[i])
            op, kp, vp, cp = _even_mixer(hp, *params, None, None, None, 0)
            osm, kq, vq, cq = _even_mixer(hs, *params, cache_k[i], cache_v[i], state_conv[i], PAST_LEN)
            kp_l.append(kp); vp_l.append(vp); cp_l.append(cp)
            ks_l.append(kq); vs_l.append(vq); cs_l.append(cq)
        else:
            op, pp = _pool_mixer(hp, pool_w[i], pool_scale[i], None, 0)
            osm, pq = _pool_mixer(hs, pool_w[i], pool_scale[i], state_pool[i], PAST_LEN)
            pp_l.append(pp); ps_l.append(pq)
        xp = xp + op
        xs = xs + osm
        xp = xp + _swiglu(_rmsnorm(xp, norm_ffn[layer]), w_gate[layer], w_up[layer], w_down[layer])
        xs = xs + _swiglu(_rmsnorm(xs, norm_ffn[layer]), w_gate[layer], w_up[layer], w_down[layer])
    new_k_prompt = jnp.stack(kp_l)
    new_v_prompt = jnp.stack(vp_l)
    new_conv_prompt = jnp.stack(cp_l)
    new_pool_prompt = jnp.stack(pp_l)
    new_k_sample = jnp.stack(ks_l)
    new_v_sample = jnp.stack(vs_l)
    new_conv_sample = jnp.stack(cs_l)
    new_pool_sample = jnp.stack(ps_l)
    return (xp, xs, new_k_prompt, new_v_prompt, new_conv_prompt, new_pool_prompt,
            new_k_sample, new_v_sample, new_conv_sample, new_pool_sample)
```

```python
import numpy as np
import concourse.bass as bass
import concourse.mybir as mybir
from concourse.bass_utils import run_bass_kernel_spmd

F32 = mybir.dt.float32
BF16 = mybir.dt.bfloat16
AF = mybir.ActivationFunctionType
ALU = mybir.AluOpType
AX = mybir.AxisListType

D = 2048
KC = 16
DFF = 5632
NE, NH, NM, NS = 128, 16, 1024, 128
TX = NH + NM + NS
TH = NE + TX
NSEQ = 16
RMS_EPS = 1e-6
LN_EPS = 1e-5
NMASK = 21
QUARTERS = [(0, 12), (12, 12), (24, 10), (34, 10)]
CONV_DVE_TAPS = 31


def _flat(xs):
    for x in xs:
        if x is None:
            continue
        if isinstance(x, (list,)):
            yield from _flat(x)
        elif isinstance(x, tuple) and len(x) == 2 and isinstance(x[1], int):
            yield x
        else:
            yield from _flat(list(x))


class Eng:
    def __init__(self, P, name, h):
        self.P, self.name, self.h = P, name, h
        self.seen = {}
        self.sem = None
        self.cnt = 0

    def _newsem(self):
        self.sem = self.P.newsem("e" + self.name)
        self.cnt = 0

    def sig(self, ins):
        if self.sem is None or self.cnt >= 3000:
            self._newsem()
        ins.then_inc(self.sem, 1)
        self.cnt += 1
        return (self.sem, self.cnt)

    def wait(self, *toks):
        for sem, c in _flat(toks):
            if self.seen.get(sem.name, 0) >= c:
                continue
            self.h.wait_ge(sem, c)
            self.seen[sem.name] = c


class DSem:
    def __init__(self, P, name):
        self.sem = P.newsem("d" + name)
        self.cnt = 0

    def add(self, ins):
        ins.then_inc(self.sem, 16)
        self.cnt += 16
        return (self.sem, self.cnt)


class Region:
    def __init__(self, P, name, nfloat):
        self.t = P.nc.alloc_sbuf_tensor(name, [128, nfloat], F32)
        self.n = nfloat
        self.o = 0

    def reset(self, o=0):
        self.o = o

    def carve(self, shape, dt):
        n = int(np.prod(shape[1:]))
        nf = (n * (2 if dt == BF16 else 4) + 3) // 4
        nf = (nf + 7) // 8 * 8
        assert self.o + nf <= self.n, (self.o, nf, self.n, shape)
        v = self.t[:shape[0], self.o:self.o + nf]
        self.o += nf
        if dt == BF16:
            v = v.bitcast(BF16)
        v = v[:, 0:n]
        if len(shape) == 3:
            v = v.rearrange("p (a b) -> p a b", a=shape[1])
        elif len(shape) == 4:
            v = v.rearrange("p (a b c) -> p a b c", a=shape[1], b=shape[2])
        return v


class _Stop(Exception):
    pass


class Prog:
    def __init__(self, debug=()):
        self.debug = set(debug)
        self.upto = None
        for d in debug:
            if d.startswith("upto:"):
                self.upto = d[5:]
        self.nsem = 0
        nc = self.nc = bass.Bass("TRN2", target_bir_lowering=False)
        self.PE = Eng(self, "pe", nc.tensor)
        self.ACT = Eng(self, "act", nc.scalar)
        self.DVE = Eng(self, "dve", nc.vector)
        self.POOL = Eng(self, "pool", nc.gpsimd)
        self.SP = Eng(self, "sp", nc.sync)
        self.out_sems = []
        self.new_out_sem()
        self.dbg_outs = {}
        self.in_names = set()
        try:
            self.build()
        except _Stop:
            pass
        self.SP.wait([(d.sem, d.cnt) for d in self.out_sems if d.cnt])

    def phase_end(self, name):
        if self.upto == name:
            raise _Stop()

    def new_out_sem(self):
        tok = None
        if self.out_sems:
            tok = (self.out_sem.sem, self.out_sem.cnt) if self.out_sem.cnt else None
        self.out_sem = DSem(self, f"out{len(self.out_sems)}")
        self.out_sems.append(self.out_sem)
        return tok

    def newsem(self, name):
        self.nsem += 1
        return self.nc.semaphore(f"{name}_{self.nsem}").__enter__()

    def din(self, name, shape, dt=F32):
        self.in_names.add(name)
        return self.nc.dram_tensor(name, list(shape), dt, kind="ExternalInput").ap()

    def dout(self, name, shape, dt=F32):
        return self.nc.dram_tensor(name, list(shape), dt, kind="ExternalOutput").ap()

    def op(self, eng, deps, fn, *a, **kw):
        eng.wait(deps)
        return eng.sig(fn(*a, **kw))

    def V(self, deps, name, *a, **kw):
        return self.op(self.DVE, deps, getattr(self.nc.vector, name), *a, **kw)

    def G(self, deps, name, *a, **kw):
        return self.op(self.POOL, deps, getattr(self.nc.gpsimd, name), *a, **kw)

    def A(self, deps, name, *a, **kw):
        return self.op(self.ACT, deps, getattr(self.nc.scalar, name), *a, **kw)

    def store(self, deps, out, in_, dsem=None):
        self.SP.wait(deps)
        return (dsem or self.out_sem).add(self.nc.sync.dma_start(out=out, in_=in_))

    def load(self, q, dsem, deps, out, in_):
        q.wait(deps)
        return dsem.add(q.h.dma_start(out=out, in_=in_))

    def dump(self, name, ap, deps):
        if name not in self.debug:
            return
        d = self.dout("dbg_" + name, list(ap.shape), ap.dtype)
        self.dbg_outs[name] = d
        self.store(deps, d, ap)

    def bank(self, b, n=1):
        return self.PS[:, 512 * b:512 * (b + n)]

    def wslab(self, src3, nchunk, ncol):
        s = self.wn % 4
        self.wn += 1
        self.POOL.wait(self.wfree[s])
        self.wfree[s] = []
        v = self.wslot[s][:, 0:nchunk * ncol].rearrange("p (a b) -> p a b", a=nchunk)
        tok = self.wsem[s].add(self.nc.gpsimd.dma_start(out=v, in_=src3))
        return v, tok, s

    def wslab2(self, src3, nchunk, ncol):
        s = self.wn % 4
        assert s % 2 == 0
        self.wn += 2
        self.POOL.wait(self.wfree[s], self.wfree[s + 1])
        self.wfree[s] = []
        self.wfree[s + 1] = []
        v = self.RW.t[:, 2048 * s:2048 * (s + 2)].bitcast(BF16)[:, 0:nchunk * ncol].rearrange("p (a b) -> p a b", a=nchunk)
        tok = self.wsem[s].add(self.nc.gpsimd.dma_start(out=v, in_=src3))
        return v, tok, s

    def wrelease(self, s, tok):
        self.wfree[s].append(tok)

    def build(self):
        nc = self.nc
        P = self
        xp = P.din("xp", [NE + NH + NM, D])
        xs = P.din("xs", [NS, D])
        ck = P.din("ck", [NSEQ, 128, 256])
        cv = P.din("cv", [NSEQ, 128, 256])
        sconv = P.din("sconv", [NSEQ, 30, 1024])
        spool = P.din("spool", [NSEQ, 15, D])
        vecs = P.din("vecs", [104, 128])
        convw = P.din("convw", [31, 1024])
        qkg = P.din("qkg", [2, 64])
        sinks = P.din("sinks", [16])
        w_in = P.din("w_in", [D, 3584])
        w_out = P.din("w_out", [D, D])
        pool_w = P.din("pool_w", [4, 512, 512])
        masks = P.din("masks", [128, NMASK, 128])
        cst = P.din("cst", [128, 128 + 128 + 64 + 1])

        y_p = P.dout("y_p", [NM, D])
        y_s = P.dout("y_s", [NS, D])
        nk_p = P.dout("nk_p", [128, 256])
        nv_p = P.dout("nv_p", [128, 256])
        nc_p = P.dout("nc_p", [30, 1024])
        np_p = P.dout("np_p", [15, D])
        nk_s = P.dout("nk_s", [NSEQ, 128, 256])
        nv_s = P.dout("nv_s", [NSEQ, 128, 256])
        nc_s = P.dout("nc_s", [NSEQ, 30, 1024])
        np_s = P.dout("np_s", [NSEQ, 15, D])

        RX = Region(P, "RX", TX * KC)
        RH = Region(P, "RH", TH * KC // 2)
        RA = Region(P, "RA", 10240)
        RW = Region(P, "RW", 8192)
        self.RW = RW
        RC = Region(P, "RC", 3072)
        self.PS = nc.alloc_psum_tensor("PS", [128, 4096], F32)
        PSb = lambda b, n=1: self.bank(b, n)
        self.wslot = [RW.t[:, 2048 * s:2048 * (s + 1)].bitcast(BF16) for s in range(4)]
        self.wsem = [DSem(P, f"w{s}") for s in range(4)]
        self.wfree = [[] for _ in range(4)]
        self.wn = 0

        identf = RC.carve([128, 128], F32)
        onesf = RC.carve([128, 128], F32)
        invc = RC.carve([128, 4, 16], F32)
        hvalid = RC.carve([128, 1], F32)
        identb = RC.carve([128, 128], BF16)
        onesb = RC.carve([128, 128], BF16)
        maskb = RC.carve([128, NMASK, 128], BF16)
        vT = RC.carve([128, 104], F32)
        cwT = RC.carve([128, 8, 31], F32)
        qg_bc = RC.carve([128, 64], F32)
        kg_bc = RC.carve([128, 64], F32)
        esink = RC.carve([128, 16], F32)
        sm = RC.carve([128, 64], F32)
        cld = DSem(P, "cld")
        cstt = RX.carve([128, 321], F32)
        vecl = RX.carve([104, 128], F32)
        cwl = RX.carve([31, 1024], F32)
        t1 = P.load(P.SP, cld, [], cstt, cst[:, :])
        P.load(P.SP, cld, [], vecl, vecs[:, :])
        P.load(P.SP, cld, [], cwl, convw[:, :])
        P.load(P.SP, cld, [], qg_bc, qkg[0].partition_broadcast(128))
        P.load(P.SP, cld, [], kg_bc, qkg[1].partition_broadcast(128))
        tl = P.load(P.SP, cld, [], esink, sinks.partition_broadcast(128))
        mld = DSem(P, "mld")
        tm = P.load(P.POOL, mld, [], maskb, masks[:, :, :])
        c1 = P.V([tl], "tensor_copy", identf, cstt[:, 0:128])
        P.V([], "tensor_copy", identb, cstt[:, 0:128])
        P.V([], "tensor_scalar", onesf, cstt[:, 128:256], 1.0 / 1024, 0.0, ALU.mult, ALU.add)
        P.V([], "tensor_scalar", onesb, cstt[:, 128:256], 1.0 / 2048, 0.0, ALU.mult, ALU.add)
        P.V([], "tensor_copy", invc, cstt[:, 256:320].rearrange("p (a b) -> p a b", a=4))
        c2 = P.V([], "tensor_copy", hvalid, cstt[:, 320:321])
        a1 = P.A([tl], "activation", esink, esink, AF.Exp)
        a2 = P.A([], "mul", qg_bc, qg_bc, 0.125)
        P.PE.wait(c1, c2)
        i1 = nc.tensor.transpose(PSb(0)[:, 0:104], vecl, identf[0:104, 0:104])
        for c in range(8):
            i2 = nc.tensor.transpose(PSb(1)[:, c * 31:(c + 1) * 31], cwl[:, c * 128:(c + 1) * 128], identf[0:31, 0:31])
        tp = P.PE.sig(i2)
        P.V([tp], "tensor_copy", vT, PSb(0)[:, 0:104])
        cdone = P.V([], "tensor_copy", cwT, PSb(1)[:, 0:248].rearrange("p (a b) -> p a b", a=8))
        cdone = [cdone, a1, a2, tm, c2]
        V_NMIX, V_NFFN, V_PSC, V_CB, V_LG, V_LB = 0, 32, 64, 80, 88, 96
        RX.reset()

        xT = None
        hT = RH.carve([128, KC, TH], BF16)

        tiles = [(xp[128 * i:128 * i + 128, :], 128, 128 * i) for i in range(9)]
        tiles += [(xp[1152:1168, :], 16, 1152), (xs[:, :], 128, 1168)]

        xt = [RX.carve([128, D], F32) for _ in range(3)]
        hn = [RX.carve([128, D], BF16) for _ in range(3)]
        junk = RX.carve([128, D], BF16)
        xld = [DSem(P, f"x{i}") for i in range(3)]
        xfree = [[cdone], [cdone], [cdone]]
        hnfree = [[], [], []]
        bfree = [[cdone], [cdone], [cdone]]
        hT_tok = []
        c0 = {}

        def p0_load(n):
            src, nr, col = tiles[n]
            s = n % 3
            c0[n] = {"tld": P.load(P.SP, xld[s], xfree[s], xt[s][:nr], src)}

        def p0_sq(n):
            src, nr, col = tiles[n]
            s = n % 3
            c0[n]["tA"] = P.A([c0[n]["tld"]], "activation", junk[:nr], xt[s][:nr], AF.Square, accum_out=sm[:nr, 2 * s:2 * s + 1])

        def p0_v1(n):
            src, nr, col = tiles[n]
            s = n % 3
            c0[n]["tB"] = P.V([c0[n]["tA"]], "tensor_scalar", sm[:nr, 2 * s + 1:2 * s + 2], sm[:nr, 2 * s:2 * s + 1], 1.0 / D, RMS_EPS, ALU.mult, ALU.add)

        def p0_sqrt(n):
            src, nr, col = tiles[n]
            s = n % 3
            rs = sm[:nr, 2 * s + 1:2 * s + 2]
            c0[n]["tC"] = P.A([c0[n]["tB"]], "sqrt", rs, rs)

        def p0_v2(n):
            src, nr, col = tiles[n]
            s = n % 3
            rs = sm[:nr, 2 * s + 1:2 * s + 2]
            c0[n]["tD"] = P.V([c0[n]["tC"]], "reciprocal", rs, rs)

        def p0_mul(n):
            src, nr, col = tiles[n]
            s = n % 3
            tE = P.A([c0[n]["tD"], hnfree[s]], "mul", hn[s][:nr], xt[s][:nr], sm[:nr, 2 * s + 1:2 * s + 2])
            xfree[s] = [tE]
            c0[n]["tE"] = tE

        def p0_pe(n):
            src, nr, col = tiles[n]
            s = n % 3
            P.PE.wait(c0[n]["tE"], bfree[s], cdone)
            pb = PSb(2 * s, 2).bitcast(BF16).rearrange("p (a b) -> p a b", a=KC)
            for k in range(KC):
                ins = nc.tensor.transpose(pb[:, k, 0:nr], hn[s][:nr, k * 128:(k + 1) * 128], identb[:nr, :nr])
            tP = P.PE.sig(ins)
            hnfree[s] = [tP]
            c0[n]["tP"] = tP

        def p0_ev(n):
            src, nr, col = tiles[n]
            s = n % 3
            pb = PSb(2 * s, 2).bitcast(BF16).rearrange("p (a b) -> p a b", a=KC)
            tH = P.V([c0[n]["tP"]], "tensor_tensor", hT[:, :, col:col + nr], pb[:, :, 0:nr],
                     vT[:, V_NMIX:V_NMIX + 16].unsqueeze(2).to_broadcast([128, KC, nr]), ALU.mult)
            bfree[s] = [tH]
            hT_tok.append(tH)

        NT = len(tiles)
        ok = lambda n: 0 <= n < NT
        p0_load(0)
        p0_load(1)
        for i in range(NT + 2):
            if ok(i - 1):
                p0_sqrt(i - 1)
                p0_v2(i - 1)
                p0_mul(i - 1)
                p0_pe(i - 1)
            if ok(i - 2):
                p0_ev(i - 2)
            if ok(i + 2):
                p0_load(i + 2)
            if ok(i):
                p0_sq(i)
                p0_v1(i)
        P.dump("hT0", hT, hT_tok)
        P.phase_end("P0")
        RX.reset()

        w_in3 = w_in.rearrange("(c p) n -> p c n", p=128)
        conv_out = RX.carve([128, 8, TX], F32)
        glp = [RX.carve([128, 1072], F32) for _ in range(2)]
        gls = [RX.carve([128, NSEQ, 38], F32) for _ in range(2)]
        sg = RX.carve([128, 1200], F32)
        gsn = RX.carve([128, 128], F32)
        stc = [[RX.carve([120, 128], F32) for _ in range(4)] for _ in range(2)]
        cv2 = [RX.carve([128, TX], F32)] * 2
        NDG = 16
        dgr = [RX.carve([128, 128], F32) for _ in range(NDG)]
        dgfree = [[] for _ in range(NDG)]
        dgn = [0]
        b3free = []
        ncp = RA.carve([32, 1024], F32)
        ncs = RA.carve([128, 1024], F32)
        sld = [DSem(P, f"sld{i}") for i in range(2)]
        stfree = [[], []]
        cv2free = [[], []]
        TD = CONV_DVE_TAPS
        pieces = [(96, 512), (608, 512), (1120, 176)]
        slabs = {}
        glfree = [[], []]
        evfree = []
        b6free = []
        b7free = []
        nco_tok = []
        conv_tok = []
        def u_slabs(jj):
            return {'a': P.wslab(w_in3[:, :, 1536 + 256 * jj:1536 + 256 * jj + 256], KC, 256),
                    'g': P.wslab(w_in3[:, :, 2560 + 256 * jj:2560 + 256 * jj + 256], KC, 256)}
        nxt = u_slabs(0)
        for j in range(8):
            jj, dd = divmod(j, 2)
            if dd == 0:
                slabs = nxt
                if jj + 1 < 4:
                    nxt = u_slabs(jj + 1)
            s = j % 2
            for r in range(4):
                tst = P.load(P.SP, sld[s], stfree[s] if r == 0 else [], stc[s][r],
                             sconv[4 * r:4 * r + 4, :, j * 128:(j + 1) * 128].rearrange("n i c -> (n i) c"))
            wj = cwT[:, j, :]
            dg_tok = []

            def build_dg(n_):
                for _ in range(n_):
                    t_ = len(dg_tok)
                    if t_ >= 31:
                        return
                    r_ = dgn[0] % NDG
                    dgn[0] += 1
                    dg_tok.append((r_, P.A([dgfree[r_], cdone], "activation", dgr[r_], identf, AF.Copy, scale=wj[:, t_:t_ + 1])))
            build_dg(NDG)
            P.PE.wait(slabs['a'][1], slabs['g'][1], evfree, hT_tok)
            for k in range(KC):
                for pi, (c0, n) in enumerate(pieces):
                    ins = nc.tensor.matmul(PSb(pi)[:, 0:n], lhsT=slabs['g'][0][:, k, dd * 128:(dd + 1) * 128],
                                           rhs=hT[:, k, c0:c0 + n], start=(k == 0), stop=(k == KC - 1))
            tmmG = P.PE.sig(ins)
            tSs = []
            off = 0
            for pi, (c0, n) in enumerate(pieces):
                tSs.append(P.A([tmmG], "activation", sg[:, off:off + n], PSb(pi)[:, 0:n], AF.Sigmoid))
                off += n
            P.PE.wait(tSs)
            for k in range(KC):
                for pi, (c0, n) in enumerate(pieces):
                    ins = nc.tensor.matmul(PSb(pi)[:, 0:n], lhsT=slabs['a'][0][:, k, dd * 128:(dd + 1) * 128],
                                           rhs=hT[:, k, c0:c0 + n], start=(k == 0), stop=(k == KC - 1))
            tmm = P.PE.sig(ins)
            if dd == 1:
                P.wrelease(slabs['a'][2], tmm)
                P.wrelease(slabs['g'][2], tmm)
            P.PE.wait(tst, b6free)
            for r in range(4):
                ins = nc.tensor.transpose(PSb(6)[:, r * 120:(r + 1) * 120], stc[s][r], identf[0:120, 0:120])
            tst_t = P.PE.sig(ins)
            stfree[s] = [tst_t]
            tcp = P.V([tst_t, glfree[s]], "tensor_copy", gls[s][:, :, 0:30],
                      PSb(6)[:, 0:480].rearrange("p (a b) -> p a b", a=NSEQ))
            b6free = [tcp]
            tg0 = P.V([tmm, tSs[0], glfree[s]], "tensor_tensor", glp[s][:, 0:512], PSb(0)[:, 0:512], sg[:, 0:512], ALU.mult)
            tg1 = P.V([tSs[1]], "tensor_tensor", glp[s][:, 512:1024], PSb(1)[:, 0:512], sg[:, 512:1024], ALU.mult)
            tg2 = P.V([tSs[2]], "tensor_tensor", glp[s][:, 1024:1072], PSb(2)[:, 0:48], sg[:, 1024:1072], ALU.mult)
            tg3 = P.V([], "tensor_tensor", gsn, PSb(2)[:, 48:176], sg[:, 1072:1200], ALU.mult)
            evfree = [tg3]
            tg4 = P.V([tg3, tcp], "tensor_copy", gls[s][:, :, 30:38], gsn.rearrange("p (a b) -> p a b", a=NSEQ))
            P.PE.wait(tg2, tg3, b7free)
            nc.tensor.transpose(PSb(7)[0:32, 0:128], glp[s][:, 1040:1072], identf)
            ins = nc.tensor.transpose(PSb(7)[:, 128:256], gsn, identf)
            tno = P.PE.sig(ins)
            P.A([tno], "copy", ncp[:, j * 128:(j + 1) * 128], PSb(7)[0:32, 0:128])
            tnc = P.A([], "copy", ncs[:, j * 128:(j + 1) * 128], PSb(7)[:, 128:256])
            b7free = [tnc]
            nco_tok = [tnc]
            bj = vT[:, V_CB + j:V_CB + j + 1]
            P.PE.wait(tg4, b3free)
            for t in range(31):
                r, tk = dg_tok[t]
                P.PE.wait(tk)
                ins = nc.tensor.matmul(PSb(3)[:, 0:128], lhsT=dgr[r], rhs=gls[s][:, :, t:t + 8], start=(t == 0), stop=(t == 30))
                if t % 4 == 3 or t == 30:
                    tkd = P.PE.sig(ins)
                    for t2 in range(t - (t % 4), t + 1):
                        dgfree[dg_tok[t2][0]] = [tkd]
                    build_dg(4)
            tsv = P.A([tkd], "activation", conv_out[:, j, NH + NM:TX], PSb(3)[:, 0:128], AF.Identity, bias=bj)
            b3free = [tsv]
            yp = conv_out[:, j, 0:NH + NM]
            tpv = P.V([tg0, tg1, tg2], "tensor_scalar", yp, glp[s][:, 2:2 + 1040], wj[:, 0:1], bj, ALU.mult, ALU.add)
            if TD == 31:
                for t in range(1, 31):
                    tpv = P.V([tpv], "scalar_tensor_tensor", yp, glp[s][:, 2 + t:2 + t + 1040], wj[:, t:t + 1], yp, ALU.mult, ALU.add)
            for t in range(1, TD if TD < 31 else 0):
                tpv = P.V([tpv], "scalar_tensor_tensor", yp, glp[s][:, 2 + t:2 + t + 1040], wj[:, t:t + 1], yp, ALU.mult, ALU.add)
                tsv = P.V([tsv], "scalar_tensor_tensor", ysv, gls[s][:, :, t:t + 8], wj[:, t:t + 1], ysv, ALU.mult, ALU.add)
            if TD < 31:
                def prod(dst, t, extra):
                    ta_ = P.A([tg0, tg1, tg2, extra], "activation", dst[:, 0:NH + NM], glp[s][:, 2 + t:2 + t + 1040], AF.Copy, scale=wj[:, t:t + 1])
                    tb_ = P.A([tg4], "activation", dst[:, NH + NM:TX].rearrange("p (a b) -> p a b", a=NSEQ), gls[s][:, :, t:t + 8], AF.Copy,
                              scale=wj[:, t:t + 1])
                    return [ta_, tb_]
                tacc = prod(cv2[s], TD, cv2free[s])
                for t in range(TD + 1, 31):
                    k2 = t % 2
                    tpr = prod(ctmp[k2], t, ctfree[k2])
                    tacc = [P.G([tacc, tpr], "tensor_tensor", cv2[s], cv2[s], ctmp[k2], ALU.add)]
                    ctfree[k2] = tacc
                tpv = P.V([tpv, tsv, tacc], "tensor_tensor", conv_out[:, j, :], conv_out[:, j, :], cv2[s], ALU.add)
                tsv = tpv
                cv2free[s] = [tpv]
                cv2free[1 - s] = [tpv]
            glfree[s] = [tpv, tsv]
            conv_tok = [tpv, tsv]
        P.store(nco_tok, nc_p[:, :], ncp[2:32, :])
        P.store(nco_tok, nc_s[:, 22:30, :], ncs)
        P.store([], nc_s[:, 0:22, :], sconv[:, 8:30, :])
        nco_store = P.new_out_sem()
        P.dump("conv_out", conv_out, conv_tok)
        P.phase_end("P1")

        RA.reset()
        mixT = RA.carve([128, KC, TX], BF16)
        RX.reset(8 * TX)
        sqs = [RX.carve([128, TX], F32) for _ in range(2)]
        rsb = RX.carve([128, TX], F32)
        xpieces = [(0, 512), (512, 512), (1024, 144)]
        P.PE.wait(conv_tok, evfree)
        for j in range(8):
            for pi, (c0, n) in enumerate(xpieces):
                ins = nc.tensor.matmul(PSb(pi)[:, 0:n], lhsT=onesf, rhs=conv_out[:, j, c0:c0 + n], start=(j == 0), stop=(j == 7))
        tmean = P.PE.sig(ins)
        sqfree = [[], []]
        yc_tok = []
        for j in range(8):
            tks = []
            for pi, (c0, n) in enumerate(xpieces):
                tks.append(P.V([tmean], "tensor_tensor", conv_out[:, j, c0:c0 + n], conv_out[:, j, c0:c0 + n], PSb(pi)[:, 0:n], ALU.subtract))
            tq = P.A([tks, sqfree[j % 2]], "activation", sqs[j % 2], conv_out[:, j, :], AF.Square)
            P.PE.wait(tq)
            for pi, (c0, n) in enumerate(xpieces):
                ins = nc.tensor.matmul(PSb(3 + pi)[:, 0:n], lhsT=onesf, rhs=sqs[j % 2][:, c0:c0 + n], start=(j == 0), stop=(j == 7))
            sqfree[j % 2] = [P.PE.sig(ins)]
            yc_tok.append(tks)
        tvar = sqfree[1]
        tr = []
        for pi, (c0, n) in enumerate(xpieces):
            tr.append(P.V([tvar], "tensor_scalar", rsb[:, c0:c0 + n], PSb(3 + pi)[:, 0:n], 1.0, LN_EPS, ALU.mult, ALU.add))
        tr2 = P.A([tr], "sqrt", rsb, rsb)
        tr3 = P.V([tr2], "reciprocal", rsb, rsb)
        ln_tok = []
        for j in range(8):
            tz = P.V([tr3, yc_tok[j]], "tensor_tensor", conv_out[:, j, :], conv_out[:, j, :], rsb, ALU.mult)
            ln_tok.append(P.A([tz], "activation", mixT[:, 8 + j, :], conv_out[:, j, :], AF.Silu,
                              bias=vT[:, V_LB + j:V_LB + j + 1], scale=vT[:, V_LG + j:V_LG + j + 1]))
        ps_free = [tr, yc_tok[-1]]
        P.dump("cT", mixT[:, 8:16, :], ln_tok)
        P.phase_end("P2")
        RX.reset()

        qT = RX.carve([128, 8, TX], BF16)
        kT = RX.carve([128, 4, TH], BF16)
        vaug = RX.carve([128, 11, 4 * 80], BF16)
        sq3 = [RX.carve([128, 512], F32) for _ in range(4)]
        qn = [RX.carve([128, 512], F32) for _ in range(4)]
        qb = [RX.carve([128, 512], BF16) for _ in range(4)]
        kst = [RX.carve([128, 256], F32) for _ in range(3)]
        vst = [RX.carve([128, 256], F32) for _ in range(3)]
        p3free = [ln_tok, conv_tok, nco_store]
        tva = P.V([p3free], "memset", vaug, 1.0)
        bankfree = {b: [ps_free] for b in range(8)}
        qfree = [[tva], [tva], [tva], [tva]]
        qT_tok, kT_tok, v_tok = [], [], []
        groups = []
        for g in range(3):
            tl3 = list(range(1, 11)) if g < 2 else list(range(0, 11))
            for ti in tl3:
                groups.append({"g": g, "ti": ti, "first": ti == tl3[0], "last": ti == tl3[-1]})
        for n, c in enumerate(groups):
            c["n"] = n
            c["b"] = n % 4
            c["tb"] = 4 + n % 3
            c["s"] = n % 4
        slab3 = {}
        lastmm_box = [None]

        def s0(c):
            g, ti, b = c["g"], c["ti"], c["b"]
            src, nr, col = tiles[ti]
            if c["first"]:
                slab3[g] = P.wslab2(w_in3[:, :, 512 * g:512 * g + 512], KC, 512)
            sl2 = slab3[g]
            P.PE.wait(sl2[1], bankfree[b], hT_tok)
            for k in range(KC):
                ins = nc.tensor.matmul(PSb(b)[:nr, :], lhsT=hT[:, k, col:col + nr], rhs=sl2[0][:, k, :], start=(k == 0), stop=(k == KC - 1))
            c["tmm"] = P.PE.sig(ins)
            lastmm_box[0] = c["tmm"]
            if c["last"]:
                P.wrelease(sl2[2], c["tmm"])
                P.wrelease(sl2[2] + 1, c["tmm"])

        def s1(c):
            g, ti, b, s = c["g"], c["ti"], c["b"], c["s"]
            src, nr, col = tiles[ti]
            nhd = 8 if g < 2 else 4
            wd = nhd * 64
            c["t_a"] = P.A([c["tmm"], qfree[s]], "activation", sq3[s][:nr, 0:wd], PSb(b)[:nr, 0:wd], AF.Square)
            c["tvs"] = []
            if g == 2:
                oi = {8: 0, 9: 1, 10: 2}.get(ti, None)
                vsrc = PSb(b)[:nr, 256:512]
                t_v = P.A([], "copy", vaug[:nr, ti, :].rearrange("p (h e) -> p h e", e=80)[:, :, 0:64], vsrc.rearrange("p (h d) -> p h d", d=64))
                c["tvs"] = [t_v]
                v_tok.append(t_v)
                if oi is not None:
                    t_v2 = P.A([], "copy", vst[oi][:nr], vsrc)
                    c["tvs"].append(t_v2)
                    if ti == 8:
                        P.store([t_v2], nv_p[0:112, :], vst[0][16:128, :])
                    elif ti == 9:
                        P.store([t_v2], nv_p[112:128, :], vst[1][0:16, :])
                    else:
                        P.store([t_v2], nv_s[:, 120:128, :], vst[2])

        def s2(c):
            g, ti, s = c["g"], c["ti"], c["s"]
            src, nr, col = tiles[ti]
            nhd = 8 if g < 2 else 4
            wd = nhd * 64
            ssq = sm[:nr, 8 + 8 * s:8 + 8 * s + nhd]
            t_b = P.V([c["t_a"]], "tensor_reduce", ssq, sq3[s][:nr, 0:wd].rearrange("p (h d) -> p h d", d=64), AX.X, ALU.add)
            c["t_c"] = P.V([t_b], "tensor_scalar", ssq, ssq, 1.0 / 64, RMS_EPS, ALU.mult, ALU.add)

        def s3(c):
            g, ti, s = c["g"], c["ti"], c["s"]
            src, nr, col = tiles[ti]
            nhd = 8 if g < 2 else 4
            ssq = sm[:nr, 8 + 8 * s:8 + 8 * s + nhd]
            c["t_d"] = P.A([c["t_c"]], "sqrt", ssq, ssq)

        def s4(c):
            g, ti, b, s = c["g"], c["ti"], c["b"], c["s"]
            src, nr, col = tiles[ti]
            nhd = 8 if g < 2 else 4
            wd = nhd * 64
            ssq = sm[:nr, 8 + 8 * s:8 + 8 * s + nhd]
            t_e = P.V([c["t_d"]], "reciprocal", ssq, ssq)
            qn3 = qn[s][:nr, 0:wd].rearrange("p (h d) -> p h d", d=64)
            t_f = P.V([t_e], "tensor_tensor", qn3, PSb(b)[:nr, 0:wd].rearrange("p (h d) -> p h d", d=64),
                      ssq.unsqueeze(2).to_broadcast([nr, nhd, 64]), ALU.mult)
            bankfree[b] = [t_f, c["tvs"]]
            if g < 2:
                t_g = P.V([t_f], "tensor_tensor", qb[s][:nr, :].rearrange("p (h d) -> p h d", d=64), qn3,
                          qg_bc[:nr].unsqueeze(1).to_broadcast([nr, 8, 64]), ALU.mult)
                c["rdy"] = [t_g]
            else:
                oi = {8: 0, 9: 1, 10: 2}.get(ti, None)
                kdst = kst[oi] if oi is not None else qn[s][:, 256:512]
                kd3 = kdst[:nr].rearrange("p (h d) -> p h d", d=64)
                t_g = P.V([t_f], "tensor_tensor", kd3, qn3, kg_bc[:nr].unsqueeze(1).to_broadcast([nr, 4, 64]), ALU.mult)
                kd = qb[s][:nr, :].rearrange("p (h t d) -> p h t d", h=4, t=2)
                t_h = P.V([t_g], "tensor_copy", kd[:, :, 0, :], kd3)
                t_i = P.V([t_g], "tensor_copy", kd[:, :, 1, :], kd3)
                c["rdy"] = [t_h, t_i]
                if ti == 8:
                    P.store([t_g], nk_p[0:112, :], kst[0][16:128, :])
                elif ti == 9:
                    P.store([t_g], nk_p[112:128, :], kst[1][0:16, :])
                elif ti == 10:
                    P.store([t_g], nk_s[:, 120:128, :], kst[2])

        def s5(c):
            g, ti, tb, s = c["g"], c["ti"], c["tb"], c["s"]
            src, nr, col = tiles[ti]
            P.PE.wait(c["rdy"], bankfree[tb])
            pb = PSb(tb).bitcast(BF16)[:, 0:512].rearrange("p (a b) -> p a b", a=4)
            for cc in range(4):
                ins = nc.tensor.transpose(pb[:, cc, 0:nr], qb[s][:nr, cc * 128:(cc + 1) * 128], identb[:nr, :nr])
            c["t_t"] = P.PE.sig(ins)

        def s6(c):
            g, ti, tb, s = c["g"], c["ti"], c["tb"], c["s"]
            src, nr, col = tiles[ti]
            pb = PSb(tb).bitcast(BF16)[:, 0:512].rearrange("p (a b) -> p a b", a=4)
            if g < 2:
                t_q = P.A([c["t_t"]], "copy", qT[:, 4 * g:4 * g + 4, col - NE:col - NE + nr], pb[:, :, 0:nr])
                qT_tok.append(t_q)
            else:
                t_q = P.A([c["t_t"]], "copy", kT[:, :, col:col + nr], pb[:, :, 0:nr])
                kT_tok.append(t_q)
            bankfree[tb] = [t_q]
            qfree[s] = [c["t_t"], c["rdy"]]

        NG = len(groups)
        gok = lambda n: 0 <= n < NG
        for i in range(NG + 3):
            if gok(i):
                s0(groups[i])
            if gok(i - 3):
                s5(groups[i - 3])
                s6(groups[i - 3])
            if gok(i - 2):
                s3(groups[i - 2])
                s4(groups[i - 2])
            if gok(i - 1):
                s1(groups[i - 1])
                s2(groups[i - 1])
        lastmm = lastmm_box[0]
        P.store([], nk_s[:, 0:120, :], ck[:, 8:128, :])
        P.store([], nv_s[:, 0:120, :], cv[:, 8:128, :])
        P.dump("qT", qT, qT_tok)
        P.dump("kT", kT, kT_tok)
        P.dump("vaug", vaug, v_tok)
        P.phase_end("P3")
        allfree = [bankfree[b] for b in range(8)]

        pT = [RX.carve([128, 4, 128], BF16) for _ in range(4)]
        att = [RX.carve([128, 1024], BF16) for _ in range(2)]
        den = [RX.carve([128, 16], F32) for _ in range(2)]
        RH.reset()
        ckl = [RH.carve([128, 256], F32) for _ in range(4)]
        cvl = [RH.carve([128, 256], F32) for _ in range(4)]
        ckd = [RH.carve([128, 4, 2, 64], BF16) for _ in range(4)]
        kcT = [RH.carve([128, 4, 128], BF16) for _ in range(4)]
        vca = [RH.carve([128, 4 * 80], BF16) for _ in range(4)]
        cld2 = [DSem(P, f"ck{i}") for i in range(4)]
        tvc = P.V([lastmm], "memset", vca[0], 1.0)
        tvc = P.V([], "memset", vca[1], 1.0)
        tvc = P.V([], "memset", vca[2], 1.0)
        tvc = P.V([], "memset", vca[3], 1.0)
        qtiles = []
        for ti in range(1, 10):
            src, nr, col = tiles[ti]
            prev = tiles[ti - 1]
            qtiles.append((ti, nr, col - NE, [("p", ti - 1, 128, 2 if ti == 1 else 0), ("p", ti, nr, 3 if ti == 1 else 1)]))
        qtiles.append((10, 128, tiles[10][2] - NE, [("c", n, 128, 4 + n) for n in range(NSEQ)] + [("p", 10, 128, 20)]))
        sbank = 0
        sfree = [allfree] * 4
        pfree = [[], [], [], []]
        ofree = [allfree]
        tbfree = [allfree]
        cfree = [[tvc], [tvc], [tvc], [tvc]]
        ncache = 0
        units = []
        for qi, (ti, nq, xc, keys) in enumerate(qtiles):
            for ki, kspec in enumerate(keys):
                for kh in range(4):
                    units.append((qi, ti, nq, xc, ki, len(keys), kspec, kh))
        qk_tok = {}
        cache_ready = {}
        cache_late = set()

        cache_dma = {}

        def prep_dma(n):
            if n >= NSEQ or n in cache_dma:
                return
            s = n % 4
            t1 = P.load(P.SP, cld2[s], cfree[s], ckl[s], ck[n])
            cache_dma[n] = P.load(P.SP, cld2[s], [], cvl[s], cv[n])

        def prep_cache(n):
            if n >= NSEQ or n in cache_ready:
                return
            prep_dma(n)
            s = n % 4
            t2 = cache_dma[n]
            a = P.V([t2], "tensor_copy", ckd[s][:, :, 0, :], ckl[s].rearrange("p (h d) -> p h d", d=64))
            b_ = P.V([], "tensor_copy", ckd[s][:, :, 1, :], ckl[s].rearrange("p (h d) -> p h d", d=64))
            c_ = P.V([], "tensor_copy", vca[s].rearrange("p (h e) -> p h e", e=80)[:, :, 0:64], cvl[s].rearrange("p (h d) -> p h d", d=64))
            cache_ready[n] = (a, b_, c_)

        def prep_cache_late(n):
            if n >= NSEQ or n in cache_late:
                return
            cache_late.add(n)
            prep_cache(n)
            s = n % 4
            a, b_, c_ = cache_ready[n]
            P.PE.wait(a, b_, tbfree[0])
            pb = PSb(7).bitcast(BF16)[:, 0:512].rearrange("p (a b) -> p a b", a=4)
            for kh2 in range(4):
                ins = nc.tensor.transpose(pb[:, kh2, :], ckd[s][:, kh2].rearrange("p t d -> p (t d)"), identb)
            tt = P.PE.sig(ins)
            tq = P.A([tt], "copy", kcT[s], pb)
            tbfree[0] = [tq]
            cache_ready[n] = (tq, c_)

        def emit_qk(u):
            qi, ti, nq, xc, ki, nk, kspec, kh = units[u]
            b = u % 4
            kind, kidx, nkeys, mi = kspec
            deps = [sfree[b], qT_tok]
            if kind == "c":
                if kh == 0:
                    prep_cache_late(kidx)
                    prep_cache_late(kidx + 1)
                deps.append(cache_ready[kidx][0])
            else:
                deps.append(kT_tok)
            P.PE.wait(deps)
            sv4 = PSb(b)[:nkeys, :].rearrange("p (a b) -> p a b", a=4)
            for g2 in range(2):
                h = (kh // 2) * 8 + 4 * g2 + (kh % 2)
                half = h % 2
                ch = h // 2
                kvh = h // 4
                ps_ = slice(64 * half, 64 * half + 64)
                if kind == "c":
                    lhsT = kcT[kidx % 4][ps_, kvh, :]
                else:
                    kcol = tiles[kidx][2]
                    lhsT = kT[ps_, kvh, kcol:kcol + nkeys]
                if nq == 128:
                    ins = nc.tensor.matmul(PSb(b)[:nkeys, 256 * g2:256 * g2 + 256], lhsT=lhsT, rhs=qT[ps_, ch:ch + 2, xc:xc + nq],
                                           start=True, stop=True, skip_group_check=True)
                else:
                    for e in range(2):
                        ins = nc.tensor.matmul(sv4[:, 2 * g2 + e, 0:nq], lhsT=lhsT, rhs=qT[ps_, ch + e, xc:xc + nq],
                                               start=True, stop=True, skip_group_check=True)
            qk_tok[u] = P.PE.sig(ins)

        em_tok = {}

        def emit_em(u):
            qi, ti, nq, xc, ki, nk, kspec, kh = units[u]
            b = u % 4
            kind, kidx, nkeys, mi = kspec
            sv = PSb(b)[:nkeys, :].rearrange("p (a b) -> p a b", a=4)[:, :, 0:nq]
            pv = pT[b][:nkeys, :, 0:nq]
            te = P.A([qk_tok[u], pfree[b]], "activation", pv, sv, AF.Exp)
            sfree[b] = [te]
            em_tok[u] = P.V([te], "tensor_tensor", pv, pv, maskb[:nkeys, mi, 0:nq].unsqueeze(1).to_broadcast([nkeys, 4, nq]), ALU.mult)

        def emit_pv(u):
            qi, ti, nq, xc, ki, nk, kspec, kh = units[u]
            b = u % 4
            kind, kidx, nkeys, mi = kspec
            deps = [em_tok[u]]
            if ki == 0 and kh == 0:
                deps.append(ofree[0])
            if kind == "c":
                deps.append(cache_ready[kidx][1])
            else:
                deps.append(v_tok)
            P.PE.wait(deps)
            for g4 in range(4):
                h = (kh // 2) * 8 + 2 * g4 + (kh % 2)
                kvh = h // 4
                ob, osl = divmod(h, 6)
                if kind == "c":
                    rhs = vca[kidx % 4][:, kvh * 80:kvh * 80 + 66]
                else:
                    rhs = vaug[:nkeys, kidx, kvh * 80:kvh * 80 + 66]
                ins = nc.tensor.matmul(PSb(4 + ob)[:nq, osl * 72:osl * 72 + 66], lhsT=pT[b][:nkeys, g4, 0:nq], rhs=rhs,
                                       start=(ki == 0 and h in (0, 6, 12)), stop=(ki == nk - 1), skip_group_check=True)
            tpv = P.PE.sig(ins)
            pfree[b] = [tpv]
            if kind == "c" and kh == 3:
                cfree[kidx % 4] = [tpv]
            return tpv

        att_tok = []
        LOOK = 3
        for n0 in range(4):
            prep_dma(n0)
        for n0 in range(3):
            prep_cache(n0)
        for u in range(min(LOOK, len(units))):
            emit_qk(u)
        emit_em(0)
        for u in range(len(units)):
            if u + LOOK < len(units):
                emit_qk(u + LOOK)
            if u + 1 < len(units):
                emit_em(u + 1)
            tpv = emit_pv(u)
            qi, ti, nq, xc, ki, nk, kspec, kh = units[u]
            if self.upto == f"P4u{u}":
                raise _Stop()
            if kspec[0] == "c" and kh == 3:
                prep_dma(kspec[1] + 4)
                prep_cache(kspec[1] + 3)
            if ki == nk - 1 and kh == 3:
                s = qi % 2
                tn = []
                hb_ = [(0, 6), (6, 6), (12, 4)]
                ovs = [PSb(4 + ob)[:nq, 0:nh_ * 72].rearrange("p (h e) -> p h e", e=72) for ob, (h0, nh_) in enumerate(hb_)]
                tas = [P.V([tpv], "tensor_tensor", den[s][:nq, h0:h0 + nh_], ovs[ob][:, :, 64], esink[:nq, h0:h0 + nh_], ALU.add)
                       for ob, (h0, nh_) in enumerate(hb_)]
                tbs = [P.V([tas[ob]], "reciprocal", den[s][:nq, h0:h0 + nh_], den[s][:nq, h0:h0 + nh_]) for ob, (h0, nh_) in enumerate(hb_)]
                for ob, (h0, nh_) in enumerate(hb_):
                    tn.append(P.V([tbs[ob]], "tensor_tensor", att[s][:nq, h0 * 64:(h0 + nh_) * 64].rearrange("p (h d) -> p h d", d=64),
                                  ovs[ob][:, :, 0:64], den[s][:nq, h0:h0 + nh_].unsqueeze(2).to_broadcast([nq, nh_, 64]), ALU.mult))
                ofree[0] = [tn]
                P.PE.wait(tn, tbfree[0])
                pb = PSb(7).bitcast(BF16).rearrange("p (a b) -> p a b", a=8)
                for c in range(8):
                    ins = nc.tensor.transpose(pb[:, c, 0:nq], att[s][:nq, c * 128:(c + 1) * 128], identb[:nq, :nq])
                tt = P.PE.sig(ins)
                tq = P.A([tt], "copy", mixT[:, 0:8, xc:xc + nq], pb[:, :, 0:nq])
                tbfree[0] = [tq]
                att_tok.append(tq)
                if self.upto == f"P4q{qi}":
                    P.dump("attT", mixT[:, 0:8, :], att_tok)
                    raise _Stop()
        P.dump("attT", mixT[:, 0:8, :], att_tok)
        P.phase_end("P4")
        p4done = [att_tok, ofree[0], tbfree[0], [sfree[b] for b in range(4)], [pfree[b] for b in range(4)], P.new_out_sem()]

        RX.reset()
        xT = RX.carve([128, KC, TX], F32)
        RH.reset()
        xt5 = [RH.carve([128, D], F32) for _ in range(2)]
        xfree = [[p4done, hT_tok], [p4done, hT_tok]]
        bk5 = [[p4done]] * 4
        xT_tok = []
        nb5 = 0
        for ti in range(1, 11):
            src, nr, col = tiles[ti]
            xc = col - NE
            s = ti % 2
            tld = P.load(P.SP, xld[s], xfree[s], xt5[s][:nr], src)
            for g in range(4):
                b = nb5 % 4
                nb5 += 1
                P.PE.wait(tld, bk5[b])
                pv = PSb(b).rearrange("p (a b) -> p a b", a=4)
                for c in range(4):
                    k = 4 * g + c
                    ins = nc.tensor.transpose(pv[:, c, 0:nr], xt5[s][:nr, k * 128:(k + 1) * 128], identf[:nr, :nr])
                tt = P.PE.sig(ins)
                if g % 2 == 0:
                    tc5 = P.A([tt, p4done], "copy", xT[:, 4 * g:4 * g + 4, xc:xc + nr], pv[:, :, 0:nr])
                else:
                    tc5 = P.V([tt, p4done], "tensor_copy", xT[:, 4 * g:4 * g + 4, xc:xc + nr], pv[:, :, 0:nr])
                bk5[b] = [tc5]
                xT_tok.append(tc5)
            xfree[s] = [tt]
        P.dump("xT0", xT, xT_tok)
        P.phase_end("P5")

        self.yset = 0
        self.yfree = [[xT_tok, bk5], [xT_tok, bk5]]

        def proj_add(slab_iter, nk, act, act_tok, cpieces, xoff, scale_col=None, on_done=None):
            last = None
            pending = None
            for (wv, wtok, slot, d0, nd) in slab_iter:
                for dd in range(nd):
                    d = d0 + dd
                    ys = self.yset
                    self.yset ^= 1
                    P.PE.wait(wtok, act_tok, self.yfree[ys])
                    for k in range(nk):
                        for pi, (c0, n) in enumerate(cpieces):
                            ins = nc.tensor.matmul(PSb(3 * ys + pi)[:, 0:n], lhsT=wv[:, k, dd * 128:(dd + 1) * 128],
                                                   rhs=act[:, k, c0:c0 + n], start=(k == 0), stop=(k == nk - 1))
                    tmm = P.PE.sig(ins)
                    tks = []
                    for pi, (c0, n) in enumerate(cpieces):
                        xv = xT[:, d, xoff + c0 - cpieces[0][0]:xoff + c0 - cpieces[0][0] + n]
                        if scale_col is None:
                            tks.append(P.V([tmm, xT_tok], "tensor_tensor", xv, xv, PSb(3 * ys + pi)[:, 0:n], ALU.add))
                        else:
                            tks.append(P.V([tmm, xT_tok], "scalar_tensor_tensor", xv, PSb(3 * ys + pi)[:, 0:n],
                                           vT[:, scale_col + d:scale_col + d + 1], xv, ALU.mult, ALU.add))
                    self.yfree[ys] = [tks]
                    last = tks
                    if on_done is not None:
                        if pending is not None:
                            on_done(*pending)
                        pending = (d, tks)
                P.wrelease(slot, tmm)
            if on_done is not None and pending is not None:
                on_done(*pending)
            return last

        w_out3 = w_out.rearrange("(c p) n -> p c n", p=128)

        def wout_slabs():
            for s8 in range(8):
                wv, wtok, slot = P.wslab(w_out3[:, :, 256 * s8:256 * s8 + 256], KC, 256)
                yield wv, wtok, slot, 2 * s8, 2
        res_tok = proj_add(wout_slabs(), KC, mixT, [att_tok, ln_tok], xpieces, 0)
        res_tok = [res_tok, self.yfree]
        P.dump("xT1", xT, res_tok)
        P.phase_end("P6")

        def rms_stats(x_tok, sqbufs, rs):
            P.PE.wait(self.yfree)
            sfr = [[], []]
            for k in range(KC):
                if k % 2 == 0:
                    tq = P.A([x_tok, sfr[k % 2]], "activation", sqbufs[k % 2], xT[:, k, :], AF.Square)
                else:
                    tq = P.V([x_tok, sfr[k % 2]], "tensor_tensor", sqbufs[k % 2], xT[:, k, :], xT[:, k, :], ALU.mult)
                P.PE.wait(tq)
                for pi, (c0, n) in enumerate(xpieces):
                    ins = nc.tensor.matmul(PSb(pi)[:, 0:n], lhsT=onesb, rhs=sqbufs[k % 2][:, c0:c0 + n], start=(k == 0), stop=(k == KC - 1))
                sfr[k % 2] = [P.PE.sig(ins)]
            tr = []
            for pi, (c0, n) in enumerate(xpieces):
                tr.append(P.V([sfr[1]], "tensor_scalar", rs[:, c0:c0 + n], PSb(pi)[:, 0:n], 1.0, RMS_EPS, ALU.mult, ALU.add))
            tr2 = P.A([tr], "sqrt", rs, rs)
            tr3 = P.V([tr2], "reciprocal", rs, rs)
            self.yfree[0] = [self.yfree[0], tr]
            return tr3

        def ffn(layer, x_tok):
            RH.reset()
            hT2 = RH.carve([128, KC, TH], BF16)
            RA.reset()
            aT = RA.carve([128, 12, TX], BF16)
            sgb = [RA.carve([128, TX], F32) for _ in range(2)]
            sqb = [sgb[0].bitcast(BF16)[:, 0:TX], sgb[1].bitcast(BF16)[:, 0:TX]]
            rs = self.rsF
            t_rs = rms_stats(x_tok, sqb, rs)
            h_tok = []
            for k in range(KC):
                h_tok.append(P.V([t_rs, x_tok], "scalar_tensor_tensor", hT2[:, k, NE:TH], xT[:, k, :],
                                 vT[:, V_NFFN + 16 * layer + k:V_NFFN + 16 * layer + k + 1], rs, ALU.mult, ALU.mult))
            wg3 = w_gate[layer].rearrange("(c p) n -> p c n", p=128)
            wu3 = w_up[layer].rearrange("(c p) n -> p c n", p=128)
            wd3 = w_down[layer].rearrange("(c p) n -> p c n", p=128)
            x0 = 0 if layer == 0 else NH
            cp = [(NE + x0, 512), (NE + x0 + 512, 512), (NE + x0 + 1024, TX - x0 - 1024)]
            dpieces = [(0, 512), (512, 512), (1024, TX - x0 - 1024)]
            sgfree = [[], []]
            a_free = [h_tok]
            last_res = x_tok
            gfree = [self.yfree]
            ufree = [self.yfree]
            for (f0, nf) in QUARTERS:
                a_tok = []
                for sp in range(nf // 2):
                    fs = f0 + 2 * sp
                    gs_ = P.wslab(wg3[:, :, 128 * fs:128 * fs + 256], KC, 256)
                    us_ = P.wslab(wu3[:, :, 128 * fs:128 * fs + 256], KC, 256)
                    for dd in range(2):
                        fi = 2 * sp + dd
                        s = fi % 2
                        P.PE.wait(gs_[1], gfree)
                        for k in range(KC):
                            P.PE.wait(h_tok[k])
                            for pi, (c0, n) in enumerate(cp):
                                ins = nc.tensor.matmul(PSb(pi)[:, 0:n], lhsT=gs_[0][:, k, dd * 128:(dd + 1) * 128], rhs=hT2[:, k, c0:c0 + n],
                                                       start=(k == 0), stop=(k == KC - 1))
                        tg = P.PE.sig(ins)
                        P.PE.wait(us_[1], ufree)
                        for k in range(KC):
                            for pi, (c0, n) in enumerate(cp):
                                ins = nc.tensor.matmul(PSb(3 + pi)[:, 0:n], lhsT=us_[0][:, k, dd * 128:(dd + 1) * 128], rhs=hT2[:, k, c0:c0 + n],
                                                       start=(k == 0), stop=(k == KC - 1))
                        tu = P.PE.sig(ins)
                        tsl = []
                        off = 0
                        for pi, (c0, n) in enumerate(cp):
                            tsl.append(P.A([tg, sgfree[s]], "activation", sgb[s][:, off:off + n], PSb(pi)[:, 0:n], AF.Silu))
                            off += n
                        tml = []
                        off = 0
                        for pi, (c0, n) in enumerate(cp):
                            tml.append(P.V([tu, tsl[pi], a_free], "tensor_tensor", aT[:, fi, off:off + n], sgb[s][:, off:off + n], PSb(3 + pi)[:, 0:n], ALU.mult))
                            off += n
                        sgfree[s] = [tml]
                        gfree = [tsl]
                        ufree = [tml]
                        a_tok.append(tml)
                    P.wrelease(gs_[2], tg)
                    P.wrelease(us_[2], tu)
                self.yfree = [[gfree], [ufree]]

                def wd_slabs():
                    for s8 in range(8):
                        wv, wtok, slot = P.wslab(wd3[:, f0:f0 + nf, 256 * s8:256 * s8 + 256], nf, 256)
                        yield wv, wtok, slot, 2 * s8, 2
                last_res = proj_add(wd_slabs(), nf, aT, a_tok, dpieces, x0,
                                    on_done=(self.out_cb if (layer == 1 and f0 == QUARTERS[-1][0]) else None))
                a_free = [self.yfree]
                gfree = [self.yfree[0]]
                ufree = [self.yfree[1]]
            return [last_res, self.yfree]

        self.RS = Region(P, "RS", TX)
        self.rsF = self.RS.carve([128, TX], F32)
        w_gate = P.din("w_gate", [2, D, DFF])
        w_up = P.din("w_up", [2, D, DFF])
        w_down = P.din("w_down", [2, DFF, D])
        x1_tok = ffn(0, res_tok)
        P.dump("xT2", xT, x1_tok)
        P.phase_end("P8")

        RA.reset()
        sqb = [RA.carve([128, TX], BF16) for _ in range(2)]
        hbp = [RA.carve([128, 1040], F32) for _ in range(2)]
        hbs = [RA.carve([128, NSEQ, 23], F32) for _ in range(2)]
        Sp = [RA.carve([128, 1040], F32) for _ in range(2)]
        Ss = [RA.carve([128, NSEQ, 23], F32) for _ in range(2)]
        spt = [[RA.carve([120, 128], F32) for _ in range(2)] for _ in range(2)]
        hsc = [RA.carve([128, 128], F32) for _ in range(2)]
        npo = [RA.carve([16, 128], F32) for _ in range(2)]
        nso = [RA.carve([128, 128], F32) for _ in range(2)]
        fixb = RA.carve([128, 16], F32)
        RH.reset()
        dpT = RH.carve([128, KC, NM + NS], BF16)
        rs = self.rsF
        pld = [DSem(P, f"pl{i}") for i in range(2)]
        pst = [DSem(P, f"pst{i}") for i in range(2)]
        self.out_sems += pst
        t_rs = rms_stats(x1_tok, sqb, rs)
        hbfree = [[], []]
        sptfree = [[x1_tok], [x1_tok]]
        stgfree = [[], []]
        b6free = [self.yfree]
        b7free = [self.yfree]
        dp_tok = []
        self.yfree = [[self.yfree, b6free, b7free], [self.yfree, b6free, b7free]]
        ppieces = [(0, 512), (512, 512), (1024, 128)]
        pslab = [P.wslab(pool_w[g].rearrange("(c p) n -> p c n", p=128), 4, 512) for g in range(4)]
        pool_last = [None]
        pool_sched = []
        pmm = {}

        def pool_mm(g, e):
            wv, wtok, slot = pslab[g]
            d = 4 * g + e
            ys = self.yset
            self.yset ^= 1
            P.PE.wait(wtok, dp_tok[4 * g:4 * g + 4], self.yfree[ys])
            for cc in range(4):
                for pi, (c0, n) in enumerate(ppieces):
                    ins = nc.tensor.matmul(PSb(3 * ys + pi)[:, 0:n], lhsT=wv[:, cc, e * 128:(e + 1) * 128],
                                           rhs=dpT[:, 4 * g + cc, c0:c0 + n], start=(cc == 0), stop=(cc == 3))
            pmm[d] = (P.PE.sig(ins), ys)
            if e == 3:
                P.wrelease(slot, pmm[d][0])

        def pool_ev(g, e):
            d = 4 * g + e
            tmm, ys = pmm[d]
            tks = []
            for pi, (c0, n) in enumerate(ppieces):
                xv = xT[:, d, NH + c0:NH + c0 + n]
                tks.append(P.V([tmm, x1_tok], "scalar_tensor_tensor", xv, PSb(3 * ys + pi)[:, 0:n],
                               vT[:, V_PSC + d:V_PSC + d + 1], xv, ALU.mult, ALU.add))
            self.yfree[ys] = [tks]
            pool_last[0] = tks

        def pool_steps(g):
            return [lambda: (pool_mm(g, 0), pool_mm(g, 1)),
                    lambda: (pool_ev(g, 0), pool_ev(g, 1), pool_mm(g, 2), pool_mm(g, 3)),
                    lambda: (pool_ev(g, 2), pool_ev(g, 3))]

        def pool_loads(k):
            s = k % 2
            for r in range(2):
                tk = P.load(P.SP, pld[s], sptfree[s] if r == 0 else [], spt[s][r],
                            spool[8 * r:8 * r + 8, :, k * 128:(k + 1) * 128].rearrange("n i c -> (n i) c"))
            return tk

        for k in range(KC):
            s = k % 2
            g = k // 4
            w = (2, 4, 8, 16)[g]
            gcol = vT[:, V_NMIX + 16 + k:V_NMIX + 16 + k + 1]
            if k == 0:
                tsp_next = pool_loads(0)
            tsp = tsp_next
            if k + 1 < KC:
                tsp_next = pool_loads(k + 1)
            t0 = P.V([t_rs, hbfree[s]], "scalar_tensor_tensor", hbp[s][:, 0:1039], xT[:, k, 1:1040], gcol, rs[:, 1:1040], ALU.mult, ALU.mult)
            t0b = P.V([t0], "tensor_scalar", hbp[s][:, 0:15], hbp[s][:, 0:15], hvalid[:, 0:1], 0.0, ALU.mult, ALU.add)
            t1_ = P.V([], "scalar_tensor_tensor", hbs[s][:, :, 15:23], xT[:, k, 1040:1168].rearrange("p (a b) -> p a b", a=NSEQ), gcol,
                      rs[:, 1040:1168].rearrange("p (a b) -> p a b", a=NSEQ), ALU.mult, ALU.mult)
            tcs = P.A([t1_, stgfree[s]], "copy", hsc[s].rearrange("p (a b) -> p a b", a=NSEQ), hbs[s][:, :, 15:23])
            P.PE.wait(tsp, b6free)
            for r in range(2):
                ins = nc.tensor.transpose(PSb(6)[:, r * 120:(r + 1) * 120], spt[s][r], identf[0:120, 0:120])
            tt = P.PE.sig(ins)
            sptfree[s] = [tt]
            t2_ = P.A([tt, t1_], "copy", hbs[s][:, :, 0:15], PSb(6)[:, 0:240].rearrange("p (a b) -> p a b", a=NSEQ))
            b6free = [t2_]
            P.PE.wait(t0b, tcs, b7free)
            nc.tensor.transpose(PSb(7)[0:16, 0:128], hbp[s][:, 1023:1039], identf)
            ins = nc.tensor.transpose(PSb(7)[:, 128:256], hsc[s], identf)
            tn7 = P.PE.sig(ins)
            ta7 = P.A([tn7, stgfree[s]], "copy", npo[s], PSb(7)[0:16, 0:128])
            tnp = P.A([], "copy", nso[s], PSb(7)[:, 128:256])
            b7free = [tnp]
            st1 = P.store([ta7, tnp], np_p[:, k * 128:(k + 1) * 128], npo[s][1:16, :], pst[s])
            st2 = P.store([], np_s[:, 7:15, k * 128:(k + 1) * 128], nso[s], pst[s])
            stgfree[s] = [tn7, st2]
            curp, curs = hbp[s], hbs[s]
            tp_, ts_ = [t0b], [t2_, t1_]
            sh = 1
            bi = 0
            while sh < w:
                op_, os_ = Sp[bi % 2], Ss[bi % 2]
                tp_ = [P.V([tp_], "tensor_tensor", op_[:, sh:1039], curp[:, sh:1039], curp[:, 0:1039 - sh], ALU.add)]
                ts_ = [P.V([ts_], "tensor_tensor", os_[:, :, sh:23], curs[:, :, sh:23], curs[:, :, 0:23 - sh], ALU.add)]
                curp, curs = op_, os_
                sh *= 2
                bi += 1
            td = P.V([tp_], "scalar_tensor_tensor", dpT[:, k, 0:NM], curp[:, 15:1039], 1.0 / w, hbp[s][:, 15:1039], ALU.mult, ALU.subtract)
            tf1 = P.V([tp_], "tensor_tensor", fixb, curp[:, 15:31], invc[:, g, :], ALU.mult)
            tf2 = P.V([tf1, td], "tensor_tensor", dpT[:, k, 0:16], fixb, hbp[s][:, 15:31], ALU.subtract)
            te_ = P.V([ts_], "scalar_tensor_tensor", dpT[:, k, NM:NM + NS].rearrange("p (a b) -> p a b", a=NSEQ), curs[:, :, 15:23], 1.0 / w,
                      hbs[s][:, :, 15:23], ALU.mult, ALU.subtract)
            hbfree[s] = [tn7]
            dp_tok.append([tf2, te_, td])
            if pool_sched:
                pool_sched.pop(0)()
            if k % 4 == 3:
                pool_sched += pool_steps(k // 4)
        P.store([], np_s[:, 0:7, :], spool[:, 8:15, :])
        P.dump("dpT", dpT, dp_tok)
        P.phase_end("P9")
        while pool_sched:
            pool_sched.pop(0)()
        last = pool_last[0]
        x2_tok = [last, self.yfree, [(d.sem, d.cnt) for d in pst]]
        P.dump("xT3", xT, x2_tok)
        P.phase_end("P10")
        ostg = self.RS.t[:, 0:1024].rearrange("p (a b) -> p a b", a=8)
        osem = DSem(P, "osem")
        self.out_sems.append(osem)
        ost = {"free": [], "b67": []}
        y_p3 = y_p.rearrange("(t p) c -> p t c", p=128)

        def out_cb(d, tks):
            P.PE.wait(tks, ost["b67"])
            for t in range(8):
                ins = nc.tensor.transpose(PSb(6 + t // 4)[:, (t % 4) * 128:(t % 4) * 128 + 128], xT[:, d, NH + 128 * t:NH + 128 * t + 128], identf)
            tt = P.PE.sig(ins)
            ta = P.A([tt, ost["free"]], "copy", ostg[:, 0:4, :], PSb(6).rearrange("p (a b) -> p a b", a=4))
            tb2 = P.V([tt, ost["free"]], "tensor_copy", ostg[:, 4:8, :], PSb(7).rearrange("p (a b) -> p a b", a=4))
            ost["b67"] = [ta, tb2]
            ost["free"] = [P.store([ta, tb2], y_p3[:, :, d * 128:(d + 1) * 128], ostg, osem)]
        self.out_cb = out_cb
        x3_tok = ffn(1, x2_tok)
        x3_tok = [x3_tok, ost["b67"], ost["free"]]

        RH.reset()
        yt = [RH.carve([128, D], F32) for _ in range(2)]
        ysem = [DSem(P, f"y{i}") for i in range(2)]
        ytfree = [[x3_tok], [x3_tok]]
        bk = [[x3_tok]] * 4
        nb = 0
        for oi in range(8, 9):
            s = oi % 2
            xc = NH + 128 * oi
            dst = y_p[128 * oi:128 * oi + 128, :] if oi < 8 else y_s[:, :]
            tcs = []
            for g in range(4):
                b = nb % 4
                nb += 1
                P.PE.wait(x3_tok, bk[b])
                pv = PSb(b).rearrange("p (a b) -> p a b", a=4)
                for c in range(4):
                    k = 4 * g + c
                    ins = nc.tensor.transpose(pv[:, c, :], xT[:, k, xc:xc + 128], identf)
                tt = P.PE.sig(ins)
                if g % 2 == 0:
                    tc = P.A([tt, ytfree[s]], "copy", yt[s][:, 512 * g:512 * g + 512], PSb(b))
                else:
                    tc = P.V([tt, ytfree[s]], "tensor_copy", yt[s][:, 512 * g:512 * g + 512], PSb(b))
                bk[b] = [tc]
                tcs.append(tc)
            P.SP.wait(tcs)
            tok = ysem[s].add(nc.sync.dma_start(out=dst, in_=yt[s]))
            ytfree[s] = [tok]
        P.SP.wait(ytfree)


def _prep_inputs(inp):
    f = lambda a: np.ascontiguousarray(np.asarray(a, dtype=np.float32))
    x_prompt, x_sample = f(inp["x_prompt"]), f(inp["x_sample"])
    vecs = np.concatenate([f(inp["norm_mix"]).reshape(32, 128), f(inp["norm_ffn"]).reshape(32, 128),
                           f(inp["pool_scale"]).reshape(16, 128), f(inp["conv_b"]).reshape(8, 128),
                           f(inp["conv_ln_g"]).reshape(8, 128), f(inp["conv_ln_b"]).reshape(8, 128)], 0)
    shared = {
        "vecs": f(vecs), "convw": f(inp["conv_w"][0]), "qkg": f(np.stack([inp["q_norm"][0], inp["k_norm"][0]])),
        "sinks": f(inp["sinks"][0]), "w_in": f(inp["w_in"][0]), "w_out": f(inp["w_out"][0]), "pool_w": f(inp["pool_w"][0]),
        "w_gate": f(inp["w_gate"]), "w_up": f(inp["w_up"]), "w_down": f(inp["w_down"]),
    }
    j = np.arange(128)[:, None]
    i = np.arange(128)[None, :]
    mp = (j > i).astype(np.float32)
    mc = (j <= i).astype(np.float32)
    in_maps = []
    for r in range(8):
        b, q = divmod(r, 4)
        c = 1024 * q
        lo = c - (NE + NH)
        xp = np.zeros((NE + NH + NM, D), np.float32)
        s0 = max(lo, 0)
        xp[s0 - lo:] = x_prompt[b, s0:c + NM]
        first = (q == 0)
        masks = np.zeros((128, NMASK, 128), np.float32)
        masks[:, 0], masks[:, 1] = mp, mc
        masks[:, 2] = 0.0 if first else mp
        masks[:, 3] = mc * ((j >= NH) if first else 1.0)
        for n in range(NSEQ):
            masks[:, 4 + n] = ((i // 8) == n) * (j > (i % 8))
        masks[:, 20] = ((j // 8) == (i // 8)) * ((j % 8) <= (i % 8))
        cst = np.zeros((128, 321), np.float32)
        cst[:, 0:128] = np.eye(128)
        cst[:, 128:256] = 1.0
        invc = np.zeros((4, 16), np.float32)
        for g, w in enumerate((2, 4, 8, 16)):
            pos = np.arange(16) + (0 if first else 10 ** 6)
            invc[g] = 1.0 / np.minimum(pos + 1, w)
        cst[:, 256:320] = invc.reshape(1, 64)
        cst[:, 320] = 0.0 if first else 1.0
        m = dict(shared)
        m.update({
            "xp": xp, "xs": f(x_sample[16 * r:16 * r + 16].reshape(128, D)),
            "ck": f(inp["cache_k"][0, 16 * r:16 * r + 16].reshape(16, 128, 256)),
            "cv": f(inp["cache_v"][0, 16 * r:16 * r + 16].reshape(16, 128, 256)),
            "sconv": f(inp["state_conv"][0, 16 * r:16 * r + 16]), "spool": f(inp["state_pool"][0, 16 * r:16 * r + 16]),
            "masks": masks, "cst": cst,
        })
        in_maps.append(m)
    return in_maps


_CACHE = {}


def kernel(**inp):
    debug = tuple(inp.pop("_debug", ()))
    in_maps = _prep_inputs(inp)
    key = debug
    if key not in _CACHE:
        _CACHE[key] = Prog(debug)
    prog = _CACHE[key]
    in_maps = [{k: v for k, v in m.items() if k in prog.in_names} for m in in_maps]
    res = run_bass_kernel_spmd(prog.nc, in_maps, core_ids=list(range(8)))
    R = res.results
    y_prompt = np.zeros((2, 4096, D), np.float32)
    for r in range(8):
        b, q = divmod(r, 4)
        y_prompt[b, 1024 * q:1024 * q + 1024] = R[r]["y_p"]
    y_sample = np.concatenate([R[r]["y_s"].reshape(16, 8, D) for r in range(8)], 0)
    last = [3, 7]
    nkp = np.stack([R[r]["nk_p"].reshape(128, 4, 64) for r in last])[None]
    nvp = np.stack([R[r]["nv_p"].reshape(128, 4, 64) for r in last])[None]
    ncp = np.stack([R[r]["nc_p"] for r in last])[None]
    npp = np.stack([R[r]["np_p"] for r in last])[None]
    nks = np.concatenate([R[r]["nk_s"].reshape(16, 128, 4, 64) for r in range(8)], 0)[None]
    nvs = np.concatenate([R[r]["nv_s"].reshape(16, 128, 4, 64) for r in range(8)], 0)[None]
    ncs = np.concatenate([R[r]["nc_s"] for r in range(8)], 0)[None]
    nps = np.concatenate([R[r]["np_s"] for r in range(8)], 0)[None]
    outs = (y_prompt, y_sample, nkp, nvp, ncp, npp, nks, nvs, ncs, nps)
    outs = tuple(np.ascontiguousarray(o.astype(np.float32)) for o in outs)
    if debug:
        return outs, [{k: R[r]["dbg_" + k] for k in prog.dbg_outs} for r in range(8)]
    return outs
```

```python
import numpy as np
import concourse.bass as bass
import concourse.mybir as mybir
from concourse.bass_utils import run_bass_kernel_spmd

F32 = mybir.dt.float32
BF16 = mybir.dt.bfloat16
AF = mybir.ActivationFunctionType
ALU = mybir.AluOpType
AX = mybir.AxisListType

D = 2048
KC = 16
DFF = 5632
NE, NH, NM, NS = 128, 16, 1024, 128
TX = NH + NM + NS
TH = NE + TX
NSEQ = 16
RMS_EPS = 1e-6
LN_EPS = 1e-5
NMASK = 21
QUARTERS = [(0, 12), (12, 12), (24, 10), (34, 10)]
CONV_DVE_TAPS = 31


def _flat(xs):
    for x in xs:
        if x is None:
            continue
        if isinstance(x, (list,)):
            yield from _flat(x)
        elif isinstance(x, tuple) and len(x) == 2 and isinstance(x[1], int):
            yield x
        else:
            yield from _flat(list(x))


class Eng:
    def __init__(self, P, name, h):
        self.P, self.name, self.h = P, name, h
        self.seen = {}
        self.sem = None
        self.cnt = 0

    def _newsem(self):
        self.sem = self.P.newsem("e" + self.name)
        self.cnt = 0

    def sig(self, ins):
        if self.sem is None or self.cnt >= 3000:
            self._newsem()
        ins.then_inc(self.sem, 1)
        self.cnt += 1
        return (self.sem, self.cnt)

    def wait(self, *toks):
        for sem, c in _flat(toks):
            if self.seen.get(sem.name, 0) >= c:
                continue
            self.h.wait_ge(sem, c)
            self.seen[sem.name] = c


class DSem:
    def __init__(self, P, name):
        self.sem = P.newsem("d" + name)
        self.cnt = 0

    def add(self, ins):
        ins.then_inc(self.sem, 16)
        self.cnt += 16
        return (self.sem, self.cnt)


class Region:
    def __init__(self, P, name, nfloat):
        self.t = P.nc.alloc_sbuf_tensor(name, [128, nfloat], F32)
        self.n = nfloat
        self.o = 0

    def reset(self, o=0):
        self.o = o

    def carve(self, shape, dt):
        n = int(np.prod(shape[1:]))
        nf = (n * (2 if dt == BF16 else 4) + 3) // 4
        nf = (nf + 7) // 8 * 8
        assert self.o + nf <= self.n, (self.o, nf, self.n, shape)
        v = self.t[:shape[0], self.o:self.o + nf]
        self.o += nf
        if dt == BF16:
            v = v.bitcast(BF16)
        v = v[:, 0:n]
        if len(shape) == 3:
            v = v.rearrange("p (a b) -> p a b", a=shape[1])
        elif len(shape) == 4:
            v = v.rearrange("p (a b c) -> p a b c", a=shape[1], b=shape[2])
        return v


class _Stop(Exception):
    pass


class Prog:
    def __init__(self, debug=()):
        self.debug = set(debug)
        self.upto = None
        for d in debug:
            if d.startswith("upto:"):
                self.upto = d[5:]
        self.nsem = 0
        nc = self.nc = bass.Bass("TRN2", target_bir_lowering=False)
        self.PE = Eng(self, "pe", nc.tensor)
        self.ACT = Eng(self, "act", nc.scalar)
        self.DVE = Eng(self, "dve", nc.vector)
        self.POOL = Eng(self, "pool", nc.gpsimd)
        self.SP = Eng(self, "sp", nc.sync)
        self.out_sems = []
        self.new_out_sem()
        self.dbg_outs = {}
        self.in_names = set()
        try:
            self.build()
        except _Stop:
            pass
        self.SP.wait([(d.sem, d.cnt) for d in self.out_sems if d.cnt])

    def phase_end(self, name):
        if self.upto == name:
            raise _Stop()

    def new_out_sem(self):
        tok = None
        if self.out_sems:
            tok = (self.out_sem.sem, self.out_sem.cnt) if self.out_sem.cnt else None
        self.out_sem = DSem(self, f"out{len(self.out_sems)}")
        self.out_sems.append(self.out_sem)
        return tok

    def newsem(self, name):
        self.nsem += 1
        return self.nc.semaphore(f"{name}_{self.nsem}").__enter__()

    def din(self, name, shape, dt=F32):
        self.in_names.add(name)
        return self.nc.dram_tensor(name, list(shape), dt, kind="ExternalInput").ap()

    def dout(self, name, shape, dt=F32):
        return self.nc.dram_tensor(name, list(shape), dt, kind="ExternalOutput").ap()

    def op(self, eng, deps, fn, *a, **kw):
        eng.wait(deps)
        return eng.sig(fn(*a, **kw))

    def V(self, deps, name, *a, **kw):
        return self.op(self.DVE, deps, getattr(self.nc.vector, name), *a, **kw)

    def G(self, deps, name, *a, **kw):
        return self.op(self.POOL, deps, getattr(self.nc.gpsimd, name), *a, **kw)

    def A(self, deps, name, *a, **kw):
        return self.op(self.ACT, deps, getattr(self.nc.scalar, name), *a, **kw)

    def store(self, deps, out, in_, dsem=None):
        self.SP.wait(deps)
        return (dsem or self.out_sem).add(self.nc.sync.dma_start(out=out, in_=in_))

    def load(self, q, dsem, deps, out, in_):
        q.wait(deps)
        return dsem.add(q.h.dma_start(out=out, in_=in_))

    def dump(self, name, ap, deps):
        if name not in self.debug:
            return
        d = self.dout("dbg_" + name, list(ap.shape), ap.dtype)
        self.dbg_outs[name] = d
        self.store(deps, d, ap)

    def bank(self, b, n=1):
        return self.PS[:, 512 * b:512 * (b + n)]

    def wslab(self, src3, nchunk, ncol):
        s = self.wn % 4
        self.wn += 1
        self.POOL.wait(self.wfree[s])
        self.wfree[s] = []
        v = self.wslot[s][:, 0:nchunk * ncol].rearrange("p (a b) -> p a b", a=nchunk)
        tok = self.wsem[s].add(self.nc.gpsimd.dma_start(out=v, in_=src3))
        return v, tok, s

    def wslab2(self, src3, nchunk, ncol):
        s = self.wn % 4
        assert s % 2 == 0
        self.wn += 2
        self.POOL.wait(self.wfree[s], self.wfree[s + 1])
        self.wfree[s] = []
        self.wfree[s + 1] = []
        v = self.RW.t[:, 2048 * s:2048 * (s + 2)].bitcast(BF16)[:, 0:nchunk * ncol].rearrange("p (a b) -> p a b", a=nchunk)
        tok = self.wsem[s].add(self.nc.gpsimd.dma_start(out=v, in_=src3))
        return v, tok, s

    def wrelease(self, s, tok):
        self.wfree[s].append(tok)

    def build(self):
        nc = self.nc
        P = self
        xp = P.din("xp", [NE + NH + NM, D])
        xs = P.din("xs", [NS, D])
        ck = P.din("ck", [NSEQ, 128, 256])
        cv = P.din("cv", [NSEQ, 128, 256])
        sconv = P.din("sconv", [NSEQ, 30, 1024])
        spool = P.din("spool", [NSEQ, 15, D])
        vecs = P.din("vecs", [104, 128])
        convw = P.din("convw", [31, 1024])
        qkg = P.din("qkg", [2, 64])
        sinks = P.din("sinks", [16])
        w_in = P.din("w_in", [D, 3584])
        w_out = P.din("w_out", [D, D])
        pool_w = P.din("pool_w", [4, 512, 512])
        masks = P.din("masks", [128, NMASK, 128])
        cst = P.din("cst", [128, 128 + 128 + 64 + 1])

        y_p = P.dout("y_p", [NM, D])
        y_s = P.dout("y_s", [NS, D])
        nk_p = P.dout("nk_p", [128, 256])
        nv_p = P.dout("nv_p", [128, 256])
        nc_p = P.dout("nc_p", [30, 1024])
        np_p = P.dout("np_p", [15, D])
        nk_s = P.dout("nk_s", [NSEQ, 128, 256])
        nv_s = P.dout("nv_s", [NSEQ, 128, 256])
        nc_s = P.dout("nc_s", [NSEQ, 30, 1024])
        np_s = P.dout("np_s", [NSEQ, 15, D])

        RX = Region(P, "RX", TX * KC)
        RH = Region(P, "RH", TH * KC // 2)
        RA = Region(P, "RA", 10240)
        RW = Region(P, "RW", 8192)
        self.RW = RW
        RC = Region(P, "RC", 3072)
        self.PS = nc.alloc_psum_tensor("PS", [128, 4096], F32)
        PSb = lambda b, n=1: self.bank(b, n)
        self.wslot = [RW.t[:, 2048 * s:2048 * (s + 1)].bitcast(BF16) for s in range(4)]
        self.wsem = [DSem(P, f"w{s}") for s in range(4)]
        self.wfree = [[] for _ in range(4)]
        self.wn = 0

        identf = RC.carve([128, 128], F32)
        onesf = RC.carve([128, 128], F32)
        invc = RC.carve([128, 4, 16], F32)
        hvalid = RC.carve([128, 1], F32)
        identb = RC.carve([128, 128], BF16)
        onesb = RC.carve([128, 128], BF16)
        maskb = RC.carve([128, NMASK, 128], BF16)
        vT = RC.carve([128, 104], F32)
        cwT = RC.carve([128, 8, 31], F32)
        qg_bc = RC.carve([128, 64], F32)
        kg_bc = RC.carve([128, 64], F32)
        esink = RC.carve([128, 16], F32)
        sm = RC.carve([128, 64], F32)
        cld = DSem(P, "cld")
        cstt = RX.carve([128, 321], F32)
        vecl = RX.carve([104, 128], F32)
        cwl = RX.carve([31, 1024], F32)
        t1 = P.load(P.SP, cld, [], cstt, cst[:, :])
        P.load(P.SP, cld, [], vecl, vecs[:, :])
        P.load(P.SP, cld, [], cwl, convw[:, :])
        P.load(P.SP, cld, [], qg_bc, qkg[0].partition_broadcast(128))
        P.load(P.SP, cld, [], kg_bc, qkg[1].partition_broadcast(128))
        tl = P.load(P.SP, cld, [], esink, sinks.partition_broadcast(128))
        mld = DSem(P, "mld")
        tm = P.load(P.POOL, mld, [], maskb, masks[:, :, :])
        c1 = P.V([tl], "tensor_copy", identf, cstt[:, 0:128])
        P.V([], "tensor_copy", identb, cstt[:, 0:128])
        P.V([], "tensor_scalar", onesf, cstt[:, 128:256], 1.0 / 1024, 0.0, ALU.mult, ALU.add)
        P.V([], "tensor_scalar", onesb, cstt[:, 128:256], 1.0 / 2048, 0.0, ALU.mult, ALU.add)
        P.V([], "tensor_copy", invc, cstt[:, 256:320].rearrange("p (a b) -> p a b", a=4))
        c2 = P.V([], "tensor_copy", hvalid, cstt[:, 320:321])
        a1 = P.A([tl], "activation", esink, esink, AF.Exp)
        a2 = P.A([], "mul", qg_bc, qg_bc, 0.125)
        P.PE.wait(c1, c2)
        i1 = nc.tensor.transpose(PSb(0)[:, 0:104], vecl, identf[0:104, 0:104])
        for c in range(8):
            i2 = nc.tensor.transpose(PSb(1)[:, c * 31:(c + 1) * 31], cwl[:, c * 128:(c + 1) * 128], identf[0:31, 0:31])
        tp = P.PE.sig(i2)
        P.V([tp], "tensor_copy", vT, PSb(0)[:, 0:104])
        cdone = P.V([], "tensor_copy", cwT, PSb(1)[:, 0:248].rearrange("p (a b) -> p a b", a=8))
        cdone = [cdone, a1, a2, tm, c2]
        V_NMIX, V_NFFN, V_PSC, V_CB, V_LG, V_LB = 0, 32, 64, 80, 88, 96
        RX.reset()

        xT = None
        hT = RH.carve([128, KC, TH], BF16)

        tiles = [(xp[128 * i:128 * i + 128, :], 128, 128 * i) for i in range(9)]
        tiles += [(xp[1152:1168, :], 16, 1152), (xs[:, :], 128, 1168)]

        xt = [RX.carve([128, D], F32) for _ in range(3)]
        hn = [RX.carve([128, D], BF16) for _ in range(3)]
        junk = RX.carve([128, D], BF16)
        xld = [DSem(P, f"x{i}") for i in range(3)]
        xfree = [[cdone], [cdone], [cdone]]
        hnfree = [[], [], []]
        bfree = [[cdone], [cdone], [cdone]]
        hT_tok = []
        c0 = {}

        def p0_load(n):
            src, nr, col = tiles[n]
            s = n % 3
            c0[n] = {"tld": P.load(P.SP, xld[s], xfree[s], xt[s][:nr], src)}

        def p0_sq(n):
            src, nr, col = tiles[n]
            s = n % 3
            c0[n]["tA"] = P.A([c0[n]["tld"]], "activation", junk[:nr], xt[s][:nr], AF.Square, accum_out=sm[:nr, 2 * s:2 * s + 1])

        def p0_v1(n):
            src, nr, col = tiles[n]
            s = n % 3
            c0[n]["tB"] = P.V([c0[n]["tA"]], "tensor_scalar", sm[:nr, 2 * s + 1:2 * s + 2], sm[:nr, 2 * s:2 * s + 1], 1.0 / D, RMS_EPS, ALU.mult, ALU.add)

        def p0_sqrt(n):
            src, nr, col = tiles[n]
            s = n % 3
            rs = sm[:nr, 2 * s + 1:2 * s + 2]
            c0[n]["tC"] = P.A([c0[n]["tB"]], "sqrt", rs, rs)

        def p0_v2(n):
            src, nr, col = tiles[n]
            s = n % 3
            rs = sm[:nr, 2 * s + 1:2 * s + 2]
            c0[n]["tD"] = P.V([c0[n]["tC"]], "reciprocal", rs, rs)

        def p0_mul(n):
            src, nr, col = tiles[n]
            s = n % 3
            tE = P.A([c0[n]["tD"], hnfree[s]], "mul", hn[s][:nr], xt[s][:nr], sm[:nr, 2 * s + 1:2 * s + 2])
            xfree[s] = [tE]
            c0[n]["tE"] = tE

        def p0_pe(n):
            src, nr, col = tiles[n]
            s = n % 3
            P.PE.wait(c0[n]["tE"], bfree[s], cdone)
            pb = PSb(2 * s, 2).bitcast(BF16).rearrange("p (a b) -> p a b", a=KC)
            for k in range(KC):
                ins = nc.tensor.transpose(pb[:, k, 0:nr], hn[s][:nr, k * 128:(k + 1) * 128], identb[:nr, :nr])
            tP = P.PE.sig(ins)
            hnfree[s] = [tP]
            c0[n]["tP"] = tP

        def p0_ev(n):
            src, nr, col = tiles[n]
            s = n % 3
            pb = PSb(2 * s, 2).bitcast(BF16).rearrange("p (a b) -> p a b", a=KC)
            tH = P.V([c0[n]["tP"]], "tensor_tensor", hT[:, :, col:col + nr], pb[:, :, 0:nr],
                     vT[:, V_NMIX:V_NMIX + 16].unsqueeze(2).to_broadcast([128, KC, nr]), ALU.mult)
            bfree[s] = [tH]
            hT_tok.append(tH)

        NT = len(tiles)
        ok = lambda n: 0 <= n < NT
        p0_load(0)
        p0_load(1)
        for i in range(NT + 2):
            if ok(i - 1):
                p0_sqrt(i - 1)
                p0_v2(i - 1)
                p0_mul(i - 1)
                p0_pe(i - 1)
            if ok(i - 2):
                p0_ev(i - 2)
            if ok(i + 2):
                p0_load(i + 2)
            if ok(i):
                p0_sq(i)
                p0_v1(i)
        P.dump("hT0", hT, hT_tok)
        P.phase_end("P0")
        RX.reset()

        w_in3 = w_in.rearrange("(c p) n -> p c n", p=128)
        conv_out = RX.carve([128, 8, TX], F32)
        glp = [RX.carve([128, 1072], F32) for _ in range(2)]
        gls = [RX.carve([128, NSEQ, 38], F32) for _ in range(2)]
        sg = RX.carve([128, 1200], F32)
        gsn = RX.carve([128, 128], F32)
        stc = [[RX.carve([120, 128], F32) for _ in range(4)] for _ in range(2)]
        cv2 = [RX.carve([128, TX], F32)] * 2
        NDG = 16
        dgr = [RX.carve([128, 128], F32) for _ in range(NDG)]
        dgfree = [[] for _ in range(NDG)]
        dgn = [0]
        b3free = []
        ncp = RA.carve([32, 1024], F32)
        ncs = RA.carve([128, 1024], F32)
        sld = [DSem(P, f"sld{i}") for i in range(2)]
        stfree = [[], []]
        cv2free = [[], []]
        TD = CONV_DVE_TAPS
        pieces = [(96, 512), (608, 512), (1120, 176)]
        slabs = {}
        glfree = [[], []]
        evfree = []
        b6free = []
        b7free = []
        nco_tok = []
        conv_tok = []
        def u_slabs(jj):
            return {'a': P.wslab(w_in3[:, :, 1536 + 256 * jj:1536 + 256 * jj + 256], KC, 256),
                    'g': P.wslab(w_in3[:, :, 2560 + 256 * jj:2560 + 256 * jj + 256], KC, 256)}
        nxt = u_slabs(0)
        for j in range(8):
            jj, dd = divmod(j, 2)
            if dd == 0:
                slabs = nxt
                if jj + 1 < 4:
                    nxt = u_slabs(jj + 1)
            s = j % 2
            for r in range(4):
                tst = P.load(P.SP, sld[s], stfree[s] if r == 0 else [], stc[s][r],
                             sconv[4 * r:4 * r + 4, :, j * 128:(j + 1) * 128].rearrange("n i c -> (n i) c"))
            wj = cwT[:, j, :]
            dg_tok = []

            def build_dg(n_):
                for _ in range(n_):
                    t_ = len(dg_tok)
                    if t_ >= 31:
                        return
                    r_ = dgn[0] % NDG
                    dgn[0] += 1
                    dg_tok.append((r_, P.A([dgfree[r_], cdone], "activation", dgr[r_], identf, AF.Copy, scale=wj[:, t_:t_ + 1])))
            build_dg(NDG)
            P.PE.wait(slabs['a'][1], slabs['g'][1], evfree, hT_tok)
            for k in range(KC):
                for pi, (c0, n) in enumerate(pieces):
                    ins = nc.tensor.matmul(PSb(pi)[:, 0:n], lhsT=slabs['g'][0][:, k, dd * 128:(dd + 1) * 128],
                                           rhs=hT[:, k, c0:c0 + n], start=(k == 0), stop=(k == KC - 1))
            tmmG = P.PE.sig(ins)
            tSs = []
            off = 0
            for pi, (c0, n) in enumerate(pieces):
                tSs.append(P.A([tmmG], "activation", sg[:, off:off + n], PSb(pi)[:, 0:n], AF.Sigmoid))
                off += n
            P.PE.wait(tSs)
            for k in range(KC):
                for pi, (c0, n) in enumerate(pieces):
                    ins = nc.tensor.matmul(PSb(pi)[:, 0:n], lhsT=slabs['a'][0][:, k, dd * 128:(dd + 1) * 128],
                                           rhs=hT[:, k, c0:c0 + n], start=(k == 0), stop=(k == KC - 1))
            tmm = P.PE.sig(ins)
            if dd == 1:
                P.wrelease(slabs['a'][2], tmm)
                P.wrelease(slabs['g'][2], tmm)
            P.PE.wait(tst, b6free)
            for r in range(4):
                ins = nc.tensor.transpose(PSb(6)[:, r * 120:(r + 1) * 120], stc[s][r], identf[0:120, 0:120])
            tst_t = P.PE.sig(ins)
            stfree[s] = [tst_t]
            tcp = P.V([tst_t, glfree[s]], "tensor_copy", gls[s][:, :, 0:30],
                      PSb(6)[:, 0:480].rearrange("p (a b) -> p a b", a=NSEQ))
            b6free = [tcp]
            tg0 = P.V([tmm, tSs[0], glfree[s]], "tensor_tensor", glp[s][:, 0:512], PSb(0)[:, 0:512], sg[:, 0:512], ALU.mult)
            tg1 = P.V([tSs[1]], "tensor_tensor", glp[s][:, 512:1024], PSb(1)[:, 0:512], sg[:, 512:1024], ALU.mult)
            tg2 = P.V([tSs[2]], "tensor_tensor", glp[s][:, 1024:1072], PSb(2)[:, 0:48], sg[:, 1024:1072], ALU.mult)
            tg3 = P.V([], "tensor_tensor", gsn, PSb(2)[:, 48:176], sg[:, 1072:1200], ALU.mult)
            evfree = [tg3]
            tg4 = P.V([tg3, tcp], "tensor_copy", gls[s][:, :, 30:38], gsn.rearrange("p (a b) -> p a b", a=NSEQ))
            P.PE.wait(tg2, tg3, b7free)
            nc.tensor.transpose(PSb(7)[0:32, 0:128], glp[s][:, 1040:1072], identf)
            ins = nc.tensor.transpose(PSb(7)[:, 128:256], gsn, identf)
            tno = P.PE.sig(ins)
            P.A([tno], "copy", ncp[:, j * 128:(j + 1) * 128], PSb(7)[0:32, 0:128])
            tnc = P.A([], "copy", ncs[:, j * 128:(j + 1) * 128], PSb(7)[:, 128:256])
            b7free = [tnc]
            nco_tok = [tnc]
            bj = vT[:, V_CB + j:V_CB + j + 1]
            P.PE.wait(tg4, b3free)
            for t in range(31):
                r, tk = dg_tok[t]
                P.PE.wait(tk)
                ins = nc.tensor.matmul(PSb(3)[:, 0:128], lhsT=dgr[r], rhs=gls[s][:, :, t:t + 8], start=(t == 0), stop=(t == 30))
                if t % 4 == 3 or t == 30:
                    tkd = P.PE.sig(ins)
                    for t2 in range(t - (t % 4), t + 1):
                        dgfree[dg_tok[t2][0]] = [tkd]
                    build_dg(4)
            tsv = P.A([tkd], "activation", conv_out[:, j, NH + NM:TX], PSb(3)[:, 0:128], AF.Identity, bias=bj)
            b3free = [tsv]
            yp = conv_out[:, j, 0:NH + NM]
            tpv = P.V([tg0, tg1, tg2], "tensor_scalar", yp, glp[s][:, 2:2 + 1040], wj[:, 0:1], bj, ALU.mult, ALU.add)
            if TD == 31:
                y2p = cv2[s][:, 0:NH + NM]
                tp2 = P.V([tg0, tg1, tg2, cv2free[s]], "tensor_scalar", y2p, glp[s][:, 3:3 + 1040], wj[:, 1:2], 0.0, ALU.mult, ALU.add)
                for t in range(2, 31):
                    if t % 2 == 0:
                        tpv = P.V([tpv], "scalar_tensor_tensor", yp, glp[s][:, 2 + t:2 + t + 1040], wj[:, t:t + 1], yp, ALU.mult, ALU.add)
                    else:
                        tp2 = P.V([tp2], "scalar_tensor_tensor", y2p, glp[s][:, 2 + t:2 + t + 1040], wj[:, t:t + 1], y2p, ALU.mult, ALU.add)
                tpv = P.V([tpv, tp2], "tensor_tensor", yp, yp, y2p, ALU.add)
                cv2free[s] = [tpv]
                cv2free[1 - s] = [tpv]
            for t in range(1, TD if TD < 31 else 0):
                tpv = P.V([tpv], "scalar_tensor_tensor", yp, glp[s][:, 2 + t:2 + t + 1040], wj[:, t:t + 1], yp, ALU.mult, ALU.add)
                tsv = P.V([tsv], "scalar_tensor_tensor", ysv, gls[s][:, :, t:t + 8], wj[:, t:t + 1], ysv, ALU.mult, ALU.add)
            if TD < 31:
                def prod(dst, t, extra):
                    ta_ = P.A([tg0, tg1, tg2, extra], "activation", dst[:, 0:NH + NM], glp[s][:, 2 + t:2 + t + 1040], AF.Copy, scale=wj[:, t:t + 1])
                    tb_ = P.A([tg4], "activation", dst[:, NH + NM:TX].rearrange("p (a b) -> p a b", a=NSEQ), gls[s][:, :, t:t + 8], AF.Copy,
                              scale=wj[:, t:t + 1])
                    return [ta_, tb_]
                tacc = prod(cv2[s], TD, cv2free[s])
                for t in range(TD + 1, 31):
                    k2 = t % 2
                    tpr = prod(ctmp[k2], t, ctfree[k2])
                    tacc = [P.G([tacc, tpr], "tensor_tensor", cv2[s], cv2[s], ctmp[k2], ALU.add)]
                    ctfree[k2] = tacc
                tpv = P.V([tpv, tsv, tacc], "tensor_tensor", conv_out[:, j, :], conv_out[:, j, :], cv2[s], ALU.add)
                tsv = tpv
                cv2free[s] = [tpv]
                cv2free[1 - s] = [tpv]
            glfree[s] = [tpv, tsv]
            conv_tok = [tpv, tsv]
        P.store(nco_tok, nc_p[:, :], ncp[2:32, :])
        P.store(nco_tok, nc_s[:, 22:30, :], ncs)
        P.store([], nc_s[:, 0:22, :], sconv[:, 8:30, :])
        nco_store = P.new_out_sem()
        P.dump("conv_out", conv_out, conv_tok)
        P.phase_end("P1")

        RA.reset()
        mixT = RA.carve([128, KC, TX], BF16)
        RX.reset(8 * TX)
        sqs = [RX.carve([128, TX], F32) for _ in range(2)]
        rsb = RX.carve([128, TX], F32)
        xpieces = [(0, 512), (512, 512), (1024, 144)]
        P.PE.wait(conv_tok, evfree)
        for j in range(8):
            for pi, (c0, n) in enumerate(xpieces):
                ins = nc.tensor.matmul(PSb(pi)[:, 0:n], lhsT=onesf, rhs=conv_out[:, j, c0:c0 + n], start=(j == 0), stop=(j == 7))
        tmean = P.PE.sig(ins)
        sqfree = [[], []]
        yc_tok = []
        for j in range(8):
            tks = []
            for pi, (c0, n) in enumerate(xpieces):
                tks.append(P.V([tmean], "tensor_tensor", conv_out[:, j, c0:c0 + n], conv_out[:, j, c0:c0 + n], PSb(pi)[:, 0:n], ALU.subtract))
            tq = P.A([tks, sqfree[j % 2]], "activation", sqs[j % 2], conv_out[:, j, :], AF.Square)
            P.PE.wait(tq)
            for pi, (c0, n) in enumerate(xpieces):
                ins = nc.tensor.matmul(PSb(3 + pi)[:, 0:n], lhsT=onesf, rhs=sqs[j % 2][:, c0:c0 + n], start=(j == 0), stop=(j == 7))
            sqfree[j % 2] = [P.PE.sig(ins)]
            yc_tok.append(tks)
        tvar = sqfree[1]
        tr = []
        for pi, (c0, n) in enumerate(xpieces):
            tr.append(P.V([tvar], "tensor_scalar", rsb[:, c0:c0 + n], PSb(3 + pi)[:, 0:n], 1.0, LN_EPS, ALU.mult, ALU.add))
        tr2 = P.A([tr], "sqrt", rsb, rsb)
        tr3 = P.V([tr2], "reciprocal", rsb, rsb)
        ln_tok = []
        for j in range(8):
            tz = P.V([tr3, yc_tok[j]], "tensor_tensor", conv_out[:, j, :], conv_out[:, j, :], rsb, ALU.mult)
            ln_tok.append(P.A([tz], "activation", mixT[:, 8 + j, :], conv_out[:, j, :], AF.Silu,
                              bias=vT[:, V_LB + j:V_LB + j + 1], scale=vT[:, V_LG + j:V_LG + j + 1]))
        ps_free = [tr, yc_tok[-1]]
        P.dump("cT", mixT[:, 8:16, :], ln_tok)
        P.phase_end("P2")
        RX.reset()

        qT = RX.carve([128, 8, TX], BF16)
        kT = RX.carve([128, 4, TH], BF16)
        vaug = RX.carve([128, 11, 4 * 80], BF16)
        sq3 = [RX.carve([128, 512], F32) for _ in range(4)]
        qn = [RX.carve([128, 512], F32) for _ in range(4)]
        qb = [RX.carve([128, 512], BF16) for _ in range(4)]
        kst = [RX.carve([128, 256], F32) for _ in range(3)]
        vst = [RX.carve([128, 256], F32) for _ in range(3)]
        p3free = [ln_tok, conv_tok, nco_store]
        tva = P.V([p3free], "memset", vaug, 1.0)
        bankfree = {b: [ps_free] for b in range(8)}
        qfree = [[tva], [tva], [tva], [tva]]
        qT_tok, kT_tok, v_tok = [], [], []
        groups = []
        for g in range(3):
            tl3 = list(range(1, 11)) if g < 2 else list(range(0, 11))
            for ti in tl3:
                groups.append({"g": g, "ti": ti, "first": ti == tl3[0], "last": ti == tl3[-1]})
        for n, c in enumerate(groups):
            c["n"] = n
            c["b"] = n % 4
            c["tb"] = 4 + n % 3
            c["s"] = n % 4
        slab3 = {}
        lastmm_box = [None]

        def s0(c):
            g, ti, b = c["g"], c["ti"], c["b"]
            src, nr, col = tiles[ti]
            if c["first"]:
                slab3[g] = P.wslab2(w_in3[:, :, 512 * g:512 * g + 512], KC, 512)
            sl2 = slab3[g]
            P.PE.wait(sl2[1], bankfree[b], hT_tok)
            for k in range(KC):
                ins = nc.tensor.matmul(PSb(b)[:nr, :], lhsT=hT[:, k, col:col + nr], rhs=sl2[0][:, k, :], start=(k == 0), stop=(k == KC - 1))
            c["tmm"] = P.PE.sig(ins)
            lastmm_box[0] = c["tmm"]
            if c["last"]:
                P.wrelease(sl2[2], c["tmm"])
                P.wrelease(sl2[2] + 1, c["tmm"])

        def s1(c):
            g, ti, b, s = c["g"], c["ti"], c["b"], c["s"]
            src, nr, col = tiles[ti]
            nhd = 8 if g < 2 else 4
            wd = nhd * 64
            c["t_a"] = P.A([c["tmm"], qfree[s]], "activation", sq3[s][:nr, 0:wd], PSb(b)[:nr, 0:wd], AF.Square)
            c["tvs"] = []
            if g == 2:
                oi = {8: 0, 9: 1, 10: 2}.get(ti, None)
                vsrc = PSb(b)[:nr, 256:512]
                t_v = P.A([], "copy", vaug[:nr, ti, :].rearrange("p (h e) -> p h e", e=80)[:, :, 0:64], vsrc.rearrange("p (h d) -> p h d", d=64))
                c["tvs"] = [t_v]
                v_tok.append(t_v)
                if oi is not None:
                    t_v2 = P.A([], "copy", vst[oi][:nr], vsrc)
                    c["tvs"].append(t_v2)
                    if ti == 8:
                        P.store([t_v2], nv_p[0:112, :], vst[0][16:128, :])
                    elif ti == 9:
                        P.store([t_v2], nv_p[112:128, :], vst[1][0:16, :])
                    else:
                        P.store([t_v2], nv_s[:, 120:128, :], vst[2])

        def s2(c):
            g, ti, s = c["g"], c["ti"], c["s"]
            src, nr, col = tiles[ti]
            nhd = 8 if g < 2 else 4
            wd = nhd * 64
            ssq = sm[:nr, 8 + 8 * s:8 + 8 * s + nhd]
            t_b = P.V([c["t_a"]], "tensor_reduce", ssq, sq3[s][:nr, 0:wd].rearrange("p (h d) -> p h d", d=64), AX.X, ALU.add)
            c["t_c"] = P.V([t_b], "tensor_scalar", ssq, ssq, 1.0 / 64, RMS_EPS, ALU.mult, ALU.add)

        def s3(c):
            g, ti, s = c["g"], c["ti"], c["s"]
            src, nr, col = tiles[ti]
            nhd = 8 if g < 2 else 4
            ssq = sm[:nr, 8 + 8 * s:8 + 8 * s + nhd]
            c["t_d"] = P.A([c["t_c"]], "sqrt", ssq, ssq)

        def s4(c):
            g, ti, b, s = c["g"], c["ti"], c["b"], c["s"]
            src, nr, col = tiles[ti]
            nhd = 8 if g < 2 else 4
            wd = nhd * 64
            ssq = sm[:nr, 8 + 8 * s:8 + 8 * s + nhd]
            t_e = P.V([c["t_d"]], "reciprocal", ssq, ssq)
            qn3 = qn[s][:nr, 0:wd].rearrange("p (h d) -> p h d", d=64)
            t_f = P.V([t_e], "tensor_tensor", qn3, PSb(b)[:nr, 0:wd].rearrange("p (h d) -> p h d", d=64),
                      ssq.unsqueeze(2).to_broadcast([nr, nhd, 64]), ALU.mult)
            bankfree[b] = [t_f, c["tvs"]]
            if g < 2:
                t_g = P.V([t_f], "tensor_tensor", qb[s][:nr, :].rearrange("p (h d) -> p h d", d=64), qn3,
                          qg_bc[:nr].unsqueeze(1).to_broadcast([nr, 8, 64]), ALU.mult)
                c["rdy"] = [t_g]
            else:
                oi = {8: 0, 9: 1, 10: 2}.get(ti, None)
                kdst = kst[oi] if oi is not None else qn[s][:, 256:512]
                kd3 = kdst[:nr].rearrange("p (h d) -> p h d", d=64)
                t_g = P.V([t_f], "tensor_tensor", kd3, qn3, kg_bc[:nr].unsqueeze(1).to_broadcast([nr, 4, 64]), ALU.mult)
                kd = qb[s][:nr, :].rearrange("p (h t d) -> p h t d", h=4, t=2)
                t_h = P.V([t_g], "tensor_copy", kd[:, :, 0, :], kd3)
                t_i = P.V([t_g], "tensor_copy", kd[:, :, 1, :], kd3)
                c["rdy"] = [t_h, t_i]
                if ti == 8:
                    P.store([t_g], nk_p[0:112, :], kst[0][16:128, :])
                elif ti == 9:
                    P.store([t_g], nk_p[112:128, :], kst[1][0:16, :])
                elif ti == 10:
                    P.store([t_g], nk_s[:, 120:128, :], kst[2])

        def s5(c):
            g, ti, tb, s = c["g"], c["ti"], c["tb"], c["s"]
            src, nr, col = tiles[ti]
            P.PE.wait(c["rdy"], bankfree[tb])
            pb = PSb(tb).bitcast(BF16)[:, 0:512].rearrange("p (a b) -> p a b", a=4)
            for cc in range(4):
                ins = nc.tensor.transpose(pb[:, cc, 0:nr], qb[s][:nr, cc * 128:(cc + 1) * 128], identb[:nr, :nr])
            c["t_t"] = P.PE.sig(ins)

        def s6(c):
            g, ti, tb, s = c["g"], c["ti"], c["tb"], c["s"]
            src, nr, col = tiles[ti]
            pb = PSb(tb).bitcast(BF16)[:, 0:512].rearrange("p (a b) -> p a b", a=4)
            if g < 2:
                t_q = P.A([c["t_t"]], "copy", qT[:, 4 * g:4 * g + 4, col - NE:col - NE + nr], pb[:, :, 0:nr])
                qT_tok.append(t_q)
            else:
                t_q = P.A([c["t_t"]], "copy", kT[:, :, col:col + nr], pb[:, :, 0:nr])
                kT_tok.append(t_q)
            bankfree[tb] = [t_q]
            qfree[s] = [c["t_t"], c["rdy"]]

        NG = len(groups)
        gok = lambda n: 0 <= n < NG
        for i in range(NG + 3):
            if gok(i):
                s0(groups[i])
            if gok(i - 3):
                s5(groups[i - 3])
                s6(groups[i - 3])
            if gok(i - 2):
                s3(groups[i - 2])
                s4(groups[i - 2])
            if gok(i - 1):
                s1(groups[i - 1])
                s2(groups[i - 1])
        lastmm = lastmm_box[0]
        P.store([], nk_s[:, 0:120, :], ck[:, 8:128, :])
        P.store([], nv_s[:, 0:120, :], cv[:, 8:128, :])
        P.dump("qT", qT, qT_tok)
        P.dump("kT", kT, kT_tok)
        P.dump("vaug", vaug, v_tok)
        P.phase_end("P3")
        allfree = [bankfree[b] for b in range(8)]

        pT = [RX.carve([128, 4, 128], BF16) for _ in range(4)]
        att = [RX.carve([128, 1024], BF16) for _ in range(2)]
        den = [RX.carve([128, 16], F32) for _ in range(2)]
        RH.reset()
        ckl = [RH.carve([128, 256], F32) for _ in range(4)]
        cvl = [RH.carve([128, 256], F32) for _ in range(4)]
        ckd = [RH.carve([128, 4, 2, 64], BF16) for _ in range(4)]
        kcT = [RH.carve([128, 4, 128], BF16) for _ in range(4)]
        vca = [RH.carve([128, 4 * 80], BF16) for _ in range(4)]
        qbd = [RH.carve([128, 8, 2, 128], BF16) for _ in range(2)]
        cld2 = [DSem(P, f"ck{i}") for i in range(4)]
        tvc = P.V([lastmm], "memset", vca[0], 1.0)
        tvc = P.V([], "memset", vca[1], 1.0)
        tvc = P.V([], "memset", vca[2], 1.0)
        tvc = P.V([], "memset", vca[3], 1.0)
        tz0 = P.V([lastmm], "memset", qbd[0], 0.0)
        tz1 = P.V([], "memset", qbd[1], 0.0)
        qbd_free = [[tz0, tz1], [tz0, tz1]]
        qbd_tok = {}
        last_qk_of_tile = {}
        qtiles = []
        for ti in range(1, 10):
            src, nr, col = tiles[ti]
            prev = tiles[ti - 1]
            qtiles.append((ti, nr, col - NE, [("p", ti - 1, 128, 2 if ti == 1 else 0), ("p", ti, nr, 3 if ti == 1 else 1)]))
        qtiles.append((10, 128, tiles[10][2] - NE, [("c", n, 128, 4 + n) for n in range(NSEQ)] + [("p", 10, 128, 20)]))
        sbank = 0
        sfree = [allfree] * 4
        pfree = [[], [], [], []]
        ofree = [allfree]
        tbfree = [allfree]
        cfree = [[tvc], [tvc], [tvc], [tvc]]
        ncache = 0
        units = []
        for qi, (ti, nq, xc, keys) in enumerate(qtiles):
            for ki, kspec in enumerate(keys):
                for kh in range(4):
                    units.append((qi, ti, nq, xc, ki, len(keys), kspec, kh))
        qk_tok = {}
        cache_ready = {}
        cache_late = set()

        cache_dma = {}

        def prep_dma(n):
            if n >= NSEQ or n in cache_dma:
                return
            s = n % 4
            t1 = P.load(P.SP, cld2[s], cfree[s], ckl[s], ck[n])
            cache_dma[n] = P.load(P.SP, cld2[s], [], cvl[s], cv[n])

        def prep_cache(n):
            if n >= NSEQ or n in cache_ready:
                return
            prep_dma(n)
            s = n % 4
            t2 = cache_dma[n]
            a = P.V([t2], "tensor_copy", ckd[s][:, :, 0, :], ckl[s].rearrange("p (h d) -> p h d", d=64))
            b_ = P.V([], "tensor_copy", ckd[s][:, :, 1, :], ckl[s].rearrange("p (h d) -> p h d", d=64))
            c_ = P.V([], "tensor_copy", vca[s].rearrange("p (h e) -> p h e", e=80)[:, :, 0:64], cvl[s].rearrange("p (h d) -> p h d", d=64))
            cache_ready[n] = (a, b_, c_)

        def prep_cache_late(n):
            if n >= NSEQ or n in cache_late:
                return
            cache_late.add(n)
            prep_cache(n)
            s = n % 4
            a, b_, c_ = cache_ready[n]
            P.PE.wait(a, b_, tbfree[0])
            pb = PSb(7).bitcast(BF16)[:, 0:512].rearrange("p (a b) -> p a b", a=4)
            for kh2 in range(4):
                ins = nc.tensor.transpose(pb[:, kh2, :], ckd[s][:, kh2].rearrange("p t d -> p (t d)"), identb)
            tt = P.PE.sig(ins)
            tq = P.A([tt], "copy", kcT[s], pb)
            tbfree[0] = [tq]
            cache_ready[n] = (tq, c_)

        def build_qbd(qi_):
            if qi_ >= len(qtiles) or qi_ in qbd_tok:
                return
            ti_, nq_, xc_, _k = qtiles[qi_]
            s_ = qi_ % 2
            fr = [qbd_free[s_], qT_tok, last_qk_of_tile.get(qi_ - 2)]
            ta_ = P.A(fr, "copy", qbd[s_][0:64, :, 0, 0:nq_], qT[0:64, :, xc_:xc_ + nq_])
            tb_ = P.A(fr, "copy", qbd[s_][64:128, :, 1, 0:nq_], qT[64:128, :, xc_:xc_ + nq_])
            qbd_tok[qi_] = [ta_, tb_]

        def emit_qk(u):
            qi, ti, nq, xc, ki, nk, kspec, kh = units[u]
            b = u % 4
            kind, kidx, nkeys, mi = kspec
            if ki == 0 and kh == 0:
                build_qbd(qi)
                build_qbd(qi + 1)
            deps = [sfree[b], qbd_tok[qi]]
            if kind == "c":
                if kh == 0:
                    prep_cache_late(kidx)
                    prep_cache_late(kidx + 1)
                deps.append(cache_ready[kidx][0])
            else:
                deps.append(kT_tok)
            P.PE.wait(deps)
            sv4 = PSb(b)[:nkeys, :].rearrange("p (a b) -> p a b", a=4)
            qb_ = qbd[qi % 2]
            for g2 in range(2):
                ch = 2 * kh + g2
                if kind == "c":
                    lhsT = kcT[kidx % 4][:, kh, :]
                else:
                    kcol = tiles[kidx][2]
                    lhsT = kT[:, kh, kcol:kcol + nkeys]
                if nq == 128:
                    ins = nc.tensor.matmul(PSb(b)[:nkeys, 256 * g2:256 * g2 + 256], lhsT=lhsT, rhs=qb_[:, ch, :, :].rearrange("p a b -> p (a b)"),
                                           start=True, stop=True, skip_group_check=True)
                else:
                    for e in range(2):
                        ins = nc.tensor.matmul(sv4[:, 2 * g2 + e, 0:nq], lhsT=lhsT, rhs=qb_[:, ch, e, 0:nq],
                                               start=True, stop=True, skip_group_check=True)
            qk_tok[u] = P.PE.sig(ins)
            last_qk_of_tile[qi] = qk_tok[u]

        em_tok = {}

        def emit_em(u):
            qi, ti, nq, xc, ki, nk, kspec, kh = units[u]
            b = u % 4
            kind, kidx, nkeys, mi = kspec
            sv = PSb(b)[:nkeys, :].rearrange("p (a b) -> p a b", a=4)[:, :, 0:nq]
            pv = pT[b][:nkeys, :, 0:nq]
            te = P.A([qk_tok[u], pfree[b]], "activation", pv, sv, AF.Exp)
            sfree[b] = [te]
            em_tok[u] = P.V([te], "tensor_tensor", pv, pv, maskb[:nkeys, mi, 0:nq].unsqueeze(1).to_broadcast([nkeys, 4, nq]), ALU.mult)

        def emit_pv(u):
            qi, ti, nq, xc, ki, nk, kspec, kh = units[u]
            b = u % 4
            kind, kidx, nkeys, mi = kspec
            deps = [em_tok[u]]
            if ki == 0 and kh == 0:
                deps.append(ofree[0])
            if kind == "c":
                deps.append(cache_ready[kidx][1])
            else:
                deps.append(v_tok)
            P.PE.wait(deps)
            for g4 in range(4):
                h = 4 * kh + g4
                kvh = kh
                ob, osl = divmod(h, 6)
                if kind == "c":
                    rhs = vca[kidx % 4][:, kvh * 80:kvh * 80 + 66]
                else:
                    rhs = vaug[:nkeys, kidx, kvh * 80:kvh * 80 + 66]
                ins = nc.tensor.matmul(PSb(4 + ob)[:nq, osl * 72:osl * 72 + 66], lhsT=pT[b][:nkeys, g4, 0:nq], rhs=rhs,
                                       start=(ki == 0 and h in (0, 6, 12)), stop=(ki == nk - 1), skip_group_check=True)
            tpv = P.PE.sig(ins)
            pfree[b] = [tpv]
            if kind == "c" and kh == 3:
                cfree[kidx % 4] = [tpv]
            return tpv

        att_tok = []
        LOOK = 3
        for n0 in range(4):
            prep_dma(n0)
        for n0 in range(3):
            prep_cache(n0)
        for u in range(min(LOOK, len(units))):
            emit_qk(u)
        emit_em(0)
        for u in range(len(units)):
            if u + LOOK < len(units):
                emit_qk(u + LOOK)
            if u + 1 < len(units):
                emit_em(u + 1)
            tpv = emit_pv(u)
            qi, ti, nq, xc, ki, nk, kspec, kh = units[u]
            if self.upto == f"P4u{u}":
                raise _Stop()
            if kspec[0] == "c" and kh == 3:
                prep_dma(kspec[1] + 4)
                prep_cache(kspec[1] + 3)
            if ki == nk - 1 and kh == 3:
                s = qi % 2
                tn = []
                hb_ = [(0, 6), (6, 6), (12, 4)]
                ovs = [PSb(4 + ob)[:nq, 0:nh_ * 72].rearrange("p (h e) -> p h e", e=72) for ob, (h0, nh_) in enumerate(hb_)]
                tas = [P.V([tpv], "tensor_tensor", den[s][:nq, h0:h0 + nh_], ovs[ob][:, :, 64], esink[:nq, h0:h0 + nh_], ALU.add)
                       for ob, (h0, nh_) in enumerate(hb_)]
                tbs = [P.V([tas[ob]], "reciprocal", den[s][:nq, h0:h0 + nh_], den[s][:nq, h0:h0 + nh_]) for ob, (h0, nh_) in enumerate(hb_)]
                for ob, (h0, nh_) in enumerate(hb_):
                    tn.append(P.V([tbs[ob]], "tensor_tensor", att[s][:nq, h0 * 64:(h0 + nh_) * 64].rearrange("p (h d) -> p h d", d=64),
                                  ovs[ob][:, :, 0:64], den[s][:nq, h0:h0 + nh_].unsqueeze(2).to_broadcast([nq, nh_, 64]), ALU.mult))
                ofree[0] = [tn]
                P.PE.wait(tn, tbfree[0])
                pb = PSb(7).bitcast(BF16).rearrange("p (a b) -> p a b", a=8)
                for c in range(8):
                    ins = nc.tensor.transpose(pb[:, c, 0:nq], att[s][:nq, c * 128:(c + 1) * 128], identb[:nq, :nq])
                tt = P.PE.sig(ins)
                tq = P.A([tt], "copy", mixT[:, 0:8, xc:xc + nq], pb[:, :, 0:nq])
                tbfree[0] = [tq]
                att_tok.append(tq)
                if self.upto == f"P4q{qi}":
                    P.dump("attT", mixT[:, 0:8, :], att_tok)
                    raise _Stop()
        P.dump("attT", mixT[:, 0:8, :], att_tok)
        P.phase_end("P4")
        p4done = [att_tok, ofree[0], tbfree[0], [sfree[b] for b in range(4)], [pfree[b] for b in range(4)], P.new_out_sem()]

        RX.reset()
        xT = RX.carve([128, KC, TX], F32)
        RH.reset()
        xt5 = [RH.carve([128, D], F32) for _ in range(2)]
        xfree = [[p4done, hT_tok], [p4done, hT_tok]]
        bk5 = [[p4done]] * 4
        xT_tok = []
        nb5 = 0
        for ti in range(1, 11):
            src, nr, col = tiles[ti]
            xc = col - NE
            s = ti % 2
            tld = P.load(P.SP, xld[s], xfree[s], xt5[s][:nr], src)
            for g in range(4):
                b = nb5 % 4
                nb5 += 1
                P.PE.wait(tld, bk5[b])
                pv = PSb(b).rearrange("p (a b) -> p a b", a=4)
                for c in range(4):
                    k = 4 * g + c
                    ins = nc.tensor.transpose(pv[:, c, 0:nr], xt5[s][:nr, k * 128:(k + 1) * 128], identf[:nr, :nr])
                tt = P.PE.sig(ins)
                if g % 2 == 0:
                    tc5 = P.A([tt, p4done], "copy", xT[:, 4 * g:4 * g + 4, xc:xc + nr], pv[:, :, 0:nr])
                else:
                    tc5 = P.V([tt, p4done], "tensor_copy", xT[:, 4 * g:4 * g + 4, xc:xc + nr], pv[:, :, 0:nr])
                bk5[b] = [tc5]
                xT_tok.append(tc5)
            xfree[s] = [tt]
        P.dump("xT0", xT, xT_tok)
        P.phase_end("P5")

        self.yset = 0
        self.yfree = [[xT_tok, bk5], [xT_tok, bk5]]

        def proj_add(slab_iter, nk, act, act_tok, cpieces, xoff, scale_col=None, on_done=None):
            last = None
            pending = None
            for (wv, wtok, slot, d0, nd) in slab_iter:
                for dd in range(nd):
                    d = d0 + dd
                    ys = self.yset
                    self.yset ^= 1
                    P.PE.wait(wtok, act_tok, self.yfree[ys])
                    for k in range(nk):
                        for pi, (c0, n) in enumerate(cpieces):
                            ins = nc.tensor.matmul(PSb(3 * ys + pi)[:, 0:n], lhsT=wv[:, k, dd * 128:(dd + 1) * 128],
                                                   rhs=act[:, k, c0:c0 + n], start=(k == 0), stop=(k == nk - 1))
                    tmm = P.PE.sig(ins)
                    tks = []
                    for pi, (c0, n) in enumerate(cpieces):
                        xv = xT[:, d, xoff + c0 - cpieces[0][0]:xoff + c0 - cpieces[0][0] + n]
                        if scale_col is None:
                            tks.append(P.V([tmm, xT_tok], "tensor_tensor", xv, xv, PSb(3 * ys + pi)[:, 0:n], ALU.add))
                        else:
                            tks.append(P.V([tmm, xT_tok], "scalar_tensor_tensor", xv, PSb(3 * ys + pi)[:, 0:n],
                                           vT[:, scale_col + d:scale_col + d + 1], xv, ALU.mult, ALU.add))
                    self.yfree[ys] = [tks]
                    last = tks
                    if on_done is not None:
                        if pending is not None:
                            on_done(*pending)
                        pending = (d, tks)
                P.wrelease(slot, tmm)
            if on_done is not None and pending is not None:
                on_done(*pending)
            return last

        w_out3 = w_out.rearrange("(c p) n -> p c n", p=128)

        def wout_slabs():
            for s8 in range(8):
                wv, wtok, slot = P.wslab(w_out3[:, :, 256 * s8:256 * s8 + 256], KC, 256)
                yield wv, wtok, slot, 2 * s8, 2
        res_tok = proj_add(wout_slabs(), KC, mixT, [att_tok, ln_tok], xpieces, 0)
        res_tok = [res_tok, self.yfree]
        P.dump("xT1", xT, res_tok)
        P.phase_end("P6")

        def rms_stats(x_tok, sqbufs, rs):
            P.PE.wait(self.yfree)
            sfr = [[], []]
            for k in range(KC):
                if k % 2 == 0:
                    tq = P.A([x_tok, sfr[k % 2]], "activation", sqbufs[k % 2], xT[:, k, :], AF.Square)
                else:
                    tq = P.V([x_tok, sfr[k % 2]], "tensor_tensor", sqbufs[k % 2], xT[:, k, :], xT[:, k, :], ALU.mult)
                P.PE.wait(tq)
                for pi, (c0, n) in enumerate(xpieces):
                    ins = nc.tensor.matmul(PSb(pi)[:, 0:n], lhsT=onesb, rhs=sqbufs[k % 2][:, c0:c0 + n], start=(k == 0), stop=(k == KC - 1))
                sfr[k % 2] = [P.PE.sig(ins)]
            tr = []
            for pi, (c0, n) in enumerate(xpieces):
                tr.append(P.V([sfr[1]], "tensor_scalar", rs[:, c0:c0 + n], PSb(pi)[:, 0:n], 1.0, RMS_EPS, ALU.mult, ALU.add))
            tr2 = P.A([tr], "sqrt", rs, rs)
            tr3 = P.V([tr2], "reciprocal", rs, rs)
            self.yfree[0] = [self.yfree[0], tr]
            return tr3

        def ffn(layer, x_tok):
            RH.reset()
            hT2 = RH.carve([128, KC, TH], BF16)
            RA.reset()
            aT = RA.carve([128, 12, TX], BF16)
            sgb = [RA.carve([128, TX], F32) for _ in range(2)]
            sqb = [sgb[0].bitcast(BF16)[:, 0:TX], sgb[1].bitcast(BF16)[:, 0:TX]]
            rs = self.rsF
            t_rs = rms_stats(x_tok, sqb, rs)
            h_tok = []
            for k in range(KC):
                h_tok.append(P.V([t_rs, x_tok], "scalar_tensor_tensor", hT2[:, k, NE:TH], xT[:, k, :],
                                 vT[:, V_NFFN + 16 * layer + k:V_NFFN + 16 * layer + k + 1], rs, ALU.mult, ALU.mult))
            wg3 = w_gate[layer].rearrange("(c p) n -> p c n", p=128)
            wu3 = w_up[layer].rearrange("(c p) n -> p c n", p=128)
            wd3 = w_down[layer].rearrange("(c p) n -> p c n", p=128)
            x0 = 0 if layer == 0 else NH
            cp = [(NE + x0, 512), (NE + x0 + 512, 512), (NE + x0 + 1024, TX - x0 - 1024)]
            dpieces = [(0, 512), (512, 512), (1024, TX - x0 - 1024)]
            sgfree = [[], []]
            a_free = [h_tok]
            last_res = x_tok
            gfree = [self.yfree]
            ufree = [self.yfree]
            for (f0, nf) in QUARTERS:
                a_tok = []
                for sp in range(nf // 2):
                    fs = f0 + 2 * sp
                    gs_ = P.wslab(wg3[:, :, 128 * fs:128 * fs + 256], KC, 256)
                    us_ = P.wslab(wu3[:, :, 128 * fs:128 * fs + 256], KC, 256)
                    for dd in range(2):
                        fi = 2 * sp + dd
                        s = fi % 2
                        P.PE.wait(gs_[1], gfree)
                        for k in range(KC):
                            P.PE.wait(h_tok[k])
                            for pi, (c0, n) in enumerate(cp):
                                ins = nc.tensor.matmul(PSb(pi)[:, 0:n], lhsT=gs_[0][:, k, dd * 128:(dd + 1) * 128], rhs=hT2[:, k, c0:c0 + n],
                                                       start=(k == 0), stop=(k == KC - 1))
                        tg = P.PE.sig(ins)
                        P.PE.wait(us_[1], ufree)
                        for k in range(KC):
                            for pi, (c0, n) in enumerate(cp):
                                ins = nc.tensor.matmul(PSb(3 + pi)[:, 0:n], lhsT=us_[0][:, k, dd * 128:(dd + 1) * 128], rhs=hT2[:, k, c0:c0 + n],
                                                       start=(k == 0), stop=(k == KC - 1))
                        tu = P.PE.sig(ins)
                        tsl = []
                        off = 0
                        for pi, (c0, n) in enumerate(cp):
                            tsl.append(P.A([tg, sgfree[s]], "activation", sgb[s][:, off:off + n], PSb(pi)[:, 0:n], AF.Silu))
                            off += n
                        tml = []
                        off = 0
                        for pi, (c0, n) in enumerate(cp):
                            tml.append(P.V([tu, tsl[pi], a_free], "tensor_tensor", aT[:, fi, off:off + n], sgb[s][:, off:off + n], PSb(3 + pi)[:, 0:n], ALU.mult))
                            off += n
                        sgfree[s] = [tml]
                        gfree = [tsl]
                        ufree = [tml]
                        a_tok.append(tml)
                    P.wrelease(gs_[2], tg)
                    P.wrelease(us_[2], tu)
                self.yfree = [[gfree], [ufree]]

                def wd_slabs():
                    for s8 in range(8):
                        wv, wtok, slot = P.wslab(wd3[:, f0:f0 + nf, 256 * s8:256 * s8 + 256], nf, 256)
                        yield wv, wtok, slot, 2 * s8, 2
                last_res = proj_add(wd_slabs(), nf, aT, a_tok, dpieces, x0,
                                    on_done=(self.out_cb if (layer == 1 and f0 == QUARTERS[-1][0]) else None))
                a_free = [self.yfree]
                gfree = [self.yfree[0]]
                ufree = [self.yfree[1]]
            return [last_res, self.yfree]

        self.RS = Region(P, "RS", TX)
        self.rsF = self.RS.carve([128, TX], F32)
        w_gate = P.din("w_gate", [2, D, DFF])
        w_up = P.din("w_up", [2, D, DFF])
        w_down = P.din("w_down", [2, DFF, D])
        x1_tok = ffn(0, res_tok)
        P.dump("xT2", xT, x1_tok)
        P.phase_end("P8")

        RA.reset()
        sqb = [RA.carve([128, TX], BF16) for _ in range(2)]
        hbp = [RA.carve([128, 1040], F32) for _ in range(2)]
        hbs = [RA.carve([128, NSEQ, 23], F32) for _ in range(2)]
        Sp = [RA.carve([128, 1040], F32) for _ in range(2)]
        Ss = [RA.carve([128, NSEQ, 23], F32) for _ in range(2)]
        spt = [[RA.carve([120, 128], F32) for _ in range(2)] for _ in range(2)]
        hsc = [RA.carve([128, 128], F32) for _ in range(2)]
        npo = [RA.carve([16, 128], F32) for _ in range(2)]
        nso = [RA.carve([128, 128], F32) for _ in range(2)]
        fixb = RA.carve([128, 16], F32)
        RH.reset()
        dpT = RH.carve([128, KC, NM + NS], BF16)
        rs = self.rsF
        pld = [DSem(P, f"pl{i}") for i in range(2)]
        pst = [DSem(P, f"pst{i}") for i in range(2)]
        self.out_sems += pst
        t_rs = rms_stats(x1_tok, sqb, rs)
        hbfree = [[], []]
        sptfree = [[x1_tok], [x1_tok]]
        stgfree = [[], []]
        b6free = [self.yfree]
        b7free = [self.yfree]
        dp_tok = []
        self.yfree = [[self.yfree, b6free, b7free], [self.yfree, b6free, b7free]]
        ppieces = [(0, 512), (512, 512), (1024, 128)]
        pslab = [P.wslab(pool_w[g].rearrange("(c p) n -> p c n", p=128), 4, 512) for g in range(4)]
        pool_last = [None]
        pool_sched = []
        pmm = {}

        def pool_mm(g, e):
            wv, wtok, slot = pslab[g]
            d = 4 * g + e
            ys = self.yset
            self.yset ^= 1
            P.PE.wait(wtok, dp_tok[4 * g:4 * g + 4], self.yfree[ys])
            for cc in range(4):
                for pi, (c0, n) in enumerate(ppieces):
                    ins = nc.tensor.matmul(PSb(3 * ys + pi)[:, 0:n], lhsT=wv[:, cc, e * 128:(e + 1) * 128],
                                           rhs=dpT[:, 4 * g + cc, c0:c0 + n], start=(cc == 0), stop=(cc == 3))
            pmm[d] = (P.PE.sig(ins), ys)
            if e == 3:
                P.wrelease(slot, pmm[d][0])

        def pool_ev(g, e):
            d = 4 * g + e
            tmm, ys = pmm[d]
            tks = []
            for pi, (c0, n) in enumerate(ppieces):
                xv = xT[:, d, NH + c0:NH + c0 + n]
                tks.append(P.V([tmm, x1_tok], "scalar_tensor_tensor", xv, PSb(3 * ys + pi)[:, 0:n],
                               vT[:, V_PSC + d:V_PSC + d + 1], xv, ALU.mult, ALU.add))
            self.yfree[ys] = [tks]
            pool_last[0] = tks

        def pool_steps(g):
            return [lambda: (pool_mm(g, 0), pool_mm(g, 1)),
                    lambda: (pool_ev(g, 0), pool_ev(g, 1), pool_mm(g, 2), pool_mm(g, 3)),
                    lambda: (pool_ev(g, 2), pool_ev(g, 3))]

        def pool_loads(k):
            s = k % 2
            for r in range(2):
                tk = P.load(P.SP, pld[s], sptfree[s] if r == 0 else [], spt[s][r],
                            spool[8 * r:8 * r + 8, :, k * 128:(k + 1) * 128].rearrange("n i c -> (n i) c"))
            return tk

        for k in range(KC):
            s = k % 2
            g = k // 4
            w = (2, 4, 8, 16)[g]
            gcol = vT[:, V_NMIX + 16 + k:V_NMIX + 16 + k + 1]
            if k == 0:
                tsp_next = pool_loads(0)
            tsp = tsp_next
            if k + 1 < KC:
                tsp_next = pool_loads(k + 1)
            t0 = P.V([t_rs, hbfree[s]], "scalar_tensor_tensor", hbp[s][:, 0:1039], xT[:, k, 1:1040], gcol, rs[:, 1:1040], ALU.mult, ALU.mult)
            t0b = P.V([t0], "tensor_scalar", hbp[s][:, 0:15], hbp[s][:, 0:15], hvalid[:, 0:1], 0.0, ALU.mult, ALU.add)
            t1_ = P.V([], "scalar_tensor_tensor", hbs[s][:, :, 15:23], xT[:, k, 1040:1168].rearrange("p (a b) -> p a b", a=NSEQ), gcol,
                      rs[:, 1040:1168].rearrange("p (a b) -> p a b", a=NSEQ), ALU.mult, ALU.mult)
            tcs = P.V([t1_, stgfree[s]], "tensor_copy", hsc[s].rearrange("p (a b) -> p a b", a=NSEQ), hbs[s][:, :, 15:23])
            P.PE.wait(tsp, b6free)
            for r in range(2):
                ins = nc.tensor.transpose(PSb(6)[:, r * 120:(r + 1) * 120], spt[s][r], identf[0:120, 0:120])
            tt = P.PE.sig(ins)
            sptfree[s] = [tt]
            t2_ = P.V([tt, t1_], "tensor_copy", hbs[s][:, :, 0:15], PSb(6)[:, 0:240].rearrange("p (a b) -> p a b", a=NSEQ))
            b6free = [t2_]
            P.PE.wait(t0b, tcs, b7free)
            nc.tensor.transpose(PSb(7)[0:16, 0:128], hbp[s][:, 1023:1039], identf)
            ins = nc.tensor.transpose(PSb(7)[:, 128:256], hsc[s], identf)
            tn7 = P.PE.sig(ins)
            ta7 = P.A([tn7, stgfree[s]], "copy", npo[s], PSb(7)[0:16, 0:128])
            tnp = P.A([], "copy", nso[s], PSb(7)[:, 128:256])
            b7free = [tnp]
            st1 = P.store([ta7, tnp], np_p[:, k * 128:(k + 1) * 128], npo[s][1:16, :], pst[s])
            st2 = P.store([], np_s[:, 7:15, k * 128:(k + 1) * 128], nso[s], pst[s])
            stgfree[s] = [tn7, st2]
            curp, curs = hbp[s], hbs[s]
            tp_, ts_ = [t0b], [t2_, t1_]
            sh = 1
            bi = 0
            while sh < w:
                op_, os_ = Sp[bi % 2], Ss[bi % 2]
                tp_ = [P.V([tp_], "tensor_tensor", op_[:, sh:1039], curp[:, sh:1039], curp[:, 0:1039 - sh], ALU.add)]
                ts_ = [P.V([ts_], "tensor_tensor", os_[:, :, sh:23], curs[:, :, sh:23], curs[:, :, 0:23 - sh], ALU.add)]
                curp, curs = op_, os_
                sh *= 2
                bi += 1
            td = P.V([tp_], "scalar_tensor_tensor", dpT[:, k, 0:NM], curp[:, 15:1039], 1.0 / w, hbp[s][:, 15:1039], ALU.mult, ALU.subtract)
            tf1 = P.V([tp_], "tensor_tensor", fixb, curp[:, 15:31], invc[:, g, :], ALU.mult)
            tf2 = P.V([tf1, td], "tensor_tensor", dpT[:, k, 0:16], fixb, hbp[s][:, 15:31], ALU.subtract)
            te_ = P.V([ts_], "scalar_tensor_tensor", dpT[:, k, NM:NM + NS].rearrange("p (a b) -> p a b", a=NSEQ), curs[:, :, 15:23], 1.0 / w,
                      hbs[s][:, :, 15:23], ALU.mult, ALU.subtract)
            hbfree[s] = [tn7]
            dp_tok.append([tf2, te_, td])
            if pool_sched:
                pool_sched.pop(0)()
            if k % 4 == 3:
                pool_sched += pool_steps(k // 4)
        P.store([], np_s[:, 0:7, :], spool[:, 8:15, :])
        P.dump("dpT", dpT, dp_tok)
        P.phase_end("P9")
        while pool_sched:
            pool_sched.pop(0)()
        last = pool_last[0]
        x2_tok = [last, self.yfree, [(d.sem, d.cnt) for d in pst]]
        P.dump("xT3", xT, x2_tok)
        P.phase_end("P10")
        ostg = self.RS.t[:, 0:1024].rearrange("p (a b) -> p a b", a=8)
        osem = DSem(P, "osem")
        self.out_sems.append(osem)
        ost = {"free": [], "b67": []}
        y_p3 = y_p.rearrange("(t p) c -> p t c", p=128)

        def out_cb(d, tks):
            P.PE.wait(tks, ost["b67"])
            for t in range(8):
                ins = nc.tensor.transpose(PSb(6 + t // 4)[:, (t % 4) * 128:(t % 4) * 128 + 128], xT[:, d, NH + 128 * t:NH + 128 * t + 128], identf)
            tt = P.PE.sig(ins)
            ta = P.A([tt, ost["free"]], "copy", ostg[:, 0:4, :], PSb(6).rearrange("p (a b) -> p a b", a=4))
            tb2 = P.V([tt, ost["free"]], "tensor_copy", ostg[:, 4:8, :], PSb(7).rearrange("p (a b) -> p a b", a=4))
            ost["b67"] = [ta, tb2]
            ost["free"] = [P.store([ta, tb2], y_p3[:, :, d * 128:(d + 1) * 128], ostg, osem)]
        self.out_cb = out_cb
        x3_tok = ffn(1, x2_tok)
        x3_tok = [x3_tok, ost["b67"], ost["free"]]

        RH.reset()
        yt = [RH.carve([128, D], F32) for _ in range(2)]
        ysem = [DSem(P, f"y{i}") for i in range(2)]
        ytfree = [[x3_tok], [x3_tok]]
        bk = [[x3_tok]] * 4
        nb = 0
        for oi in range(8, 9):
            s = oi % 2
            xc = NH + 128 * oi
            dst = y_p[128 * oi:128 * oi + 128, :] if oi < 8 else y_s[:, :]
            tcs = []
            for g in range(4):
                b = nb % 4
                nb += 1
                P.PE.wait(x3_tok, bk[b])
                pv = PSb(b).rearrange("p (a b) -> p a b", a=4)
                for c in range(4):
                    k = 4 * g + c
                    ins = nc.tensor.transpose(pv[:, c, :], xT[:, k, xc:xc + 128], identf)
                tt = P.PE.sig(ins)
                if g % 2 == 0:
                    tc = P.A([tt, ytfree[s]], "copy", yt[s][:, 512 * g:512 * g + 512], PSb(b))
                else:
                    tc = P.V([tt, ytfree[s]], "tensor_copy", yt[s][:, 512 * g:512 * g + 512], PSb(b))
                bk[b] = [tc]
                tcs.append(tc)
            P.SP.wait(tcs)
            tok = ysem[s].add(nc.sync.dma_start(out=dst, in_=yt[s]))
            ytfree[s] = [tok]
        P.SP.wait(ytfree)


def _prep_inputs(inp):
    f = lambda a: np.ascontiguousarray(np.asarray(a, dtype=np.float32))
    x_prompt, x_sample = f(inp["x_prompt"]), f(inp["x_sample"])
    vecs = np.concatenate([f(inp["norm_mix"]).reshape(32, 128), f(inp["norm_ffn"]).reshape(32, 128),
                           f(inp["pool_scale"]).reshape(16, 128), f(inp["conv_b"]).reshape(8, 128),
                           f(inp["conv_ln_g"]).reshape(8, 128), f(inp["conv_ln_b"]).reshape(8, 128)], 0)
    shared = {
        "vecs": f(vecs), "convw": f(inp["conv_w"][0]), "qkg": f(np.stack([inp["q_norm"][0], inp["k_norm"][0]])),
        "sinks": f(inp["sinks"][0]), "w_in": f(inp["w_in"][0]), "w_out": f(inp["w_out"][0]), "pool_w": f(inp["pool_w"][0]),
        "w_gate": f(inp["w_gate"]), "w_up": f(inp["w_up"]), "w_down": f(inp["w_down"]),
    }
    j = np.arange(128)[:, None]
    i = np.arange(128)[None, :]
    mp = (j > i).astype(np.float32)
    mc = (j <= i).astype(np.float32)
    in_maps = []
    for r in range(8):
        b, q = divmod(r, 4)
        c = 1024 * q
        lo = c - (NE + NH)
        xp = np.zeros((NE + NH + NM, D), np.float32)
        s0 = max(lo, 0)
        xp[s0 - lo:] = x_prompt[b, s0:c + NM]
        first = (q == 0)
        masks = np.zeros((128, NMASK, 128), np.float32)
        masks[:, 0], masks[:, 1] = mp, mc
        masks[:, 2] = 0.0 if first else mp
        masks[:, 3] = mc * ((j >= NH) if first else 1.0)
        for n in range(NSEQ):
            masks[:, 4 + n] = ((i // 8) == n) * (j > (i % 8))
        masks[:, 20] = ((j // 8) == (i // 8)) * ((j % 8) <= (i % 8))
        cst = np.zeros((128, 321), np.float32)
        cst[:, 0:128] = np.eye(128)
        cst[:, 128:256] = 1.0
        invc = np.zeros((4, 16), np.float32)
        for g, w in enumerate((2, 4, 8, 16)):
            pos = np.arange(16) + (0 if first else 10 ** 6)
            invc[g] = 1.0 / np.minimum(pos + 1, w)
        cst[:, 256:320] = invc.reshape(1, 64)
        cst[:, 320] = 0.0 if first else 1.0
        m = dict(shared)
        m.update({
            "xp": xp, "xs": f(x_sample[16 * r:16 * r + 16].reshape(128, D)),
            "ck": f(inp["cache_k"][0, 16 * r:16 * r + 16].reshape(16, 128, 256)),
            "cv": f(inp["cache_v"][0, 16 * r:16 * r + 16].reshape(16, 128, 256)),
            "sconv": f(inp["state_conv"][0, 16 * r:16 * r + 16]), "spool": f(inp["state_pool"][0, 16 * r:16 * r + 16]),
            "masks": masks, "cst": cst,
        })
        in_maps.append(m)
    return in_maps


_CACHE = {}


def kernel(**inp):
    debug = tuple(inp.pop("_debug", ()))
    in_maps = _prep_inputs(inp)
    key = debug
    if key not in _CACHE:
        _CACHE[key] = Prog(debug)
    prog = _CACHE[key]
    in_maps = [{k: v for k, v in m.items() if k in prog.in_names} for m in in_maps]
    res = run_bass_kernel_spmd(prog.nc, in_maps, core_ids=list(range(8)))
    R = res.results
    y_prompt = np.zeros((2, 4096, D), np.float32)
    for r in range(8):
        b, q = divmod(r, 4)
        y_prompt[b, 1024 * q:1024 * q + 1024] = R[r]["y_p"]
    y_sample = np.concatenate([R[r]["y_s"].reshape(16, 8, D) for r in range(8)], 0)
    last = [3, 7]
    nkp = np.stack([R[r]["nk_p"].reshape(128, 4, 64) for r in last])[None]
    nvp = np.stack([R[r]["nv_p"].reshape(128, 4, 64) for r in last])[None]
    ncp = np.stack([R[r]["nc_p"] for r in last])[None]
    npp = np.stack([R[r]["np_p"] for r in last])[None]
    nks = np.concatenate([R[r]["nk_s"].reshape(16, 128, 4, 64) for r in range(8)], 0)[None]
    nvs = np.concatenate([R[r]["nv_s"].reshape(16, 128, 4, 64) for r in range(8)], 0)[None]
    ncs = np.concatenate([R[r]["nc_s"] for r in range(8)], 0)[None]
    nps = np.concatenate([R[r]["np_s"] for r in range(8)], 0)[None]
    outs = (y_prompt, y_sample, nkp, nvp, ncp, npp, nks, nvs, ncs, nps)
    outs = tuple(np.ascontiguousarray(o.astype(np.float32)) for o in outs)
    if debug:
        return outs, [{k: R[r]["dbg_" + k] for k in prog.dbg_outs} for r in range(8)]
    return outs
```

```python
import numpy as np
import concourse.bass as bass
import concourse.mybir as mybir
from concourse.bass_utils import run_bass_kernel_spmd

F32 = mybir.dt.float32
BF16 = mybir.dt.bfloat16
AF = mybir.ActivationFunctionType
ALU = mybir.AluOpType
AX = mybir.AxisListType

D = 2048
KC = 16
DFF = 5632
NE, NH, NM, NS = 128, 16, 1024, 128
TX = NH + NM + NS
TH = NE + TX
NSEQ = 16
RMS_EPS = 1e-6
LN_EPS = 1e-5
NMASK = 21
QUARTERS = [(0, 12), (12, 12), (24, 10), (34, 10)]
CONV_DVE_TAPS = 31


def _flat(xs):
    for x in xs:
        if x is None:
            continue
        if isinstance(x, (list,)):
            yield from _flat(x)
        elif isinstance(x, tuple) and len(x) == 2 and isinstance(x[1], int):
            yield x
        else:
            yield from _flat(list(x))


class Eng:
    def __init__(self, P, name, h):
        self.P, self.name, self.h = P, name, h
        self.seen = {}
        self.sem = None
        self.cnt = 0

    def _newsem(self):
        self.sem = self.P.newsem("e" + self.name)
        self.cnt = 0

    def sig(self, ins):
        if self.sem is None or self.cnt >= 3000:
            self._newsem()
        ins.then_inc(self.sem, 1)
        self.cnt += 1
        return (self.sem, self.cnt)

    def wait(self, *toks):
        for sem, c in _flat(toks):
            if self.seen.get(sem.name, 0) >= c:
                continue
            self.h.wait_ge(sem, c)
            self.seen[sem.name] = c


class DSem:
    def __init__(self, P, name):
        self.sem = P.newsem("d" + name)
        self.cnt = 0

    def add(self, ins):
        ins.then_inc(self.sem, 16)
        self.cnt += 16
        return (self.sem, self.cnt)


class Region:
    def __init__(self, P, name, nfloat):
        self.t = P.nc.alloc_sbuf_tensor(name, [128, nfloat], F32)
        self.n = nfloat
        self.o = 0

    def reset(self, o=0):
        self.o = o

    def carve(self, shape, dt):
        n = int(np.prod(shape[1:]))
        nf = (n * (2 if dt == BF16 else 4) + 3) // 4
        nf = (nf + 7) // 8 * 8
        assert self.o + nf <= self.n, (self.o, nf, self.n, shape)
        v = self.t[:shape[0], self.o:self.o + nf]
        self.o += nf
        if dt == BF16:
            v = v.bitcast(BF16)
        v = v[:, 0:n]
        if len(shape) == 3:
            v = v.rearrange("p (a b) -> p a b", a=shape[1])
        elif len(shape) == 4:
            v = v.rearrange("p (a b c) -> p a b c", a=shape[1], b=shape[2])
        return v


class _Stop(Exception):
    pass


class Prog:
    def __init__(self, debug=()):
        self.debug = set(debug)
        self.upto = None
        for d in debug:
            if d.startswith("upto:"):
                self.upto = d[5:]
        self.nsem = 0
        nc = self.nc = bass.Bass("TRN2", target_bir_lowering=False)
        self.PE = Eng(self, "pe", nc.tensor)
        self.ACT = Eng(self, "act", nc.scalar)
        self.DVE = Eng(self, "dve", nc.vector)
        self.POOL = Eng(self, "pool", nc.gpsimd)
        self.SP = Eng(self, "sp", nc.sync)
        self.out_sems = []
        self.new_out_sem()
        self.dbg_outs = {}
        self.in_names = set()
        try:
            self.build()
        except _Stop:
            pass
        self.SP.wait([(d.sem, d.cnt) for d in self.out_sems if d.cnt])

    def phase_end(self, name):
        if self.upto == name:
            raise _Stop()

    def new_out_sem(self):
        tok = None
        if self.out_sems:
            tok = (self.out_sem.sem, self.out_sem.cnt) if self.out_sem.cnt else None
        self.out_sem = DSem(self, f"out{len(self.out_sems)}")
        self.out_sems.append(self.out_sem)
        return tok

    def newsem(self, name):
        self.nsem += 1
        return self.nc.semaphore(f"{name}_{self.nsem}").__enter__()

    def din(self, name, shape, dt=F32):
        self.in_names.add(name)
        return self.nc.dram_tensor(name, list(shape), dt, kind="ExternalInput").ap()

    def dout(self, name, shape, dt=F32):
        return self.nc.dram_tensor(name, list(shape), dt, kind="ExternalOutput").ap()

    def op(self, eng, deps, fn, *a, **kw):
        eng.wait(deps)
        return eng.sig(fn(*a, **kw))

    def V(self, deps, name, *a, **kw):
        return self.op(self.DVE, deps, getattr(self.nc.vector, name), *a, **kw)

    def G(self, deps, name, *a, **kw):
        return self.op(self.POOL, deps, getattr(self.nc.gpsimd, name), *a, **kw)

    def A(self, deps, name, *a, **kw):
        return self.op(self.ACT, deps, getattr(self.nc.scalar, name), *a, **kw)

    def store(self, deps, out, in_, dsem=None):
        self.SP.wait(deps)
        return (dsem or self.out_sem).add(self.nc.sync.dma_start(out=out, in_=in_))

    def load(self, q, dsem, deps, out, in_):
        q.wait(deps)
        return dsem.add(q.h.dma_start(out=out, in_=in_))

    def dump(self, name, ap, deps):
        if name not in self.debug:
            return
        d = self.dout("dbg_" + name, list(ap.shape), ap.dtype)
        self.dbg_outs[name] = d
        self.store(deps, d, ap)

    def bank(self, b, n=1):
        return self.PS[:, 512 * b:512 * (b + n)]

    def wslab(self, src3, nchunk, ncol):
        s = self.wn % 4
        self.wn += 1
        self.POOL.wait(self.wfree[s])
        self.wfree[s] = []
        v = self.wslot[s][:, 0:nchunk * ncol].rearrange("p (a b) -> p a b", a=nchunk)
        tok = self.wsem[s].add(self.nc.gpsimd.dma_start(out=v, in_=src3))
        return v, tok, s

    def wslab2(self, src3, nchunk, ncol):
        s = self.wn % 4
        assert s % 2 == 0
        self.wn += 2
        self.POOL.wait(self.wfree[s], self.wfree[s + 1])
        self.wfree[s] = []
        self.wfree[s + 1] = []
        v = self.RW.t[:, 2048 * s:2048 * (s + 2)].bitcast(BF16)[:, 0:nchunk * ncol].rearrange("p (a b) -> p a b", a=nchunk)
        tok = self.wsem[s].add(self.nc.gpsimd.dma_start(out=v, in_=src3))
        return v, tok, s

    def wrelease(self, s, tok):
        self.wfree[s].append(tok)

    def build(self):
        nc = self.nc
        P = self
        xp = P.din("xp", [NE + NH + NM, D])
        xs = P.din("xs", [NS, D])
        ck = P.din("ck", [NSEQ, 128, 256])
        cv = P.din("cv", [NSEQ, 128, 256])
        sconv = P.din("sconv", [NSEQ, 30, 1024])
        spool = P.din("spool", [NSEQ, 15, D])
        vecs = P.din("vecs", [104, 128])
        convw = P.din("convw", [31, 1024])
        qkg = P.din("qkg", [2, 64])
        sinks = P.din("sinks", [16])
        w_in = P.din("w_in", [D, 3584])
        w_out = P.din("w_out", [D, D])
        pool_w = P.din("pool_w", [4, 512, 512])
        masks = P.din("masks", [128, NMASK, 128])
        cst = P.din("cst", [128, 128 + 128 + 64 + 1])

        y_p = P.dout("y_p", [NM, D])
        y_s = P.dout("y_s", [NS, D])
        nk_p = P.dout("nk_p", [128, 256])
        nv_p = P.dout("nv_p", [128, 256])
        nc_p = P.dout("nc_p", [30, 1024])
        np_p = P.dout("np_p", [15, D])
        nk_s = P.dout("nk_s", [NSEQ, 128, 256])
        nv_s = P.dout("nv_s", [NSEQ, 128, 256])
        nc_s = P.dout("nc_s", [NSEQ, 30, 1024])
        np_s = P.dout("np_s", [NSEQ, 15, D])

        RX = Region(P, "RX", TX * KC)
        RH = Region(P, "RH", TH * KC // 2)
        RA = Region(P, "RA", 10240)
        RW = Region(P, "RW", 8192)
        self.RW = RW
        RC = Region(P, "RC", 3072)
        self.PS = nc.alloc_psum_tensor("PS", [128, 4096], F32)
        PSb = lambda b, n=1: self.bank(b, n)
        self.wslot = [RW.t[:, 2048 * s:2048 * (s + 1)].bitcast(BF16) for s in range(4)]
        self.wsem = [DSem(P, f"w{s}") for s in range(4)]
        self.wfree = [[] for _ in range(4)]
        self.wn = 0

        identf = RC.carve([128, 128], F32)
        onesf = RC.carve([128, 128], F32)
        invc = RC.carve([128, 4, 16], F32)
        hvalid = RC.carve([128, 1], F32)
        identb = RC.carve([128, 128], BF16)
        onesb = RC.carve([128, 128], BF16)
        maskb = RC.carve([128, NMASK, 128], BF16)
        vT = RC.carve([128, 104], F32)
        cwT = RC.carve([128, 8, 31], F32)
        qg_bc = RC.carve([128, 64], F32)
        kg_bc = RC.carve([128, 64], F32)
        esink = RC.carve([128, 16], F32)
        sm = RC.carve([128, 64], F32)
        cld = DSem(P, "cld")
        cstt = RX.carve([128, 321], F32)
        vecl = RX.carve([104, 128], F32)
        cwl = RX.carve([31, 1024], F32)
        t1 = P.load(P.SP, cld, [], cstt, cst[:, :])
        P.load(P.SP, cld, [], vecl, vecs[:, :])
        P.load(P.SP, cld, [], cwl, convw[:, :])
        P.load(P.SP, cld, [], qg_bc, qkg[0].partition_broadcast(128))
        P.load(P.SP, cld, [], kg_bc, qkg[1].partition_broadcast(128))
        tl = P.load(P.SP, cld, [], esink, sinks.partition_broadcast(128))
        mld = DSem(P, "mld")
        tm = P.load(P.POOL, mld, [], maskb, masks[:, :, :])
        c1 = P.V([tl], "tensor_copy", identf, cstt[:, 0:128])
        P.V([], "tensor_copy", identb, cstt[:, 0:128])
        P.V([], "tensor_scalar", onesf, cstt[:, 128:256], 1.0 / 1024, 0.0, ALU.mult, ALU.add)
        P.V([], "tensor_scalar", onesb, cstt[:, 128:256], 1.0 / 2048, 0.0, ALU.mult, ALU.add)
        P.V([], "tensor_copy", invc, cstt[:, 256:320].rearrange("p (a b) -> p a b", a=4))
        c2 = P.V([], "tensor_copy", hvalid, cstt[:, 320:321])
        a1 = P.A([tl], "activation", esink, esink, AF.Exp)
        a2 = P.A([], "mul", qg_bc, qg_bc, 0.125)
        P.PE.wait(c1, c2)
        i1 = nc.tensor.transpose(PSb(0)[:, 0:104], vecl, identf[0:104, 0:104])
        for c in range(8):
            i2 = nc.tensor.transpose(PSb(1)[:, c * 31:(c + 1) * 31], cwl[:, c * 128:(c + 1) * 128], identf[0:31, 0:31])
        tp = P.PE.sig(i2)
        P.V([tp], "tensor_copy", vT, PSb(0)[:, 0:104])
        cdone = P.V([], "tensor_copy", cwT, PSb(1)[:, 0:248].rearrange("p (a b) -> p a b", a=8))
        cdone = [cdone, a1, a2, tm, c2]
        V_NMIX, V_NFFN, V_PSC, V_CB, V_LG, V_LB = 0, 32, 64, 80, 88, 96
        RX.reset()

        xT = None
        hT = RH.carve([128, KC, TH], BF16)

        tiles = [(xp[128 * i:128 * i + 128, :], 128, 128 * i) for i in range(9)]
        tiles += [(xp[1152:1168, :], 16, 1152), (xs[:, :], 128, 1168)]

        xt = [RX.carve([128, D], F32) for _ in range(3)]
        hn = [RX.carve([128, D], BF16) for _ in range(3)]
        junk = RX.carve([128, D], BF16)
        xld = [DSem(P, f"x{i}") for i in range(3)]
        xfree = [[cdone], [cdone], [cdone]]
        hnfree = [[], [], []]
        bfree = [[cdone], [cdone], [cdone]]
        hT_tok = []
        c0 = {}

        def p0_load(n):
            src, nr, col = tiles[n]
            s = n % 3
            c0[n] = {"tld": P.load(P.SP, xld[s], xfree[s], xt[s][:nr], src)}

        def p0_sq(n):
            src, nr, col = tiles[n]
            s = n % 3
            c0[n]["tA"] = P.A([c0[n]["tld"]], "activation", junk[:nr], xt[s][:nr], AF.Square, accum_out=sm[:nr, 2 * s:2 * s + 1])

        def p0_v1(n):
            src, nr, col = tiles[n]
            s = n % 3
            c0[n]["tB"] = P.V([c0[n]["tA"]], "tensor_scalar", sm[:nr, 2 * s + 1:2 * s + 2], sm[:nr, 2 * s:2 * s + 1], 1.0 / D, RMS_EPS, ALU.mult, ALU.add)

        def p0_sqrt(n):
            src, nr, col = tiles[n]
            s = n % 3
            rs = sm[:nr, 2 * s + 1:2 * s + 2]
            c0[n]["tC"] = P.A([c0[n]["tB"]], "sqrt", rs, rs)

        def p0_v2(n):
            src, nr, col = tiles[n]
            s = n % 3
            rs = sm[:nr, 2 * s + 1:2 * s + 2]
            c0[n]["tD"] = P.V([c0[n]["tC"]], "reciprocal", rs, rs)

        def p0_mul(n):
            src, nr, col = tiles[n]
            s = n % 3
            tE = P.A([c0[n]["tD"], hnfree[s]], "mul", hn[s][:nr], xt[s][:nr], sm[:nr, 2 * s + 1:2 * s + 2])
            xfree[s] = [tE]
            c0[n]["tE"] = tE

        def p0_pe(n):
            src, nr, col = tiles[n]
            s = n % 3
            P.PE.wait(c0[n]["tE"], bfree[s], cdone)
            pb = PSb(2 * s, 2).bitcast(BF16).rearrange("p (a b) -> p a b", a=KC)
            for k in range(KC):
                ins = nc.tensor.transpose(pb[:, k, 0:nr], hn[s][:nr, k * 128:(k + 1) * 128], identb[:nr, :nr])
            tP = P.PE.sig(ins)
            hnfree[s] = [tP]
            c0[n]["tP"] = tP

        def p0_ev(n):
            src, nr, col = tiles[n]
            s = n % 3
            pb = PSb(2 * s, 2).bitcast(BF16).rearrange("p (a b) -> p a b", a=KC)
            tH = P.V([c0[n]["tP"]], "tensor_tensor", hT[:, :, col:col + nr], pb[:, :, 0:nr],
                     vT[:, V_NMIX:V_NMIX + 16].unsqueeze(2).to_broadcast([128, KC, nr]), ALU.mult)
            bfree[s] = [tH]
            hT_tok.append(tH)

        NT = len(tiles)
        ok = lambda n: 0 <= n < NT
        p0_load(0)
        p0_load(1)
        for i in range(NT + 2):
            if ok(i - 1):
                p0_sqrt(i - 1)
                p0_v2(i - 1)
                p0_mul(i - 1)
                p0_pe(i - 1)
            if ok(i - 2):
                p0_ev(i - 2)
            if ok(i + 2):
                p0_load(i + 2)
            if ok(i):
                p0_sq(i)
                p0_v1(i)
        P.dump("hT0", hT, hT_tok)
        P.phase_end("P0")
        RX.reset()

        w_in3 = w_in.rearrange("(c p) n -> p c n", p=128)
        conv_out = RX.carve([128, 8, TX], F32)
        glp = [RX.carve([128, 1072], F32) for _ in range(2)]
        gls = [RX.carve([128, NSEQ, 38], F32) for _ in range(2)]
        sg = RX.carve([128, 1200], F32)
        gsn = RX.carve([128, 128], F32)
        stc = [[RX.carve([120, 128], F32) for _ in range(4)] for _ in range(2)]
        cv2 = [RX.carve([128, TX], F32)] * 2
        NDG = 16
        dgr = [RX.carve([128, 128], F32) for _ in range(NDG)]
        dgfree = [[] for _ in range(NDG)]
        dgn = [0]
        b3free = []
        ncp = RA.carve([32, 1024], F32)
        ncs = RA.carve([128, 1024], F32)
        sld = [DSem(P, f"sld{i}") for i in range(2)]
        stfree = [[], []]
        cv2free = [[], []]
        TD = CONV_DVE_TAPS
        pieces = [(96, 512), (608, 512), (1120, 176)]
        slabs = {}
        glfree = [[], []]
        evfree = []
        b6free = []
        b7free = []
        nco_tok = []
        conv_tok = []
        def u_slabs(jj):
            return {'a': P.wslab(w_in3[:, :, 1536 + 256 * jj:1536 + 256 * jj + 256], KC, 256),
                    'g': P.wslab(w_in3[:, :, 2560 + 256 * jj:2560 + 256 * jj + 256], KC, 256)}
        nxt = u_slabs(0)
        for j in range(8):
            jj, dd = divmod(j, 2)
            if dd == 0:
                slabs = nxt
                if jj + 1 < 4:
                    nxt = u_slabs(jj + 1)
            s = j % 2
            for r in range(4):
                tst = P.load(P.SP, sld[s], stfree[s] if r == 0 else [], stc[s][r],
                             sconv[4 * r:4 * r + 4, :, j * 128:(j + 1) * 128].rearrange("n i c -> (n i) c"))
            wj = cwT[:, j, :]
            dg_tok = []

            def build_dg(n_):
                for _ in range(n_):
                    t_ = len(dg_tok)
                    if t_ >= 31:
                        return
                    r_ = dgn[0] % NDG
                    dgn[0] += 1
                    dg_tok.append((r_, P.A([dgfree[r_], cdone], "activation", dgr[r_], identf, AF.Copy, scale=wj[:, t_:t_ + 1])))
            build_dg(NDG)
            P.PE.wait(slabs['a'][1], slabs['g'][1], evfree, hT_tok)
            for k in range(KC):
                for pi, (c0, n) in enumerate(pieces):
                    ins = nc.tensor.matmul(PSb(pi)[:, 0:n], lhsT=slabs['g'][0][:, k, dd * 128:(dd + 1) * 128],
                                           rhs=hT[:, k, c0:c0 + n], start=(k == 0), stop=(k == KC - 1))
            tmmG = P.PE.sig(ins)
            tSs = []
            off = 0
            for pi, (c0, n) in enumerate(pieces):
                tSs.append(P.A([tmmG], "activation", sg[:, off:off + n], PSb(pi)[:, 0:n], AF.Sigmoid))
                off += n
            P.PE.wait(tSs)
            for k in range(KC):
                for pi, (c0, n) in enumerate(pieces):
                    ins = nc.tensor.matmul(PSb(pi)[:, 0:n], lhsT=slabs['a'][0][:, k, dd * 128:(dd + 1) * 128],
                                           rhs=hT[:, k, c0:c0 + n], start=(k == 0), stop=(k == KC - 1))
            tmm = P.PE.sig(ins)
            if dd == 1:
                P.wrelease(slabs['a'][2], tmm)
                P.wrelease(slabs['g'][2], tmm)
            P.PE.wait(tst, b6free)
            for r in range(4):
                ins = nc.tensor.transpose(PSb(6)[:, r * 120:(r + 1) * 120], stc[s][r], identf[0:120, 0:120])
            tst_t = P.PE.sig(ins)
            stfree[s] = [tst_t]
            tcp = P.A([tst_t, glfree[s]], "copy", gls[s][:, :, 0:30],
                      PSb(6)[:, 0:480].rearrange("p (a b) -> p a b", a=NSEQ))
            b6free = [tcp]
            tg0 = P.V([tmm, tSs[0], glfree[s]], "tensor_tensor", glp[s][:, 0:512], PSb(0)[:, 0:512], sg[:, 0:512], ALU.mult)
            tg1 = P.V([tSs[1]], "tensor_tensor", glp[s][:, 512:1024], PSb(1)[:, 0:512], sg[:, 512:1024], ALU.mult)
            tg2 = P.V([tSs[2]], "tensor_tensor", glp[s][:, 1024:1072], PSb(2)[:, 0:48], sg[:, 1024:1072], ALU.mult)
            tg3 = P.V([], "tensor_tensor", gsn, PSb(2)[:, 48:176], sg[:, 1072:1200], ALU.mult)
            evfree = [tg3]
            tg4 = P.A([tg3, tcp], "copy", gls[s][:, :, 30:38], gsn.rearrange("p (a b) -> p a b", a=NSEQ))
            P.PE.wait(tg2, tg3, b7free)
            nc.tensor.transpose(PSb(7)[0:32, 0:128], glp[s][:, 1040:1072], identf)
            ins = nc.tensor.transpose(PSb(7)[:, 128:256], gsn, identf)
            tno = P.PE.sig(ins)
            P.A([tno], "copy", ncp[:, j * 128:(j + 1) * 128], PSb(7)[0:32, 0:128])
            tnc = P.A([], "copy", ncs[:, j * 128:(j + 1) * 128], PSb(7)[:, 128:256])
            b7free = [tnc]
            nco_tok = [tnc]
            bj = vT[:, V_CB + j:V_CB + j + 1]
            P.PE.wait(tg4, b3free)
            for t in range(31):
                r, tk = dg_tok[t]
                P.PE.wait(tk)
                ins = nc.tensor.matmul(PSb(3)[:, 0:128], lhsT=dgr[r], rhs=gls[s][:, :, t:t + 8], start=(t == 0), stop=(t == 30))
                if t % 4 == 3 or t == 30:
                    tkd = P.PE.sig(ins)
                    for t2 in range(t - (t % 4), t + 1):
                        dgfree[dg_tok[t2][0]] = [tkd]
                    build_dg(4)
            tsv = P.A([tkd], "activation", conv_out[:, j, NH + NM:TX], PSb(3)[:, 0:128], AF.Identity, bias=bj)
            b3free = [tsv]
            yp = conv_out[:, j, 0:NH + NM]
            tpv = P.V([tg0, tg1, tg2], "tensor_scalar", yp, glp[s][:, 2:2 + 1040], wj[:, 0:1], bj, ALU.mult, ALU.add)
            if TD == 31:
                y2p = cv2[s][:, 0:NH + NM]
                tp2 = P.V([tg0, tg1, tg2, cv2free[s]], "tensor_scalar", y2p, glp[s][:, 3:3 + 1040], wj[:, 1:2], 0.0, ALU.mult, ALU.add)
                for t in range(2, 31):
                    if t % 2 == 0:
                        tpv = P.V([tpv], "scalar_tensor_tensor", yp, glp[s][:, 2 + t:2 + t + 1040], wj[:, t:t + 1], yp, ALU.mult, ALU.add)
                    else:
                        tp2 = P.V([tp2], "scalar_tensor_tensor", y2p, glp[s][:, 2 + t:2 + t + 1040], wj[:, t:t + 1], y2p, ALU.mult, ALU.add)
                tpv = P.V([tpv, tp2], "tensor_tensor", yp, yp, y2p, ALU.add)
                cv2free[s] = [tpv]
                cv2free[1 - s] = [tpv]
            for t in range(1, TD if TD < 31 else 0):
                tpv = P.V([tpv], "scalar_tensor_tensor", yp, glp[s][:, 2 + t:2 + t + 1040], wj[:, t:t + 1], yp, ALU.mult, ALU.add)
                tsv = P.V([tsv], "scalar_tensor_tensor", ysv, gls[s][:, :, t:t + 8], wj[:, t:t + 1], ysv, ALU.mult, ALU.add)
            if TD < 31:
                def prod(dst, t, extra):
                    ta_ = P.A([tg0, tg1, tg2, extra], "activation", dst[:, 0:NH + NM], glp[s][:, 2 + t:2 + t + 1040], AF.Copy, scale=wj[:, t:t + 1])
                    tb_ = P.A([tg4], "activation", dst[:, NH + NM:TX].rearrange("p (a b) -> p a b", a=NSEQ), gls[s][:, :, t:t + 8], AF.Copy,
                              scale=wj[:, t:t + 1])
                    return [ta_, tb_]
                tacc = prod(cv2[s], TD, cv2free[s])
                for t in range(TD + 1, 31):
                    k2 = t % 2
                    tpr = prod(ctmp[k2], t, ctfree[k2])
                    tacc = [P.G([tacc, tpr], "tensor_tensor", cv2[s], cv2[s], ctmp[k2], ALU.add)]
                    ctfree[k2] = tacc
                tpv = P.V([tpv, tsv, tacc], "tensor_tensor", conv_out[:, j, :], conv_out[:, j, :], cv2[s], ALU.add)
                tsv = tpv
                cv2free[s] = [tpv]
                cv2free[1 - s] = [tpv]
            glfree[s] = [tpv, tsv]
            conv_tok = [tpv, tsv]
        P.store(nco_tok, nc_p[:, :], ncp[2:32, :])
        P.store(nco_tok, nc_s[:, 22:30, :], ncs)
        P.store([], nc_s[:, 0:22, :], sconv[:, 8:30, :])
        nco_store = P.new_out_sem()
        P.dump("conv_out", conv_out, conv_tok)
        P.phase_end("P1")

        RA.reset()
        mixT = RA.carve([128, KC, TX], BF16)
        RX.reset(8 * TX)
        sqs = [RX.carve([128, TX], F32) for _ in range(2)]
        rsb = RX.carve([128, TX], F32)
        xpieces = [(0, 512), (512, 512), (1024, 144)]
        P.PE.wait(conv_tok, evfree)
        for j in range(8):
            for pi, (c0, n) in enumerate(xpieces):
                ins = nc.tensor.matmul(PSb(pi)[:, 0:n], lhsT=onesf, rhs=conv_out[:, j, c0:c0 + n], start=(j == 0), stop=(j == 7))
        tmean = P.PE.sig(ins)
        sqfree = [[], []]
        yc_tok = []
        for j in range(8):
            tks = []
            for pi, (c0, n) in enumerate(xpieces):
                tks.append(P.V([tmean], "tensor_tensor", conv_out[:, j, c0:c0 + n], conv_out[:, j, c0:c0 + n], PSb(pi)[:, 0:n], ALU.subtract))
            tq = P.A([tks, sqfree[j % 2]], "activation", sqs[j % 2], conv_out[:, j, :], AF.Square)
            P.PE.wait(tq)
            for pi, (c0, n) in enumerate(xpieces):
                ins = nc.tensor.matmul(PSb(3 + pi)[:, 0:n], lhsT=onesf, rhs=sqs[j % 2][:, c0:c0 + n], start=(j == 0), stop=(j == 7))
            sqfree[j % 2] = [P.PE.sig(ins)]
            yc_tok.append(tks)
        tvar = sqfree[1]
        tr = []
        for pi, (c0, n) in enumerate(xpieces):
            tr.append(P.V([tvar], "tensor_scalar", rsb[:, c0:c0 + n], PSb(3 + pi)[:, 0:n], 1.0, LN_EPS, ALU.mult, ALU.add))
        tr2 = P.A([tr], "sqrt", rsb, rsb)
        tr3 = P.V([tr2], "reciprocal", rsb, rsb)
        ln_tok = []
        for j in range(8):
            tz = P.V([tr3, yc_tok[j]], "tensor_tensor", conv_out[:, j, :], conv_out[:, j, :], rsb, ALU.mult)
            ln_tok.append(P.A([tz], "activation", mixT[:, 8 + j, :], conv_out[:, j, :], AF.Silu,
                              bias=vT[:, V_LB + j:V_LB + j + 1], scale=vT[:, V_LG + j:V_LG + j + 1]))
        ps_free = [tr, yc_tok[-1]]
        P.dump("cT", mixT[:, 8:16, :], ln_tok)
        P.phase_end("P2")
        RX.reset()

        qT = RX.carve([128, 8, TX], BF16)
        kT = RX.carve([128, 4, TH], BF16)
        vaug = RX.carve([128, 11, 4 * 80], BF16)
        sq3 = [RX.carve([128, 512], F32) for _ in range(4)]
        qn = [RX.carve([128, 512], F32) for _ in range(4)]
        qb = [RX.carve([128, 512], BF16) for _ in range(4)]
        kst = [RX.carve([128, 256], F32) for _ in range(3)]
        vst = [RX.carve([128, 256], F32) for _ in range(3)]
        p3free = [ln_tok, conv_tok, nco_store]
        tva = P.V([p3free], "memset", vaug, 1.0)
        bankfree = {b: [ps_free] for b in range(8)}
        qfree = [[tva], [tva], [tva], [tva]]
        qT_tok, kT_tok, v_tok = [], [], []
        groups = []
        for g in range(3):
            tl3 = list(range(1, 11)) if g < 2 else list(range(0, 11))
            for ti in tl3:
                groups.append({"g": g, "ti": ti, "first": ti == tl3[0], "last": ti == tl3[-1]})
        for n, c in enumerate(groups):
            c["n"] = n
            c["b"] = n % 4
            c["tb"] = 4 + n % 3
            c["s"] = n % 4
        slab3 = {}
        lastmm_box = [None]

        def s0(c):
            g, ti, b = c["g"], c["ti"], c["b"]
            src, nr, col = tiles[ti]
            if c["first"]:
                slab3[g] = P.wslab2(w_in3[:, :, 512 * g:512 * g + 512], KC, 512)
            sl2 = slab3[g]
            P.PE.wait(sl2[1], bankfree[b], hT_tok)
            for k in range(KC):
                ins = nc.tensor.matmul(PSb(b)[:nr, :], lhsT=hT[:, k, col:col + nr], rhs=sl2[0][:, k, :], start=(k == 0), stop=(k == KC - 1))
            c["tmm"] = P.PE.sig(ins)
            lastmm_box[0] = c["tmm"]
            if c["last"]:
                P.wrelease(sl2[2], c["tmm"])
                P.wrelease(sl2[2] + 1, c["tmm"])

        def s1(c):
            g, ti, b, s = c["g"], c["ti"], c["b"], c["s"]
            src, nr, col = tiles[ti]
            nhd = 8 if g < 2 else 4
            wd = nhd * 64
            c["t_a"] = P.A([c["tmm"], qfree[s]], "activation", sq3[s][:nr, 0:wd], PSb(b)[:nr, 0:wd], AF.Square)
            c["tvs"] = []
            if g == 2:
                oi = {8: 0, 9: 1, 10: 2}.get(ti, None)
                vsrc = PSb(b)[:nr, 256:512]
                t_v = P.A([], "copy", vaug[:nr, ti, :].rearrange("p (h e) -> p h e", e=80)[:, :, 0:64], vsrc.rearrange("p (h d) -> p h d", d=64))
                c["tvs"] = [t_v]
                v_tok.append(t_v)
                if oi is not None:
                    t_v2 = P.A([], "copy", vst[oi][:nr], vsrc)
                    c["tvs"].append(t_v2)
                    if ti == 8:
                        P.store([t_v2], nv_p[0:112, :], vst[0][16:128, :])
                    elif ti == 9:
                        P.store([t_v2], nv_p[112:128, :], vst[1][0:16, :])
                    else:
                        P.store([t_v2], nv_s[:, 120:128, :], vst[2])

        def s2(c):
            g, ti, s = c["g"], c["ti"], c["s"]
            src, nr, col = tiles[ti]
            nhd = 8 if g < 2 else 4
            wd = nhd * 64
            ssq = sm[:nr, 8 + 8 * s:8 + 8 * s + nhd]
            t_b = P.V([c["t_a"]], "tensor_reduce", ssq, sq3[s][:nr, 0:wd].rearrange("p (h d) -> p h d", d=64), AX.X, ALU.add)
            c["t_c"] = P.V([t_b], "tensor_scalar", ssq, ssq, 1.0 / 64, RMS_EPS, ALU.mult, ALU.add)

        def s3(c):
            g, ti, s = c["g"], c["ti"], c["s"]
            src, nr, col = tiles[ti]
            nhd = 8 if g < 2 else 4
            ssq = sm[:nr, 8 + 8 * s:8 + 8 * s + nhd]
            c["t_d"] = P.A([c["t_c"]], "sqrt", ssq, ssq)

        def s4(c):
            g, ti, b, s = c["g"], c["ti"], c["b"], c["s"]
            src, nr, col = tiles[ti]
            nhd = 8 if g < 2 else 4
            wd = nhd * 64
            ssq = sm[:nr, 8 + 8 * s:8 + 8 * s + nhd]
            t_e = P.V([c["t_d"]], "reciprocal", ssq, ssq)
            qn3 = qn[s][:nr, 0:wd].rearrange("p (h d) -> p h d", d=64)
            t_f = P.V([t_e], "tensor_tensor", qn3, PSb(b)[:nr, 0:wd].rearrange("p (h d) -> p h d", d=64),
                      ssq.unsqueeze(2).to_broadcast([nr, nhd, 64]), ALU.mult)
            bankfree[b] = [t_f, c["tvs"]]
            if g < 2:
                t_g = P.V([t_f], "tensor_tensor", qb[s][:nr, :].rearrange("p (h d) -> p h d", d=64), qn3,
                          qg_bc[:nr].unsqueeze(1).to_broadcast([nr, 8, 64]), ALU.mult)
                c["rdy"] = [t_g]
            else:
                oi = {8: 0, 9: 1, 10: 2}.get(ti, None)
                kdst = kst[oi] if oi is not None else qn[s][:, 256:512]
                kd3 = kdst[:nr].rearrange("p (h d) -> p h d", d=64)
                t_g = P.V([t_f], "tensor_tensor", kd3, qn3, kg_bc[:nr].unsqueeze(1).to_broadcast([nr, 4, 64]), ALU.mult)
                kd = qb[s][:nr, :].rearrange("p (h t d) -> p h t d", h=4, t=2)
                t_h = P.V([t_g], "tensor_copy", kd[:, :, 0, :], kd3)
                t_i = P.V([t_g], "tensor_copy", kd[:, :, 1, :], kd3)
                c["rdy"] = [t_h, t_i]
                if ti == 8:
                    P.store([t_g], nk_p[0:112, :], kst[0][16:128, :])
                elif ti == 9:
                    P.store([t_g], nk_p[112:128, :], kst[1][0:16, :])
                elif ti == 10:
                    P.store([t_g], nk_s[:, 120:128, :], kst[2])

        def s5(c):
            g, ti, tb, s = c["g"], c["ti"], c["tb"], c["s"]
            src, nr, col = tiles[ti]
            P.PE.wait(c["rdy"], bankfree[tb])
            pb = PSb(tb).bitcast(BF16)[:, 0:512].rearrange("p (a b) -> p a b", a=4)
            for cc in range(4):
                ins = nc.tensor.transpose(pb[:, cc, 0:nr], qb[s][:nr, cc * 128:(cc + 1) * 128], identb[:nr, :nr])
            c["t_t"] = P.PE.sig(ins)

        def s6(c):
            g, ti, tb, s = c["g"], c["ti"], c["tb"], c["s"]
            src, nr, col = tiles[ti]
            pb = PSb(tb).bitcast(BF16)[:, 0:512].rearrange("p (a b) -> p a b", a=4)
            if g < 2:
                t_q = P.A([c["t_t"]], "copy", qT[:, 4 * g:4 * g + 4, col - NE:col - NE + nr], pb[:, :, 0:nr])
                qT_tok.append(t_q)
            else:
                t_q = P.A([c["t_t"]], "copy", kT[:, :, col:col + nr], pb[:, :, 0:nr])
                kT_tok.append(t_q)
            bankfree[tb] = [t_q]
            qfree[s] = [c["t_t"], c["rdy"]]

        NG = len(groups)
        gok = lambda n: 0 <= n < NG
        for i in range(NG + 3):
            if gok(i):
                s0(groups[i])
            if gok(i - 3):
                s5(groups[i - 3])
                s6(groups[i - 3])
            if gok(i - 2):
                s3(groups[i - 2])
                s4(groups[i - 2])
            if gok(i - 1):
                s1(groups[i - 1])
                s2(groups[i - 1])
        lastmm = lastmm_box[0]
        P.store([], nk_s[:, 0:120, :], ck[:, 8:128, :])
        P.store([], nv_s[:, 0:120, :], cv[:, 8:128, :])
        P.dump("qT", qT, qT_tok)
        P.dump("kT", kT, kT_tok)
        P.dump("vaug", vaug, v_tok)
        P.phase_end("P3")
        allfree = [bankfree[b] for b in range(8)]

        pT = [RX.carve([128, 4, 128], BF16) for _ in range(4)]
        att = [RX.carve([128, 1024], BF16) for _ in range(2)]
        den = [RX.carve([128, 16], F32) for _ in range(2)]
        RH.reset()
        ckl = [RH.carve([128, 256], F32) for _ in range(4)]
        cvl = [RH.carve([128, 256], F32) for _ in range(4)]
        ckd = [RH.carve([128, 4, 2, 64], BF16) for _ in range(4)]
        kcT = [RH.carve([128, 4, 128], BF16) for _ in range(4)]
        vca = [RH.carve([128, 4 * 80], BF16) for _ in range(4)]
        qbd = [RH.carve([128, 8, 2, 128], BF16) for _ in range(2)]
        cld2 = [DSem(P, f"ck{i}") for i in range(4)]
        tvc = P.V([lastmm], "memset", vca[0], 1.0)
        tvc = P.V([], "memset", vca[1], 1.0)
        tvc = P.V([], "memset", vca[2], 1.0)
        tvc = P.V([], "memset", vca[3], 1.0)
        tz0 = P.V([lastmm], "memset", qbd[0], 0.0)
        tz1 = P.V([], "memset", qbd[1], 0.0)
        qbd_free = [[tz0, tz1], [tz0, tz1]]
        qbd_tok = {}
        last_qk_of_tile = {}
        qtiles = []
        for ti in range(1, 10):
            src, nr, col = tiles[ti]
            prev = tiles[ti - 1]
            qtiles.append((ti, nr, col - NE, [("p", ti - 1, 128, 2 if ti == 1 else 0), ("p", ti, nr, 3 if ti == 1 else 1)]))
        qtiles.append((10, 128, tiles[10][2] - NE, [("c", n, 128, 4 + n) for n in range(NSEQ)] + [("p", 10, 128, 20)]))
        sbank = 0
        sfree = [allfree] * 4
        pfree = [[], [], [], []]
        ofree = [allfree]
        tbfree = [allfree]
        cfree = [[tvc], [tvc], [tvc], [tvc]]
        ncache = 0
        units = []
        for qi, (ti, nq, xc, keys) in enumerate(qtiles):
            for ki, kspec in enumerate(keys):
                for kh in range(4):
                    units.append((qi, ti, nq, xc, ki, len(keys), kspec, kh))
        qk_tok = {}
        cache_ready = {}
        cache_late = set()

        cache_dma = {}

        def prep_dma(n):
            if n >= NSEQ or n in cache_dma:
                return
            s = n % 4
            t1 = P.load(P.SP, cld2[s], cfree[s], ckl[s], ck[n])
            cache_dma[n] = P.load(P.SP, cld2[s], [], cvl[s], cv[n])

        def prep_cache(n):
            if n >= NSEQ or n in cache_ready:
                return
            prep_dma(n)
            s = n % 4
            t2 = cache_dma[n]
            a = P.V([t2], "tensor_copy", ckd[s][:, :, 0, :], ckl[s].rearrange("p (h d) -> p h d", d=64))
            b_ = P.V([], "tensor_copy", ckd[s][:, :, 1, :], ckl[s].rearrange("p (h d) -> p h d", d=64))
            c_ = P.V([], "tensor_copy", vca[s].rearrange("p (h e) -> p h e", e=80)[:, :, 0:64], cvl[s].rearrange("p (h d) -> p h d", d=64))
            cache_ready[n] = (a, b_, c_)

        def prep_cache_late(n):
            if n >= NSEQ or n in cache_late:
                return
            cache_late.add(n)
            prep_cache(n)
            s = n % 4
            a, b_, c_ = cache_ready[n]
            P.PE.wait(a, b_, tbfree[0])
            pb = PSb(7).bitcast(BF16)[:, 0:512].rearrange("p (a b) -> p a b", a=4)
            for kh2 in range(4):
                ins = nc.tensor.transpose(pb[:, kh2, :], ckd[s][:, kh2].rearrange("p t d -> p (t d)"), identb)
            tt = P.PE.sig(ins)
            tq = P.A([tt], "copy", kcT[s], pb)
            tbfree[0] = [tq]
            cache_ready[n] = (tq, c_)

        def build_qbd(qi_):
            if qi_ >= len(qtiles) or qi_ in qbd_tok:
                return
            ti_, nq_, xc_, _k = qtiles[qi_]
            s_ = qi_ % 2
            fr = [qbd_free[s_], qT_tok, last_qk_of_tile.get(qi_ - 2)]
            ta_ = P.A(fr, "copy", qbd[s_][0:64, :, 0, 0:nq_], qT[0:64, :, xc_:xc_ + nq_])
            tb_ = P.A(fr, "copy", qbd[s_][64:128, :, 1, 0:nq_], qT[64:128, :, xc_:xc_ + nq_])
            qbd_tok[qi_] = [ta_, tb_]

        def emit_qk(u):
            qi, ti, nq, xc, ki, nk, kspec, kh = units[u]
            b = u % 4
            kind, kidx, nkeys, mi = kspec
            if ki == 0 and kh == 0:
                build_qbd(qi)
                build_qbd(qi + 1)
            deps = [sfree[b], qbd_tok[qi]]
            if kind == "c":
                if kh == 0:
                    prep_cache_late(kidx)
                    prep_cache_late(kidx + 1)
                deps.append(cache_ready[kidx][0])
            else:
                deps.append(kT_tok)
            P.PE.wait(deps)
            sv4 = PSb(b)[:nkeys, :].rearrange("p (a b) -> p a b", a=4)
            qb_ = qbd[qi % 2]
            for g2 in range(2):
                ch = 2 * kh + g2
                if kind == "c":
                    lhsT = kcT[kidx % 4][:, kh, :]
                else:
                    kcol = tiles[kidx][2]
                    lhsT = kT[:, kh, kcol:kcol + nkeys]
                if nq == 128:
                    ins = nc.tensor.matmul(PSb(b)[:nkeys, 256 * g2:256 * g2 + 256], lhsT=lhsT, rhs=qb_[:, ch, :, :].rearrange("p a b -> p (a b)"),
                                           start=True, stop=True, skip_group_check=True)
                else:
                    for e in range(2):
                        ins = nc.tensor.matmul(sv4[:, 2 * g2 + e, 0:nq], lhsT=lhsT, rhs=qb_[:, ch, e, 0:nq],
                                               start=True, stop=True, skip_group_check=True)
            qk_tok[u] = P.PE.sig(ins)
            last_qk_of_tile[qi] = qk_tok[u]

        em_tok = {}

        def emit_em(u):
            qi, ti, nq, xc, ki, nk, kspec, kh = units[u]
            b = u % 4
            kind, kidx, nkeys, mi = kspec
            sv = PSb(b)[:nkeys, :].rearrange("p (a b) -> p a b", a=4)[:, :, 0:nq]
            pv = pT[b][:nkeys, :, 0:nq]
            te = P.A([qk_tok[u], pfree[b]], "activation", pv, sv, AF.Exp)
            sfree[b] = [te]
            em_tok[u] = P.V([te], "tensor_tensor", pv, pv, maskb[:nkeys, mi, 0:nq].unsqueeze(1).to_broadcast([nkeys, 4, nq]), ALU.mult)

        def emit_pv(u):
            qi, ti, nq, xc, ki, nk, kspec, kh = units[u]
            b = u % 4
            kind, kidx, nkeys, mi = kspec
            deps = [em_tok[u]]
            if ki == 0 and kh == 0:
                deps.append(ofree[0])
            if kind == "c":
                deps.append(cache_ready[kidx][1])
            else:
                deps.append(v_tok)
            P.PE.wait(deps)
            for g4 in range(4):
                h = 4 * kh + g4
                kvh = kh
                ob, osl = divmod(h, 6)
                if kind == "c":
                    rhs = vca[kidx % 4][:, kvh * 80:kvh * 80 + 66]
                else:
                    rhs = vaug[:nkeys, kidx, kvh * 80:kvh * 80 + 66]
                ins = nc.tensor.matmul(PSb(4 + ob)[:nq, osl * 72:osl * 72 + 66], lhsT=pT[b][:nkeys, g4, 0:nq], rhs=rhs,
                                       start=(ki == 0 and h in (0, 6, 12)), stop=(ki == nk - 1), skip_group_check=True)
            tpv = P.PE.sig(ins)
            pfree[b] = [tpv]
            if kind == "c" and kh == 3:
                cfree[kidx % 4] = [tpv]
            return tpv

        att_tok = []
        LOOK = 3
        for n0 in range(4):
            prep_dma(n0)
        for n0 in range(3):
            prep_cache(n0)
        for u in range(min(LOOK, len(units))):
            emit_qk(u)
        emit_em(0)
        for u in range(len(units)):
            if u + LOOK < len(units):
                emit_qk(u + LOOK)
            if u + 1 < len(units):
                emit_em(u + 1)
            tpv = emit_pv(u)
            qi, ti, nq, xc, ki, nk, kspec, kh = units[u]
            if self.upto == f"P4u{u}":
                raise _Stop()
            if kspec[0] == "c" and kh == 3:
                prep_dma(kspec[1] + 4)
                prep_cache(kspec[1] + 3)
            if ki == nk - 1 and kh == 3:
                s = qi % 2
                tn = []
                hb_ = [(0, 6), (6, 6), (12, 4)]
                ovs = [PSb(4 + ob)[:nq, 0:nh_ * 72].rearrange("p (h e) -> p h e", e=72) for ob, (h0, nh_) in enumerate(hb_)]
                tas = [P.V([tpv], "tensor_tensor", den[s][:nq, h0:h0 + nh_], ovs[ob][:, :, 64], esink[:nq, h0:h0 + nh_], ALU.add)
                       for ob, (h0, nh_) in enumerate(hb_)]
                tbs = [P.V([tas[ob]], "reciprocal", den[s][:nq, h0:h0 + nh_], den[s][:nq, h0:h0 + nh_]) for ob, (h0, nh_) in enumerate(hb_)]
                for ob, (h0, nh_) in enumerate(hb_):
                    tn.append(P.V([tbs[ob]], "tensor_tensor", att[s][:nq, h0 * 64:(h0 + nh_) * 64].rearrange("p (h d) -> p h d", d=64),
                                  ovs[ob][:, :, 0:64], den[s][:nq, h0:h0 + nh_].unsqueeze(2).to_broadcast([nq, nh_, 64]), ALU.mult))
                ofree[0] = [tn]
                P.PE.wait(tn, tbfree[0])
                pb = PSb(7).bitcast(BF16).rearrange("p (a b) -> p a b", a=8)
                for c in range(8):
                    ins = nc.tensor.transpose(pb[:, c, 0:nq], att[s][:nq, c * 128:(c + 1) * 128], identb[:nq, :nq])
                tt = P.PE.sig(ins)
                tq = P.A([tt], "copy", mixT[:, 0:8, xc:xc + nq], pb[:, :, 0:nq])
                tbfree[0] = [tq]
                att_tok.append(tq)
                if self.upto == f"P4q{qi}":
                    P.dump("attT", mixT[:, 0:8, :], att_tok)
                    raise _Stop()
        P.dump("attT", mixT[:, 0:8, :], att_tok)
        P.phase_end("P4")
        p4done = [att_tok, ofree[0], tbfree[0], [sfree[b] for b in range(4)], [pfree[b] for b in range(4)], P.new_out_sem()]

        RX.reset()
        xT = RX.carve([128, KC, TX], F32)
        RH.reset()
        xt5 = [RH.carve([128, D], F32) for _ in range(2)]
        xfree = [[p4done, hT_tok], [p4done, hT_tok]]
        bk5 = [[p4done]] * 4
        xT_tok = []
        nb5 = 0
        for ti in range(1, 11):
            src, nr, col = tiles[ti]
            xc = col - NE
            s = ti % 2
            tld = P.load(P.SP, xld[s], xfree[s], xt5[s][:nr], src)
            for g in range(4):
                b = nb5 % 4
                nb5 += 1
                P.PE.wait(tld, bk5[b])
                pv = PSb(b).rearrange("p (a b) -> p a b", a=4)
                for c in range(4):
                    k = 4 * g + c
                    ins = nc.tensor.transpose(pv[:, c, 0:nr], xt5[s][:nr, k * 128:(k + 1) * 128], identf[:nr, :nr])
                tt = P.PE.sig(ins)
                if g % 2 == 0:
                    tc5 = P.A([tt, p4done], "copy", xT[:, 4 * g:4 * g + 4, xc:xc + nr], pv[:, :, 0:nr])
                else:
                    tc5 = P.V([tt, p4done], "tensor_copy", xT[:, 4 * g:4 * g + 4, xc:xc + nr], pv[:, :, 0:nr])
                bk5[b] = [tc5]
                xT_tok.append(tc5)
            xfree[s] = [tt]
        P.dump("xT0", xT, xT_tok)
        P.phase_end("P5")

        self.yset = 0
        self.yfree = [[xT_tok, bk5], [xT_tok, bk5]]

        def proj_add(slab_iter, nk, act, act_tok, cpieces, xoff, scale_col=None, on_done=None):
            last = None
            pending = None
            for (wv, wtok, slot, d0, nd) in slab_iter:
                for dd in range(nd):
                    d = d0 + dd
                    ys = self.yset
                    self.yset ^= 1
                    P.PE.wait(wtok, act_tok, self.yfree[ys])
                    for k in range(nk):
                        for pi, (c0, n) in enumerate(cpieces):
                            ins = nc.tensor.matmul(PSb(3 * ys + pi)[:, 0:n], lhsT=wv[:, k, dd * 128:(dd + 1) * 128],
                                                   rhs=act[:, k, c0:c0 + n], start=(k == 0), stop=(k == nk - 1))
                    tmm = P.PE.sig(ins)
                    tks = []
                    for pi, (c0, n) in enumerate(cpieces):
                        xv = xT[:, d, xoff + c0 - cpieces[0][0]:xoff + c0 - cpieces[0][0] + n]
                        if scale_col is None:
                            tks.append(P.V([tmm, xT_tok], "tensor_tensor", xv, xv, PSb(3 * ys + pi)[:, 0:n], ALU.add))
                        else:
                            tks.append(P.V([tmm, xT_tok], "scalar_tensor_tensor", xv, PSb(3 * ys + pi)[:, 0:n],
                                           vT[:, scale_col + d:scale_col + d + 1], xv, ALU.mult, ALU.add))
                    self.yfree[ys] = [tks]
                    last = tks
                    if on_done is not None:
                        if pending is not None:
                            on_done(*pending)
                        pending = (d, tks)
                P.wrelease(slot, tmm)
            if on_done is not None and pending is not None:
                on_done(*pending)
            return last

        w_out3 = w_out.rearrange("(c p) n -> p c n", p=128)

        def wout_slabs():
            for s8 in range(8):
                wv, wtok, slot = P.wslab(w_out3[:, :, 256 * s8:256 * s8 + 256], KC, 256)
                yield wv, wtok, slot, 2 * s8, 2
        res_tok = proj_add(wout_slabs(), KC, mixT, [att_tok, ln_tok], xpieces, 0)
        res_tok = [res_tok, self.yfree]
        P.dump("xT1", xT, res_tok)
        P.phase_end("P6")

        def rms_stats(x_tok, sqbufs, rs):
            P.PE.wait(self.yfree)
            sfr = [[], []]
            for k in range(KC):
                if k % 2 == 0:
                    tq = P.A([x_tok, sfr[k % 2]], "activation", sqbufs[k % 2], xT[:, k, :], AF.Square)
                else:
                    tq = P.V([x_tok, sfr[k % 2]], "tensor_tensor", sqbufs[k % 2], xT[:, k, :], xT[:, k, :], ALU.mult)
                P.PE.wait(tq)
                for pi, (c0, n) in enumerate(xpieces):
                    ins = nc.tensor.matmul(PSb(pi)[:, 0:n], lhsT=onesb, rhs=sqbufs[k % 2][:, c0:c0 + n], start=(k == 0), stop=(k == KC - 1))
                sfr[k % 2] = [P.PE.sig(ins)]
            tr = []
            for pi, (c0, n) in enumerate(xpieces):
                tr.append(P.V([sfr[1]], "tensor_scalar", rs[:, c0:c0 + n], PSb(pi)[:, 0:n], 1.0, RMS_EPS, ALU.mult, ALU.add))
            tr2 = P.A([tr], "sqrt", rs, rs)
            tr3 = P.V([tr2], "reciprocal", rs, rs)
            self.yfree[0] = [self.yfree[0], tr]
            return tr3

        def ffn(layer, x_tok):
            RH.reset()
            hT2 = RH.carve([128, KC, TH], BF16)
            RA.reset()
            aT = RA.carve([128, 12, TX], BF16)
            sgb = [RA.carve([128, TX], F32) for _ in range(2)]
            sqb = [sgb[0].bitcast(BF16)[:, 0:TX], sgb[1].bitcast(BF16)[:, 0:TX]]
            rs = self.rsF
            t_rs = rms_stats(x_tok, sqb, rs)
            h_tok = []
            for k in range(KC):
                h_tok.append(P.V([t_rs, x_tok], "scalar_tensor_tensor", hT2[:, k, NE:TH], xT[:, k, :],
                                 vT[:, V_NFFN + 16 * layer + k:V_NFFN + 16 * layer + k + 1], rs, ALU.mult, ALU.mult))
            wg3 = w_gate[layer].rearrange("(c p) n -> p c n", p=128)
            wu3 = w_up[layer].rearrange("(c p) n -> p c n", p=128)
            wd3 = w_down[layer].rearrange("(c p) n -> p c n", p=128)
            x0 = 0 if layer == 0 else NH
            cp = [(NE + x0, 512), (NE + x0 + 512, 512), (NE + x0 + 1024, TX - x0 - 1024)]
            dpieces = [(0, 512), (512, 512), (1024, TX - x0 - 1024)]
            sgfree = [[], []]
            a_free = [h_tok]
            last_res = x_tok
            gfree = [self.yfree]
            ufree = [self.yfree]
            for (f0, nf) in QUARTERS:
                a_tok = []
                for sp in range(nf // 2):
                    fs = f0 + 2 * sp
                    gs_ = P.wslab(wg3[:, :, 128 * fs:128 * fs + 256], KC, 256)
                    us_ = P.wslab(wu3[:, :, 128 * fs:128 * fs + 256], KC, 256)
                    for dd in range(2):
                        fi = 2 * sp + dd
                        s = fi % 2
                        P.PE.wait(gs_[1], gfree)
                        for k in range(KC):
                            P.PE.wait(h_tok[k])
                            for pi, (c0, n) in enumerate(cp):
                                ins = nc.tensor.matmul(PSb(pi)[:, 0:n], lhsT=gs_[0][:, k, dd * 128:(dd + 1) * 128], rhs=hT2[:, k, c0:c0 + n],
                                                       start=(k == 0), stop=(k == KC - 1))
                        tg = P.PE.sig(ins)
                        P.PE.wait(us_[1], ufree)
                        for k in range(KC):
                            for pi, (c0, n) in enumerate(cp):
                                ins = nc.tensor.matmul(PSb(3 + pi)[:, 0:n], lhsT=us_[0][:, k, dd * 128:(dd + 1) * 128], rhs=hT2[:, k, c0:c0 + n],
                                                       start=(k == 0), stop=(k == KC - 1))
                        tu = P.PE.sig(ins)
                        tsl = []
                        off = 0
                        for pi, (c0, n) in enumerate(cp):
                            tsl.append(P.A([tg, sgfree[s]], "activation", sgb[s][:, off:off + n], PSb(pi)[:, 0:n], AF.Silu))
                            off += n
                        tml = []
                        off = 0
                        for pi, (c0, n) in enumerate(cp):
                            tml.append(P.V([tu, tsl[pi], a_free], "tensor_tensor", aT[:, fi, off:off + n], sgb[s][:, off:off + n], PSb(3 + pi)[:, 0:n], ALU.mult))
                            off += n
                        sgfree[s] = [tml]
                        gfree = [tsl]
                        ufree = [tml]
                        a_tok.append(tml)
                    P.wrelease(gs_[2], tg)
                    P.wrelease(us_[2], tu)
                self.yfree = [[gfree], [ufree]]

                def wd_slabs():
                    for s8 in range(8):
                        wv, wtok, slot = P.wslab(wd3[:, f0:f0 + nf, 256 * s8:256 * s8 + 256], nf, 256)
                        yield wv, wtok, slot, 2 * s8, 2
                last_res = proj_add(wd_slabs(), nf, aT, a_tok, dpieces, x0,
                                    on_done=(self.out_cb if (layer == 1 and f0 == QUARTERS[-1][0]) else None))
                a_free = [self.yfree]
                gfree = [self.yfree[0]]
                ufree = [self.yfree[1]]
            return [last_res, self.yfree]

        self.RS = Region(P, "RS", TX)
        self.rsF = self.RS.carve([128, TX], F32)
        w_gate = P.din("w_gate", [2, D, DFF])
        w_up = P.din("w_up", [2, D, DFF])
        w_down = P.din("w_down", [2, DFF, D])
        x1_tok = ffn(0, res_tok)
        P.dump("xT2", xT, x1_tok)
        P.phase_end("P8")

        RA.reset()
        sqb = [RA.carve([128, TX], BF16) for _ in range(2)]
        hbp = [RA.carve([128, 1040], F32) for _ in range(2)]
        hbs = [RA.carve([128, NSEQ, 23], F32) for _ in range(2)]
        Sp = [RA.carve([128, 1040], F32) for _ in range(2)]
        Ss = [RA.carve([128, NSEQ, 23], F32) for _ in range(2)]
        spt = [[RA.carve([120, 128], F32) for _ in range(2)] for _ in range(2)]
        hsc = [RA.carve([128, 128], F32) for _ in range(2)]
        npo = [RA.carve([16, 128], F32) for _ in range(2)]
        nso = [RA.carve([128, 128], F32) for _ in range(2)]
        fixb = RA.carve([128, 16], F32)
        RH.reset()
        dpT = RH.carve([128, KC, NM + NS], BF16)
        rs = self.rsF
        pld = [DSem(P, f"pl{i}") for i in range(2)]
        pst = [DSem(P, f"pst{i}") for i in range(2)]
        self.out_sems += pst
        t_rs = rms_stats(x1_tok, sqb, rs)
        hbfree = [[], []]
        sptfree = [[x1_tok], [x1_tok]]
        stgfree = [[], []]
        b6free = [self.yfree]
        b7free = [self.yfree]
        dp_tok = []
        self.yfree = [[self.yfree, b6free, b7free], [self.yfree, b6free, b7free]]
        ppieces = [(0, 512), (512, 512), (1024, 128)]
        pslab = [P.wslab(pool_w[g].rearrange("(c p) n -> p c n", p=128), 4, 512) for g in range(4)]
        pool_last = [None]
        pool_sched = []
        pmm = {}

        def pool_mm(g, e):
            wv, wtok, slot = pslab[g]
            d = 4 * g + e
            ys = self.yset
            self.yset ^= 1
            P.PE.wait(wtok, dp_tok[4 * g:4 * g + 4], self.yfree[ys])
            for cc in range(4):
                for pi, (c0, n) in enumerate(ppieces):
                    ins = nc.tensor.matmul(PSb(3 * ys + pi)[:, 0:n], lhsT=wv[:, cc, e * 128:(e + 1) * 128],
                                           rhs=dpT[:, 4 * g + cc, c0:c0 + n], start=(cc == 0), stop=(cc == 3))
            pmm[d] = (P.PE.sig(ins), ys)
            if e == 3:
                P.wrelease(slot, pmm[d][0])

        def pool_ev(g, e):
            d = 4 * g + e
            tmm, ys = pmm[d]
            tks = []
            for pi, (c0, n) in enumerate(ppieces):
                xv = xT[:, d, NH + c0:NH + c0 + n]
                tks.append(P.V([tmm, x1_tok], "scalar_tensor_tensor", xv, PSb(3 * ys + pi)[:, 0:n],
                               vT[:, V_PSC + d:V_PSC + d + 1], xv, ALU.mult, ALU.add))
            self.yfree[ys] = [tks]
            pool_last[0] = tks

        def pool_steps(g):
            return [lambda: (pool_mm(g, 0), pool_mm(g, 1)),
                    lambda: (pool_ev(g, 0), pool_ev(g, 1), pool_mm(g, 2), pool_mm(g, 3)),
                    lambda: (pool_ev(g, 2), pool_ev(g, 3))]

        def pool_loads(k):
            s = k % 2
            for r in range(2):
                tk = P.load(P.SP, pld[s], sptfree[s] if r == 0 else [], spt[s][r],
                            spool[8 * r:8 * r + 8, :, k * 128:(k + 1) * 128].rearrange("n i c -> (n i) c"))
            return tk

        for k in range(KC):
            s = k % 2
            g = k // 4
            w = (2, 4, 8, 16)[g]
            gcol = vT[:, V_NMIX + 16 + k:V_NMIX + 16 + k + 1]
            if k == 0:
                tsp_next = pool_loads(0)
            tsp = tsp_next
            if k + 1 < KC:
                tsp_next = pool_loads(k + 1)
            t0 = P.V([t_rs, hbfree[s]], "scalar_tensor_tensor", hbp[s][:, 0:1039], xT[:, k, 1:1040], gcol, rs[:, 1:1040], ALU.mult, ALU.mult)
            t0b = P.V([t0], "tensor_scalar", hbp[s][:, 0:15], hbp[s][:, 0:15], hvalid[:, 0:1], 0.0, ALU.mult, ALU.add)
            t1_ = P.V([], "scalar_tensor_tensor", hbs[s][:, :, 15:23], xT[:, k, 1040:1168].rearrange("p (a b) -> p a b", a=NSEQ), gcol,
                      rs[:, 1040:1168].rearrange("p (a b) -> p a b", a=NSEQ), ALU.mult, ALU.mult)
            tcs = P.A([t1_, stgfree[s]], "copy", hsc[s].rearrange("p (a b) -> p a b", a=NSEQ), hbs[s][:, :, 15:23])
            P.PE.wait(tsp, b6free)
            for r in range(2):
                ins = nc.tensor.transpose(PSb(6)[:, r * 120:(r + 1) * 120], spt[s][r], identf[0:120, 0:120])
            tt = P.PE.sig(ins)
            sptfree[s] = [tt]
            t2_ = P.A([tt, t1_], "copy", hbs[s][:, :, 0:15], PSb(6)[:, 0:240].rearrange("p (a b) -> p a b", a=NSEQ))
            b6free = [t2_]
            P.PE.wait(t0b, tcs, b7free)
            nc.tensor.transpose(PSb(7)[0:16, 0:128], hbp[s][:, 1023:1039], identf)
            ins = nc.tensor.transpose(PSb(7)[:, 128:256], hsc[s], identf)
            tn7 = P.PE.sig(ins)
            ta7 = P.A([tn7, stgfree[s]], "copy", npo[s], PSb(7)[0:16, 0:128])
            tnp = P.A([], "copy", nso[s], PSb(7)[:, 128:256])
            b7free = [tnp]
            st1 = P.store([ta7, tnp], np_p[:, k * 128:(k + 1) * 128], npo[s][1:16, :], pst[s])
            st2 = P.store([], np_s[:, 7:15, k * 128:(k + 1) * 128], nso[s], pst[s])
            stgfree[s] = [tn7, st2]
            curp, curs = hbp[s], hbs[s]
            tp_, ts_ = [t0b], [t2_, t1_]
            sh = 1
            bi = 0
            while sh < w:
                op_, os_ = Sp[bi % 2], Ss[bi % 2]
                tp_ = [P.V([tp_], "tensor_tensor", op_[:, sh:1039], curp[:, sh:1039], curp[:, 0:1039 - sh], ALU.add)]
                ts_ = [P.V([ts_], "tensor_tensor", os_[:, :, sh:23], curs[:, :, sh:23], curs[:, :, 0:23 - sh], ALU.add)]
                curp, curs = op_, os_
                sh *= 2
                bi += 1
            td = P.V([tp_], "scalar_tensor_tensor", dpT[:, k, 0:NM], curp[:, 15:1039], 1.0 / w, hbp[s][:, 15:1039], ALU.mult, ALU.subtract)
            tf1 = P.V([tp_], "tensor_tensor", fixb, curp[:, 15:31], invc[:, g, :], ALU.mult)
            tf2 = P.V([tf1, td], "tensor_tensor", dpT[:, k, 0:16], fixb, hbp[s][:, 15:31], ALU.subtract)
            te_ = P.V([ts_], "scalar_tensor_tensor", dpT[:, k, NM:NM + NS].rearrange("p (a b) -> p a b", a=NSEQ), curs[:, :, 15:23], 1.0 / w,
                      hbs[s][:, :, 15:23], ALU.mult, ALU.subtract)
            hbfree[s] = [tn7]
            dp_tok.append([tf2, te_, td])
            if pool_sched:
                pool_sched.pop(0)()
            if k % 4 == 3:
                pool_sched += pool_steps(k // 4)
        P.store([], np_s[:, 0:7, :], spool[:, 8:15, :])
        P.dump("dpT", dpT, dp_tok)
        P.phase_end("P9")
        while pool_sched:
            pool_sched.pop(0)()
        last = pool_last[0]
        x2_tok = [last, self.yfree, [(d.sem, d.cnt) for d in pst]]
        P.dump("xT3", xT, x2_tok)
        P.phase_end("P10")
        ostg = self.RS.t[:, 0:1024].rearrange("p (a b) -> p a b", a=8)
        osem = DSem(P, "osem")
        self.out_sems.append(osem)
        ost = {"free": [], "b67": []}
        y_p3 = y_p.rearrange("(t p) c -> p t c", p=128)

        def out_cb(d, tks):
            P.PE.wait(tks, ost["b67"])
            for t in range(8):
                ins = nc.tensor.transpose(PSb(6 + t // 4)[:, (t % 4) * 128:(t % 4) * 128 + 128], xT[:, d, NH + 128 * t:NH + 128 * t + 128], identf)
            tt = P.PE.sig(ins)
            ta = P.A([tt, ost["free"]], "copy", ostg[:, 0:4, :], PSb(6).rearrange("p (a b) -> p a b", a=4))
            tb2 = P.V([tt, ost["free"]], "tensor_copy", ostg[:, 4:8, :], PSb(7).rearrange("p (a b) -> p a b", a=4))
            ost["b67"] = [ta, tb2]
            ost["free"] = [P.store([ta, tb2], y_p3[:, :, d * 128:(d + 1) * 128], ostg, osem)]
        self.out_cb = out_cb
        x3_tok = ffn(1, x2_tok)
        x3_tok = [x3_tok, ost["b67"], ost["free"]]

        RH.reset()
        yt = [RH.carve([128, D], F32) for _ in range(2)]
        ysem = [DSem(P, f"y{i}") for i in range(2)]
        ytfree = [[x3_tok], [x3_tok]]
        bk = [[x3_tok]] * 4
        nb = 0
        for oi in range(8, 9):
            s = oi % 2
            xc = NH + 128 * oi
            dst = y_p[128 * oi:128 * oi + 128, :] if oi < 8 else y_s[:, :]
            tcs = []
            for g in range(4):
                b = nb % 4
                nb += 1
                P.PE.wait(x3_tok, bk[b])
                pv = PSb(b).rearrange("p (a b) -> p a b", a=4)
                for c in range(4):
                    k = 4 * g + c
                    ins = nc.tensor.transpose(pv[:, c, :], xT[:, k, xc:xc + 128], identf)
                tt = P.PE.sig(ins)
                if g % 2 == 0:
                    tc = P.A([tt, ytfree[s]], "copy", yt[s][:, 512 * g:512 * g + 512], PSb(b))
                else:
                    tc = P.V([tt, ytfree[s]], "tensor_copy", yt[s][:, 512 * g:512 * g + 512], PSb(b))
                bk[b] = [tc]
                tcs.append(tc)
            P.SP.wait(tcs)
            tok = ysem[s].add(nc.sync.dma_start(out=dst, in_=yt[s]))
            ytfree[s] = [tok]
        P.SP.wait(ytfree)


def _prep_inputs(inp):
    f = lambda a: np.ascontiguousarray(np.asarray(a, dtype=np.float32))
    x_prompt, x_sample = f(inp["x_prompt"]), f(inp["x_sample"])
    vecs = np.concatenate([f(inp["norm_mix"]).reshape(32, 128), f(inp["norm_ffn"]).reshape(32, 128),
                           f(inp["pool_scale"]).reshape(16, 128), f(inp["conv_b"]).reshape(8, 128),
                           f(inp["conv_ln_g"]).reshape(8, 128), f(inp["conv_ln_b"]).reshape(8, 128)], 0)
    shared = {
        "vecs": f(vecs), "convw": f(inp["conv_w"][0]), "qkg": f(np.stack([inp["q_norm"][0], inp["k_norm"][0]])),
        "sinks": f(inp["sinks"][0]), "w_in": f(inp["w_in"][0]), "w_out": f(inp["w_out"][0]), "pool_w": f(inp["pool_w"][0]),
        "w_gate": f(inp["w_gate"]), "w_up": f(inp["w_up"]), "w_down": f(inp["w_down"]),
    }
    j = np.arange(128)[:, None]
    i = np.arange(128)[None, :]
    mp = (j > i).astype(np.float32)
    mc = (j <= i).astype(np.float32)
    in_maps = []
    for r in range(8):
        b, q = divmod(r, 4)
        c = 1024 * q
        lo = c - (NE + NH)
        xp = np.zeros((NE + NH + NM, D), np.float32)
        s0 = max(lo, 0)
        xp[s0 - lo:] = x_prompt[b, s0:c + NM]
        first = (q == 0)
        masks = np.zeros((128, NMASK, 128), np.float32)
        masks[:, 0], masks[:, 1] = mp, mc
        masks[:, 2] = 0.0 if first else mp
        masks[:, 3] = mc * ((j >= NH) if first else 1.0)
        for n in range(NSEQ):
            masks[:, 4 + n] = ((i // 8) == n) * (j > (i % 8))
        masks[:, 20] = ((j // 8) == (i // 8)) * ((j % 8) <= (i % 8))
        cst = np.zeros((128, 321), np.float32)
        cst[:, 0:128] = np.eye(128)
        cst[:, 128:256] = 1.0
        invc = np.zeros((4, 16), np.float32)
        for g, w in enumerate((2, 4, 8, 16)):
            pos = np.arange(16) + (0 if first else 10 ** 6)
            invc[g] = 1.0 / np.minimum(pos + 1, w)
        cst[:, 256:320] = invc.reshape(1, 64)
        cst[:, 320] = 0.0 if first else 1.0
        m = dict(shared)
        m.update({
            "xp": xp, "xs": f(x_sample[16 * r:16 * r + 16].reshape(128, D)),
            "ck": f(inp["cache_k"][0, 16 * r:16 * r + 16].reshape(16, 128, 256)),
            "cv": f(inp["cache_v"][0, 16 * r:16 * r + 16].reshape(16, 128, 256)),
            "sconv": f(inp["state_conv"][0, 16 * r:16 * r + 16]), "spool": f(inp["state_pool"][0, 16 * r:16 * r + 16]),
            "masks": masks, "cst": cst,
        })
        in_maps.append(m)
    return in_maps


_CACHE = {}


def kernel(**inp):
    debug = tuple(inp.pop("_debug", ()))
    in_maps = _prep_inputs(inp)
    key = debug
    if key not in _CACHE:
        _CACHE[key] = Prog(debug)
    prog = _CACHE[key]
    in_maps = [{k: v for k, v in m.items() if k in prog.in_names} for m in in_maps]
    res = run_bass_kernel_spmd(prog.nc, in_maps, core_ids=list(range(8)))
    R = res.results
    y_prompt = np.zeros((2, 4096, D), np.float32)
    for r in range(8):
        b, q = divmod(r, 4)
        y_prompt[b, 1024 * q:1024 * q + 1024] = R[r]["y_p"]
    y_sample = np.concatenate([R[r]["y_s"].reshape(16, 8, D) for r in range(8)], 0)
    last = [3, 7]
    nkp = np.stack([R[r]["nk_p"].reshape(128, 4, 64) for r in last])[None]
    nvp = np.stack([R[r]["nv_p"].reshape(128, 4, 64) for r in last])[None]
    ncp = np.stack([R[r]["nc_p"] for r in last])[None]
    npp = np.stack([R[r]["np_p"] for r in last])[None]
    nks = np.concatenate([R[r]["nk_s"].reshape(16, 128, 4, 64) for r in range(8)], 0)[None]
    nvs = np.concatenate([R[r]["nv_s"].reshape(16, 128, 4, 64) for r in range(8)], 0)[None]
    ncs = np.concatenate([R[r]["nc_s"] for r in range(8)], 0)[None]
    nps = np.concatenate([R[r]["np_s"] for r in range(8)], 0)[None]
    outs = (y_prompt, y_sample, nkp, nvp, ncp, npp, nks, nvs, ncs, nps)
    outs = tuple(np.ascontiguousarray(o.astype(np.float32)) for o in outs)
    if debug:
        return outs, [{k: R[r]["dbg_" + k] for k in prog.dbg_outs} for r in range(8)]
    return outs
```

```python
import numpy as np
import concourse.bass as bass
import concourse.mybir as mybir
from concourse.bass_utils import run_bass_kernel_spmd

F32 = mybir.dt.float32
BF16 = mybir.dt.bfloat16
AF = mybir.ActivationFunctionType
ALU = mybir.AluOpType
AX = mybir.AxisListType

D = 2048
KC = 16
DFF = 5632
NE, NH, NM, NS = 128, 16, 1024, 128
TX = NH + NM + NS
TH = NE + TX
NSEQ = 16
RMS_EPS = 1e-6
LN_EPS = 1e-5
NMASK = 21
QUARTERS = [(0, 12), (12, 12), (24, 10), (34, 10)]
CONV_DVE_TAPS = 31


def _flat(xs):
    for x in xs:
        if x is None:
            continue
        if isinstance(x, (list,)):
            yield from _flat(x)
        elif isinstance(x, tuple) and len(x) == 2 and isinstance(x[1], int):
            yield x
        else:
            yield from _flat(list(x))


class Eng:
    def __init__(self, P, name, h):
        self.P, self.name, self.h = P, name, h
        self.seen = {}
        self.sem = None
        self.cnt = 0

    def _newsem(self):
        self.sem = self.P.newsem("e" + self.name)
        self.cnt = 0

    def sig(self, ins):
        if self.sem is None or self.cnt >= 3000:
            self._newsem()
        ins.then_inc(self.sem, 1)
        self.cnt += 1
        return (self.sem, self.cnt)

    def wait(self, *toks):
        for sem, c in _flat(toks):
            if self.seen.get(sem.name, 0) >= c:
                continue
            self.h.wait_ge(sem, c)
            self.seen[sem.name] = c


class DSem:
    def __init__(self, P, name):
        self.sem = P.newsem("d" + name)
        self.cnt = 0

    def add(self, ins):
        ins.then_inc(self.sem, 16)
        self.cnt += 16
        return (self.sem, self.cnt)


class Region:
    def __init__(self, P, name, nfloat):
        self.t = P.nc.alloc_sbuf_tensor(name, [128, nfloat], F32)
        self.n = nfloat
        self.o = 0

    def reset(self, o=0):
        self.o = o

    def carve(self, shape, dt):
        n = int(np.prod(shape[1:]))
        nf = (n * (2 if dt == BF16 else 4) + 3) // 4
        nf = (nf + 7) // 8 * 8
        assert self.o + nf <= self.n, (self.o, nf, self.n, shape)
        v = self.t[:shape[0], self.o:self.o + nf]
        self.o += nf
        if dt == BF16:
            v = v.bitcast(BF16)
        v = v[:, 0:n]
        if len(shape) == 3:
            v = v.rearrange("p (a b) -> p a b", a=shape[1])
        elif len(shape) == 4:
            v = v.rearrange("p (a b c) -> p a b c", a=shape[1], b=shape[2])
        return v


class _Stop(Exception):
    pass


class Prog:
    def __init__(self, debug=()):
        self.debug = set(debug)
        self.upto = None
        for d in debug:
            if d.startswith("upto:"):
                self.upto = d[5:]
        self.nsem = 0
        nc = self.nc = bass.Bass("TRN2", target_bir_lowering=False)
        self.PE = Eng(self, "pe", nc.tensor)
        self.ACT = Eng(self, "act", nc.scalar)
        self.DVE = Eng(self, "dve", nc.vector)
        self.POOL = Eng(self, "pool", nc.gpsimd)
        self.SP = Eng(self, "sp", nc.sync)
        self.out_sems = []
        self.new_out_sem()
        self.dbg_outs = {}
        self.in_names = set()
        try:
            self.build()
        except _Stop:
            pass
        self.SP.wait([(d.sem, d.cnt) for d in self.out_sems if d.cnt])

    def phase_end(self, name):
        if self.upto == name:
            raise _Stop()

    def new_out_sem(self):
        tok = None
        if self.out_sems:
            tok = (self.out_sem.sem, self.out_sem.cnt) if self.out_sem.cnt else None
        self.out_sem = DSem(self, f"out{len(self.out_sems)}")
        self.out_sems.append(self.out_sem)
        return tok

    def newsem(self, name):
        self.nsem += 1
        return self.nc.semaphore(f"{name}_{self.nsem}").__enter__()

    def din(self, name, shape, dt=F32):
        self.in_names.add(name)
        return self.nc.dram_tensor(name, list(shape), dt, kind="ExternalInput").ap()

    def dout(self, name, shape, dt=F32):
        return self.nc.dram_tensor(name, list(shape), dt, kind="ExternalOutput").ap()

    def op(self, eng, deps, fn, *a, **kw):
        eng.wait(deps)
        return eng.sig(fn(*a, **kw))

    def V(self, deps, name, *a, **kw):
        return self.op(self.DVE, deps, getattr(self.nc.vector, name), *a, **kw)

    def G(self, deps, name, *a, **kw):
        return self.op(self.POOL, deps, getattr(self.nc.gpsimd, name), *a, **kw)

    def A(self, deps, name, *a, **kw):
        return self.op(self.ACT, deps, getattr(self.nc.scalar, name), *a, **kw)

    def store(self, deps, out, in_, dsem=None):
        self.SP.wait(deps)
        return (dsem or self.out_sem).add(self.nc.sync.dma_start(out=out, in_=in_))

    def load(self, q, dsem, deps, out, in_):
        q.wait(deps)
        return dsem.add(q.h.dma_start(out=out, in_=in_))

    def dump(self, name, ap, deps):
        if name not in self.debug:
            return
        d = self.dout("dbg_" + name, list(ap.shape), ap.dtype)
        self.dbg_outs[name] = d
        self.store(deps, d, ap)

    def bank(self, b, n=1):
        return self.PS[:, 512 * b:512 * (b + n)]

    def wslab(self, src3, nchunk, ncol):
        s = self.wn % 4
        self.wn += 1
        self.POOL.wait(self.wfree[s])
        self.wfree[s] = []
        v = self.wslot[s][:, 0:nchunk * ncol].rearrange("p (a b) -> p a b", a=nchunk)
        tok = self.wsem[s].add(self.nc.gpsimd.dma_start(out=v, in_=src3))
        return v, tok, s

    def wslab2(self, src3, nchunk, ncol):
        s = self.wn % 4
        assert s % 2 == 0
        self.wn += 2
        self.POOL.wait(self.wfree[s], self.wfree[s + 1])
        self.wfree[s] = []
        self.wfree[s + 1] = []
        v = self.RW.t[:, 2048 * s:2048 * (s + 2)].bitcast(BF16)[:, 0:nchunk * ncol].rearrange("p (a b) -> p a b", a=nchunk)
        tok = self.wsem[s].add(self.nc.gpsimd.dma_start(out=v, in_=src3))
        return v, tok, s

    def wrelease(self, s, tok):
        self.wfree[s].append(tok)

    def build(self):
        nc = self.nc
        P = self
        xp = P.din("xp", [NE + NH + NM, D])
        xs = P.din("xs", [NS, D])
        ck = P.din("ck", [NSEQ, 128, 256])
        cv = P.din("cv", [NSEQ, 128, 256])
        sconv = P.din("sconv", [NSEQ, 30, 1024])
        spool = P.din("spool", [NSEQ, 15, D])
        vecs = P.din("vecs", [104, 128])
        convw = P.din("convw", [31, 1024])
        qkg = P.din("qkg", [2, 64])
        sinks = P.din("sinks", [16])
        w_in = P.din("w_in", [D, 3584])
        w_out = P.din("w_out", [D, D])
        pool_w = P.din("pool_w", [4, 512, 512])
        masks = P.din("masks", [128, NMASK, 128])
        cst = P.din("cst", [128, 128 + 128 + 64 + 1])

        y_p = P.dout("y_p", [NM, D])
        y_s = P.dout("y_s", [NS, D])
        nk_p = P.dout("nk_p", [128, 256])
        nv_p = P.dout("nv_p", [128, 256])
        nc_p = P.dout("nc_p", [30, 1024])
        np_p = P.dout("np_p", [15, D])
        nk_s = P.dout("nk_s", [NSEQ, 128, 256])
        nv_s = P.dout("nv_s", [NSEQ, 128, 256])
        nc_s = P.dout("nc_s", [NSEQ, 30, 1024])
        np_s = P.dout("np_s", [NSEQ, 15, D])

        RX = Region(P, "RX", TX * KC)
        RH = Region(P, "RH", TH * KC // 2)
        RA = Region(P, "RA", 10240)
        RW = Region(P, "RW", 8192)
        self.RW = RW
        RC = Region(P, "RC", 3072)
        self.PS = nc.alloc_psum_tensor("PS", [128, 4096], F32)
        PSb = lambda b, n=1: self.bank(b, n)
        self.wslot = [RW.t[:, 2048 * s:2048 * (s + 1)].bitcast(BF16) for s in range(4)]
        self.wsem = [DSem(P, f"w{s}") for s in range(4)]
        self.wfree = [[] for _ in range(4)]
        self.wn = 0

        identf = RC.carve([128, 128], F32)
        onesf = RC.carve([128, 128], F32)
        invc = RC.carve([128, 4, 16], F32)
        hvalid = RC.carve([128, 1], F32)
        identb = RC.carve([128, 128], BF16)
        onesb = RC.carve([128, 128], BF16)
        maskb = RC.carve([128, NMASK, 128], BF16)
        vT = RC.carve([128, 104], F32)
        cwT = RC.carve([128, 8, 31], F32)
        qg_bc = RC.carve([128, 64], F32)
        kg_bc = RC.carve([128, 64], F32)
        esink = RC.carve([128, 16], F32)
        sm = RC.carve([128, 64], F32)
        cld = DSem(P, "cld")
        cstt = RX.carve([128, 321], F32)
        vecl = RX.carve([104, 128], F32)
        cwl = RX.carve([31, 1024], F32)
        t1 = P.load(P.SP, cld, [], cstt, cst[:, :])
        P.load(P.SP, cld, [], vecl, vecs[:, :])
        P.load(P.SP, cld, [], cwl, convw[:, :])
        P.load(P.SP, cld, [], qg_bc, qkg[0].partition_broadcast(128))
        P.load(P.SP, cld, [], kg_bc, qkg[1].partition_broadcast(128))
        tl = P.load(P.SP, cld, [], esink, sinks.partition_broadcast(128))
        mld = DSem(P, "mld")
        tm = P.load(P.POOL, mld, [], maskb, masks[:, :, :])
        c1 = P.V([tl], "tensor_copy", identf, cstt[:, 0:128])
        P.V([], "tensor_copy", identb, cstt[:, 0:128])
        P.V([], "tensor_scalar", onesf, cstt[:, 128:256], 1.0 / 1024, 0.0, ALU.mult, ALU.add)
        P.V([], "tensor_scalar", onesb, cstt[:, 128:256], 1.0 / 2048, 0.0, ALU.mult, ALU.add)
        P.V([], "tensor_copy", invc, cstt[:, 256:320].rearrange("p (a b) -> p a b", a=4))
        c2 = P.V([], "tensor_copy", hvalid, cstt[:, 320:321])
        a1 = P.A([tl], "activation", esink, esink, AF.Exp)
        a2 = P.A([], "mul", qg_bc, qg_bc, 0.125)
        P.PE.wait(c1, c2)
        i1 = nc.tensor.transpose(PSb(0)[:, 0:104], vecl, identf[0:104, 0:104])
        for c in range(8):
            i2 = nc.tensor.transpose(PSb(1)[:, c * 31:(c + 1) * 31], cwl[:, c * 128:(c + 1) * 128], identf[0:31, 0:31])
        tp = P.PE.sig(i2)
        P.V([tp], "tensor_copy", vT, PSb(0)[:, 0:104])
        cdone = P.V([], "tensor_copy", cwT, PSb(1)[:, 0:248].rearrange("p (a b) -> p a b", a=8))
        cdone = [cdone, a1, a2, tm, c2]
        V_NMIX, V_NFFN, V_PSC, V_CB, V_LG, V_LB = 0, 32, 64, 80, 88, 96
        RX.reset()

        xT = None
        hT = RH.carve([128, KC, TH], BF16)

        tiles = [(xp[128 * i:128 * i + 128, :], 128, 128 * i) for i in range(9)]
        tiles += [(xp[1152:1168, :], 16, 1152), (xs[:, :], 128, 1168)]

        xt = [RX.carve([128, D], F32) for _ in range(3)]
        hn = [RX.carve([128, D], BF16) for _ in range(3)]
        junk = RX.carve([128, D], BF16)
        xld = [DSem(P, f"x{i}") for i in range(3)]
        xfree = [[cdone], [cdone], [cdone]]
        hnfree = [[], [], []]
        bfree = [[cdone], [cdone], [cdone]]
        hT_tok = []
        c0 = {}

        def p0_load(n):
            src, nr, col = tiles[n]
            s = n % 3
            c0[n] = {"tld": P.load(P.SP, xld[s], xfree[s], xt[s][:nr], src)}

        def p0_sq(n):
            src, nr, col = tiles[n]
            s = n % 3
            c0[n]["tA"] = P.A([c0[n]["tld"]], "activation", junk[:nr], xt[s][:nr], AF.Square, accum_out=sm[:nr, 2 * s:2 * s + 1])

        def p0_v1(n):
            src, nr, col = tiles[n]
            s = n % 3
            c0[n]["tB"] = P.V([c0[n]["tA"]], "tensor_scalar", sm[:nr, 2 * s + 1:2 * s + 2], sm[:nr, 2 * s:2 * s + 1], 1.0 / D, RMS_EPS, ALU.mult, ALU.add)

        def p0_sqrt(n):
            src, nr, col = tiles[n]
            s = n % 3
            rs = sm[:nr, 2 * s + 1:2 * s + 2]
            c0[n]["tC"] = P.A([c0[n]["tB"]], "sqrt", rs, rs)

        def p0_v2(n):
            src, nr, col = tiles[n]
            s = n % 3
            rs = sm[:nr, 2 * s + 1:2 * s + 2]
            c0[n]["tD"] = P.V([c0[n]["tC"]], "reciprocal", rs, rs)

        def p0_mul(n):
            src, nr, col = tiles[n]
            s = n % 3
            rsc = sm[:nr, 2 * s + 1:2 * s + 2]
            tE1 = P.A([c0[n]["tD"], hnfree[s]], "mul", hn[s][:nr, 0:1024], xt[s][:nr, 0:1024], rsc)
            tE2 = P.V([c0[n]["tD"], hnfree[s]], "tensor_scalar", hn[s][:nr, 1024:2048], xt[s][:nr, 1024:2048], rsc, 0.0, ALU.mult, ALU.add)
            tE = [tE1, tE2]
            xfree[s] = [tE]
            c0[n]["tE"] = tE

        def p0_pe(n):
            src, nr, col = tiles[n]
            s = n % 3
            P.PE.wait(c0[n]["tE"], bfree[s], cdone)
            pb = PSb(2 * s, 2).bitcast(BF16).rearrange("p (a b) -> p a b", a=KC)
            for k in range(KC):
                ins = nc.tensor.transpose(pb[:, k, 0:nr], hn[s][:nr, k * 128:(k + 1) * 128], identb[:nr, :nr])
            tP = P.PE.sig(ins)
            hnfree[s] = [tP]
            c0[n]["tP"] = tP

        def p0_ev(n):
            src, nr, col = tiles[n]
            s = n % 3
            pb = PSb(2 * s, 2).bitcast(BF16).rearrange("p (a b) -> p a b", a=KC)
            tH = P.V([c0[n]["tP"]], "tensor_tensor", hT[:, :, col:col + nr], pb[:, :, 0:nr],
                     vT[:, V_NMIX:V_NMIX + 16].unsqueeze(2).to_broadcast([128, KC, nr]), ALU.mult)
            bfree[s] = [tH]
            hT_tok.append(tH)

        NT = len(tiles)
        ok = lambda n: 0 <= n < NT
        p0_load(0)
        p0_load(1)
        for i in range(NT + 2):
            if ok(i - 1):
                p0_sqrt(i - 1)
                p0_v2(i - 1)
                p0_mul(i - 1)
                p0_pe(i - 1)
            if ok(i - 2):
                p0_ev(i - 2)
            if ok(i + 2):
                p0_load(i + 2)
            if ok(i):
                p0_sq(i)
                p0_v1(i)
        P.dump("hT0", hT, hT_tok)
        P.phase_end("P0")
        RX.reset()

        w_in3 = w_in.rearrange("(c p) n -> p c n", p=128)
        conv_out = RX.carve([128, 8, TX], F32)
        glp = [RX.carve([128, 1072], F32) for _ in range(2)]
        gls = [RX.carve([128, NSEQ, 38], F32) for _ in range(2)]
        sg = RX.carve([128, 1200], F32)
        gsn = RX.carve([128, 128], F32)
        stc = [[RX.carve([120, 128], F32) for _ in range(4)] for _ in range(2)]
        cv2 = [RX.carve([128, TX], F32)] * 2
        NDG = 16
        dgr = [RX.carve([128, 128], F32) for _ in range(NDG)]
        dgfree = [[] for _ in range(NDG)]
        dgn = [0]
        b3free = []
        ncp = RA.carve([32, 1024], F32)
        ncs = RA.carve([128, 1024], F32)
        sld = [DSem(P, f"sld{i}") for i in range(2)]
        stfree = [[], []]
        cv2free = [[], []]
        TD = CONV_DVE_TAPS
        pieces = [(96, 512), (608, 512), (1120, 176)]
        slabs = {}
        glfree = [[], []]
        evfree = []
        b6free = []
        b7free = []
        nco_tok = []
        conv_tok = []
        def u_slabs(jj):
            return {'a': P.wslab(w_in3[:, :, 1536 + 256 * jj:1536 + 256 * jj + 256], KC, 256),
                    'g': P.wslab(w_in3[:, :, 2560 + 256 * jj:2560 + 256 * jj + 256], KC, 256)}
        nxt = u_slabs(0)
        for j in range(8):
            jj, dd = divmod(j, 2)
            if dd == 0:
                slabs = nxt
                if jj + 1 < 4:
                    nxt = u_slabs(jj + 1)
            s = j % 2
            for r in range(4):
                tst = P.load(P.SP, sld[s], stfree[s] if r == 0 else [], stc[s][r],
                             sconv[4 * r:4 * r + 4, :, j * 128:(j + 1) * 128].rearrange("n i c -> (n i) c"))
            wj = cwT[:, j, :]
            dg_tok = []

            def build_dg(n_):
                for _ in range(n_):
                    t_ = len(dg_tok)
                    if t_ >= 31:
                        return
                    r_ = dgn[0] % NDG
                    dgn[0] += 1
                    dg_tok.append((r_, P.A([dgfree[r_], cdone], "activation", dgr[r_], identf, AF.Copy, scale=wj[:, t_:t_ + 1])))
            build_dg(NDG)
            P.PE.wait(slabs['a'][1], slabs['g'][1], evfree, hT_tok)
            for k in range(KC):
                for pi, (c0, n) in enumerate(pieces):
                    ins = nc.tensor.matmul(PSb(pi)[:, 0:n], lhsT=slabs['g'][0][:, k, dd * 128:(dd + 1) * 128],
                                           rhs=hT[:, k, c0:c0 + n], start=(k == 0), stop=(k == KC - 1))
            tmmG = P.PE.sig(ins)
            tSs = []
            off = 0
            for pi, (c0, n) in enumerate(pieces):
                tSs.append(P.A([tmmG], "activation", sg[:, off:off + n], PSb(pi)[:, 0:n], AF.Sigmoid))
                off += n
            P.PE.wait(tSs)
            for k in range(KC):
                for pi, (c0, n) in enumerate(pieces):
                    ins = nc.tensor.matmul(PSb(pi)[:, 0:n], lhsT=slabs['a'][0][:, k, dd * 128:(dd + 1) * 128],
                                           rhs=hT[:, k, c0:c0 + n], start=(k == 0), stop=(k == KC - 1))
            tmm = P.PE.sig(ins)
            if dd == 1:
                P.wrelease(slabs['a'][2], tmm)
                P.wrelease(slabs['g'][2], tmm)
            P.PE.wait(tst, b6free)
            for r in range(4):
                ins = nc.tensor.transpose(PSb(6)[:, r * 120:(r + 1) * 120], stc[s][r], identf[0:120, 0:120])
            tst_t = P.PE.sig(ins)
            stfree[s] = [tst_t]
            tcp = P.A([tst_t, glfree[s]], "copy", gls[s][:, :, 0:30],
                      PSb(6)[:, 0:480].rearrange("p (a b) -> p a b", a=NSEQ))
            b6free = [tcp]
            tg0 = P.V([tmm, tSs[0], glfree[s]], "tensor_tensor", glp[s][:, 0:512], PSb(0)[:, 0:512], sg[:, 0:512], ALU.mult)
            tg1 = P.V([tSs[1]], "tensor_tensor", glp[s][:, 512:1024], PSb(1)[:, 0:512], sg[:, 512:1024], ALU.mult)
            tg2 = P.V([tSs[2]], "tensor_tensor", glp[s][:, 1024:1072], PSb(2)[:, 0:48], sg[:, 1024:1072], ALU.mult)
            tg3 = P.V([], "tensor_tensor", gsn, PSb(2)[:, 48:176], sg[:, 1072:1200], ALU.mult)
            evfree = [tg3]
            tg4 = P.A([tg3, tcp], "copy", gls[s][:, :, 30:38], gsn.rearrange("p (a b) -> p a b", a=NSEQ))
            P.PE.wait(tg2, tg3, b7free)
            nc.tensor.transpose(PSb(7)[0:32, 0:128], glp[s][:, 1040:1072], identf)
            ins = nc.tensor.transpose(PSb(7)[:, 128:256], gsn, identf)
            tno = P.PE.sig(ins)
            P.A([tno], "copy", ncp[:, j * 128:(j + 1) * 128], PSb(7)[0:32, 0:128])
            tnc = P.A([], "copy", ncs[:, j * 128:(j + 1) * 128], PSb(7)[:, 128:256])
            b7free = [tnc]
            nco_tok = [tnc]
            bj = vT[:, V_CB + j:V_CB + j + 1]
            P.PE.wait(tg4, b3free)
            for t in range(31):
                r, tk = dg_tok[t]
                P.PE.wait(tk)
                ins = nc.tensor.matmul(PSb(3)[:, 0:128], lhsT=dgr[r], rhs=gls[s][:, :, t:t + 8], start=(t == 0), stop=(t == 30))
                if t % 4 == 3 or t == 30:
                    tkd = P.PE.sig(ins)
                    for t2 in range(t - (t % 4), t + 1):
                        dgfree[dg_tok[t2][0]] = [tkd]
                    build_dg(4)
            tsv = P.A([tkd], "activation", conv_out[:, j, NH + NM:TX], PSb(3)[:, 0:128], AF.Identity, bias=bj)
            b3free = [tsv]
            yp = conv_out[:, j, 0:NH + NM]
            tpv = P.V([tg0, tg1, tg2], "tensor_scalar", yp, glp[s][:, 2:2 + 1040], wj[:, 0:1], bj, ALU.mult, ALU.add)
            if TD == 31:
                y2p = cv2[s][:, 0:NH + NM]
                tp2 = P.V([tg0, tg1, tg2, cv2free[s]], "tensor_scalar", y2p, glp[s][:, 3:3 + 1040], wj[:, 1:2], 0.0, ALU.mult, ALU.add)
                for t in range(2, 31):
                    if t % 2 == 0:
                        tpv = P.V([tpv], "scalar_tensor_tensor", yp, glp[s][:, 2 + t:2 + t + 1040], wj[:, t:t + 1], yp, ALU.mult, ALU.add)
                    else:
                        tp2 = P.V([tp2], "scalar_tensor_tensor", y2p, glp[s][:, 2 + t:2 + t + 1040], wj[:, t:t + 1], y2p, ALU.mult, ALU.add)
                tpv = P.V([tpv, tp2], "tensor_tensor", yp, yp, y2p, ALU.add)
                cv2free[s] = [tpv]
                cv2free[1 - s] = [tpv]
            for t in range(1, TD if TD < 31 else 0):
                tpv = P.V([tpv], "scalar_tensor_tensor", yp, glp[s][:, 2 + t:2 + t + 1040], wj[:, t:t + 1], yp, ALU.mult, ALU.add)
                tsv = P.V([tsv], "scalar_tensor_tensor", ysv, gls[s][:, :, t:t + 8], wj[:, t:t + 1], ysv, ALU.mult, ALU.add)
            if TD < 31:
                def prod(dst, t, extra):
                    ta_ = P.A([tg0, tg1, tg2, extra], "activation", dst[:, 0:NH + NM], glp[s][:, 2 + t:2 + t + 1040], AF.Copy, scale=wj[:, t:t + 1])
                    tb_ = P.A([tg4], "activation", dst[:, NH + NM:TX].rearrange("p (a b) -> p a b", a=NSEQ), gls[s][:, :, t:t + 8], AF.Copy,
                              scale=wj[:, t:t + 1])
                    return [ta_, tb_]
                tacc = prod(cv2[s], TD, cv2free[s])
                for t in range(TD + 1, 31):
                    k2 = t % 2
                    tpr = prod(ctmp[k2], t, ctfree[k2])
                    tacc = [P.G([tacc, tpr], "tensor_tensor", cv2[s], cv2[s], ctmp[k2], ALU.add)]
                    ctfree[k2] = tacc
                tpv = P.V([tpv, tsv, tacc], "tensor_tensor", conv_out[:, j, :], conv_out[:, j, :], cv2[s], ALU.add)
                tsv = tpv
                cv2free[s] = [tpv]
                cv2free[1 - s] = [tpv]
            glfree[s] = [tpv, tsv]
            conv_tok = [tpv, tsv]
        P.store(nco_tok, nc_p[:, :], ncp[2:32, :])
        P.store(nco_tok, nc_s[:, 22:30, :], ncs)
        P.store([], nc_s[:, 0:22, :], sconv[:, 8:30, :])
        nco_store = P.new_out_sem()
        P.dump("conv_out", conv_out, conv_tok)
        P.phase_end("P1")

        RA.reset()
        mixT = RA.carve([128, KC, TX], BF16)
        RX.reset(8 * TX)
        sqs = [RX.carve([128, TX], F32) for _ in range(2)]
        rsb = RX.carve([128, TX], F32)
        xpieces = [(0, 512), (512, 512), (1024, 144)]
        P.PE.wait(conv_tok, evfree)
        for j in range(8):
            for pi, (c0, n) in enumerate(xpieces):
                ins = nc.tensor.matmul(PSb(pi)[:, 0:n], lhsT=onesf, rhs=conv_out[:, j, c0:c0 + n], start=(j == 0), stop=(j == 7))
        tmean = P.PE.sig(ins)
        sqfree = [[], []]
        yc_tok = []
        for j in range(8):
            tks = []
            for pi, (c0, n) in enumerate(xpieces):
                tks.append(P.V([tmean], "tensor_tensor", conv_out[:, j, c0:c0 + n], conv_out[:, j, c0:c0 + n], PSb(pi)[:, 0:n], ALU.subtract))
            tq = P.A([tks, sqfree[j % 2]], "activation", sqs[j % 2], conv_out[:, j, :], AF.Square)
            P.PE.wait(tq)
            for pi, (c0, n) in enumerate(xpieces):
                ins = nc.tensor.matmul(PSb(3 + pi)[:, 0:n], lhsT=onesf, rhs=sqs[j % 2][:, c0:c0 + n], start=(j == 0), stop=(j == 7))
            sqfree[j % 2] = [P.PE.sig(ins)]
            yc_tok.append(tks)
        tvar = sqfree[1]
        tr = []
        for pi, (c0, n) in enumerate(xpieces):
            tr.append(P.V([tvar], "tensor_scalar", rsb[:, c0:c0 + n], PSb(3 + pi)[:, 0:n], 1.0, LN_EPS, ALU.mult, ALU.add))
        tr2 = P.A([tr], "sqrt", rsb, rsb)
        tr3 = P.V([tr2], "reciprocal", rsb, rsb)
        ln_tok = []
        for j in range(8):
            tz = P.V([tr3, yc_tok[j]], "tensor_tensor", conv_out[:, j, :], conv_out[:, j, :], rsb, ALU.mult)
            ln_tok.append(P.A([tz], "activation", mixT[:, 8 + j, :], conv_out[:, j, :], AF.Silu,
                              bias=vT[:, V_LB + j:V_LB + j + 1], scale=vT[:, V_LG + j:V_LG + j + 1]))
        ps_free = [tr, yc_tok[-1]]
        P.dump("cT", mixT[:, 8:16, :], ln_tok)
        P.phase_end("P2")
        RX.reset()

        qT = RX.carve([128, 8, TX], BF16)
        kT = RX.carve([128, 4, TH], BF16)
        vaug = RX.carve([128, 11, 4 * 80], BF16)
        sq3 = [RX.carve([128, 512], F32) for _ in range(4)]
        qn = [RX.carve([128, 512], F32) for _ in range(4)]
        qb = [RX.carve([128, 512], BF16) for _ in range(4)]
        kst = [RX.carve([128, 256], F32) for _ in range(3)]
        vst = [RX.carve([128, 256], F32) for _ in range(3)]
        p3free = [ln_tok, conv_tok, nco_store]
        tva = P.V([p3free], "memset", vaug, 1.0)
        bankfree = {b: [ps_free] for b in range(8)}
        qfree = [[tva], [tva], [tva], [tva]]
        qT_tok, kT_tok, v_tok = [], [], []
        groups = []
        for g in range(3):
            tl3 = list(range(1, 11)) if g < 2 else list(range(0, 11))
            for ti in tl3:
                groups.append({"g": g, "ti": ti, "first": ti == tl3[0], "last": ti == tl3[-1]})
        for n, c in enumerate(groups):
            c["n"] = n
            c["b"] = n % 4
            c["tb"] = 4 + n % 3
            c["s"] = n % 4
        slab3 = {}
        lastmm_box = [None]

        def s0(c):
            g, ti, b = c["g"], c["ti"], c["b"]
            src, nr, col = tiles[ti]
            if c["first"]:
                slab3[g] = P.wslab2(w_in3[:, :, 512 * g:512 * g + 512], KC, 512)
            sl2 = slab3[g]
            P.PE.wait(sl2[1], bankfree[b], hT_tok)
            for k in range(KC):
                ins = nc.tensor.matmul(PSb(b)[:nr, :], lhsT=hT[:, k, col:col + nr], rhs=sl2[0][:, k, :], start=(k == 0), stop=(k == KC - 1))
            c["tmm"] = P.PE.sig(ins)
            lastmm_box[0] = c["tmm"]
            if c["last"]:
                P.wrelease(sl2[2], c["tmm"])
                P.wrelease(sl2[2] + 1, c["tmm"])

        def s1(c):
            g, ti, b, s = c["g"], c["ti"], c["b"], c["s"]
            src, nr, col = tiles[ti]
            nhd = 8 if g < 2 else 4
            wd = nhd * 64
            c["t_a"] = P.A([c["tmm"], qfree[s]], "activation", sq3[s][:nr, 0:wd], PSb(b)[:nr, 0:wd], AF.Square)
            c["tvs"] = []
            if g == 2:
                oi = {8: 0, 9: 1, 10: 2}.get(ti, None)
                vsrc = PSb(b)[:nr, 256:512]
                t_v = P.A([], "copy", vaug[:nr, ti, :].rearrange("p (h e) -> p h e", e=80)[:, :, 0:64], vsrc.rearrange("p (h d) -> p h d", d=64))
                c["tvs"] = [t_v]
                v_tok.append(t_v)
                if oi is not None:
                    t_v2 = P.A([], "copy", vst[oi][:nr], vsrc)
                    c["tvs"].append(t_v2)
                    if ti == 8:
                        P.store([t_v2], nv_p[0:112, :], vst[0][16:128, :])
                    elif ti == 9:
                        P.store([t_v2], nv_p[112:128, :], vst[1][0:16, :])
                    else:
                        P.store([t_v2], nv_s[:, 120:128, :], vst[2])

        def s2(c):
            g, ti, s = c["g"], c["ti"], c["s"]
            src, nr, col = tiles[ti]
            nhd = 8 if g < 2 else 4
            wd = nhd * 64
            ssq = sm[:nr, 8 + 8 * s:8 + 8 * s + nhd]
            t_b = P.V([c["t_a"]], "tensor_reduce", ssq, sq3[s][:nr, 0:wd].rearrange("p (h d) -> p h d", d=64), AX.X, ALU.add)
            c["t_c"] = P.V([t_b], "tensor_scalar", ssq, ssq, 1.0 / 64, RMS_EPS, ALU.mult, ALU.add)

        def s3(c):
            g, ti, s = c["g"], c["ti"], c["s"]
            src, nr, col = tiles[ti]
            nhd = 8 if g < 2 else 4
            ssq = sm[:nr, 8 + 8 * s:8 + 8 * s + nhd]
            c["t_d"] = P.A([c["t_c"]], "sqrt", ssq, ssq)

        def s4(c):
            g, ti, b, s = c["g"], c["ti"], c["b"], c["s"]
            src, nr, col = tiles[ti]
            nhd = 8 if g < 2 else 4
            wd = nhd * 64
            ssq = sm[:nr, 8 + 8 * s:8 + 8 * s + nhd]
            t_e = P.V([c["t_d"]], "reciprocal", ssq, ssq)
            qn3 = qn[s][:nr, 0:wd].rearrange("p (h d) -> p h d", d=64)
            t_f = P.V([t_e], "tensor_tensor", qn3, PSb(b)[:nr, 0:wd].rearrange("p (h d) -> p h d", d=64),
                      ssq.unsqueeze(2).to_broadcast([nr, nhd, 64]), ALU.mult)
            bankfree[b] = [t_f, c["tvs"]]
            if g < 2:
                t_g = P.V([t_f], "tensor_tensor", qb[s][:nr, :].rearrange("p (h d) -> p h d", d=64), qn3,
                          qg_bc[:nr].unsqueeze(1).to_broadcast([nr, 8, 64]), ALU.mult)
                c["rdy"] = [t_g]
            else:
                oi = {8: 0, 9: 1, 10: 2}.get(ti, None)
                kdst = kst[oi] if oi is not None else qn[s][:, 256:512]
                kd3 = kdst[:nr].rearrange("p (h d) -> p h d", d=64)
                t_g = P.V([t_f], "tensor_tensor", kd3, qn3, kg_bc[:nr].unsqueeze(1).to_broadcast([nr, 4, 64]), ALU.mult)
                kd = qb[s][:nr, :].rearrange("p (h t d) -> p h t d", h=4, t=2)
                t_h = P.V([t_g], "tensor_copy", kd[:, :, 0, :], kd3)
                t_i = P.V([t_g], "tensor_copy", kd[:, :, 1, :], kd3)
                c["rdy"] = [t_h, t_i]
                if ti == 8:
                    P.store([t_g], nk_p[0:112, :], kst[0][16:128, :])
                elif ti == 9:
                    P.store([t_g], nk_p[112:128, :], kst[1][0:16, :])
                elif ti == 10:
                    P.store([t_g], nk_s[:, 120:128, :], kst[2])

        def s5(c):
            g, ti, tb, s = c["g"], c["ti"], c["tb"], c["s"]
            src, nr, col = tiles[ti]
            P.PE.wait(c["rdy"], bankfree[tb])
            pb = PSb(tb).bitcast(BF16)[:, 0:512].rearrange("p (a b) -> p a b", a=4)
            for cc in range(4):
                ins = nc.tensor.transpose(pb[:, cc, 0:nr], qb[s][:nr, cc * 128:(cc + 1) * 128], identb[:nr, :nr])
            c["t_t"] = P.PE.sig(ins)

        def s6(c):
            g, ti, tb, s = c["g"], c["ti"], c["tb"], c["s"]
            src, nr, col = tiles[ti]
            pb = PSb(tb).bitcast(BF16)[:, 0:512].rearrange("p (a b) -> p a b", a=4)
            if g < 2:
                t_q = P.A([c["t_t"]], "copy", qT[:, 4 * g:4 * g + 4, col - NE:col - NE + nr], pb[:, :, 0:nr])
                qT_tok.append(t_q)
            else:
                t_q = P.A([c["t_t"]], "copy", kT[:, :, col:col + nr], pb[:, :, 0:nr])
                kT_tok.append(t_q)
            bankfree[tb] = [t_q]
            qfree[s] = [c["t_t"], c["rdy"]]

        NG = len(groups)
        gok = lambda n: 0 <= n < NG
        for i in range(NG + 3):
            if gok(i):
                s0(groups[i])
            if gok(i - 3):
                s5(groups[i - 3])
                s6(groups[i - 3])
            if gok(i - 2):
                s3(groups[i - 2])
                s4(groups[i - 2])
            if gok(i - 1):
                s1(groups[i - 1])
                s2(groups[i - 1])
        lastmm = lastmm_box[0]
        P.store([], nk_s[:, 0:120, :], ck[:, 8:128, :])
        P.store([], nv_s[:, 0:120, :], cv[:, 8:128, :])
        P.dump("qT", qT, qT_tok)
        P.dump("kT", kT, kT_tok)
        P.dump("vaug", vaug, v_tok)
        P.phase_end("P3")
        allfree = [bankfree[b] for b in range(8)]

        pT = [RX.carve([128, 4, 128], BF16) for _ in range(4)]
        att = [RX.carve([128, 1024], BF16) for _ in range(2)]
        den = [RX.carve([128, 16], F32) for _ in range(2)]
        RH.reset()
        ckl = [RH.carve([128, 256], F32) for _ in range(4)]
        cvl = [RH.carve([128, 256], F32) for _ in range(4)]
        ckd = [RH.carve([128, 4, 2, 64], BF16) for _ in range(4)]
        kcT = [RH.carve([128, 4, 128], BF16) for _ in range(4)]
        vca = [RH.carve([128, 4 * 80], BF16) for _ in range(4)]
        qbd = [RH.carve([128, 8, 2, 128], BF16) for _ in range(2)]
        cld2 = [DSem(P, f"ck{i}") for i in range(4)]
        tvc = P.V([lastmm], "memset", vca[0], 1.0)
        tvc = P.V([], "memset", vca[1], 1.0)
        tvc = P.V([], "memset", vca[2], 1.0)
        tvc = P.V([], "memset", vca[3], 1.0)
        tz0 = P.V([lastmm], "memset", qbd[0], 0.0)
        tz1 = P.V([], "memset", qbd[1], 0.0)
        qbd_free = [[tz0, tz1], [tz0, tz1]]
        qbd_tok = {}
        last_qk_of_tile = {}
        qtiles = []
        for ti in range(1, 10):
            src, nr, col = tiles[ti]
            prev = tiles[ti - 1]
            qtiles.append((ti, nr, col - NE, [("p", ti - 1, 128, 2 if ti == 1 else 0), ("p", ti, nr, 3 if ti == 1 else 1)]))
        qtiles.append((10, 128, tiles[10][2] - NE, [("c", n, 128, 4 + n) for n in range(NSEQ)] + [("p", 10, 128, 20)]))
        sbank = 0
        sfree = [allfree] * 4
        pfree = [[], [], [], []]
        ofree = [allfree]
        tbfree = [allfree]
        cfree = [[tvc], [tvc], [tvc], [tvc]]
        ncache = 0
        units = []
        for qi, (ti, nq, xc, keys) in enumerate(qtiles):
            for ki, kspec in enumerate(keys):
                for kh in range(4):
                    units.append((qi, ti, nq, xc, ki, len(keys), kspec, kh))
        qk_tok = {}
        cache_ready = {}
        cache_late = set()

        cache_dma = {}

        def prep_dma(n):
            if n >= NSEQ or n in cache_dma:
                return
            s = n % 4
            t1 = P.load(P.SP, cld2[s], cfree[s], ckl[s], ck[n])
            cache_dma[n] = P.load(P.SP, cld2[s], [], cvl[s], cv[n])

        def prep_cache(n):
            if n >= NSEQ or n in cache_ready:
                return
            prep_dma(n)
            s = n % 4
            t2 = cache_dma[n]
            a = P.V([t2], "tensor_copy", ckd[s][:, :, 0, :], ckl[s].rearrange("p (h d) -> p h d", d=64))
            b_ = P.V([], "tensor_copy", ckd[s][:, :, 1, :], ckl[s].rearrange("p (h d) -> p h d", d=64))
            c_ = P.V([], "tensor_copy", vca[s].rearrange("p (h e) -> p h e", e=80)[:, :, 0:64], cvl[s].rearrange("p (h d) -> p h d", d=64))
            cache_ready[n] = (a, b_, c_)

        def prep_cache_late(n):
            if n >= NSEQ or n in cache_late:
                return
            cache_late.add(n)
            prep_cache(n)
            s = n % 4
            a, b_, c_ = cache_ready[n]
            P.PE.wait(a, b_, tbfree[0])
            pb = PSb(7).bitcast(BF16)[:, 0:512].rearrange("p (a b) -> p a b", a=4)
            for kh2 in range(4):
                ins = nc.tensor.transpose(pb[:, kh2, :], ckd[s][:, kh2].rearrange("p t d -> p (t d)"), identb)
            tt = P.PE.sig(ins)
            tq = P.A([tt], "copy", kcT[s], pb)
            tbfree[0] = [tq]
            cache_ready[n] = (tq, c_)

        def build_qbd(qi_):
            if qi_ >= len(qtiles) or qi_ in qbd_tok:
                return
            ti_, nq_, xc_, _k = qtiles[qi_]
            s_ = qi_ % 2
            fr = [qbd_free[s_], qT_tok, last_qk_of_tile.get(qi_ - 2)]
            ta_ = P.A(fr, "copy", qbd[s_][0:64, :, 0, 0:nq_], qT[0:64, :, xc_:xc_ + nq_])
            tb_ = P.A(fr, "copy", qbd[s_][64:128, :, 1, 0:nq_], qT[64:128, :, xc_:xc_ + nq_])
            qbd_tok[qi_] = [ta_, tb_]

        def emit_qk(u):
            qi, ti, nq, xc, ki, nk, kspec, kh = units[u]
            b = u % 4
            kind, kidx, nkeys, mi = kspec
            if ki == 0 and kh == 0:
                build_qbd(qi)
                build_qbd(qi + 1)
            deps = [sfree[b], qbd_tok[qi]]
            if kind == "c":
                if kh == 0:
                    prep_cache_late(kidx)
                    prep_cache_late(kidx + 1)
                deps.append(cache_ready[kidx][0])
            else:
                deps.append(kT_tok)
            P.PE.wait(deps)
            sv4 = PSb(b)[:nkeys, :].rearrange("p (a b) -> p a b", a=4)
            qb_ = qbd[qi % 2]
            for g2 in range(2):
                ch = 2 * kh + g2
                if kind == "c":
                    lhsT = kcT[kidx % 4][:, kh, :]
                else:
                    kcol = tiles[kidx][2]
                    lhsT = kT[:, kh, kcol:kcol + nkeys]
                if nq == 128:
                    ins = nc.tensor.matmul(PSb(b)[:nkeys, 256 * g2:256 * g2 + 256], lhsT=lhsT, rhs=qb_[:, ch, :, :].rearrange("p a b -> p (a b)"),
                                           start=True, stop=True, skip_group_check=True)
                else:
                    for e in range(2):
                        ins = nc.tensor.matmul(sv4[:, 2 * g2 + e, 0:nq], lhsT=lhsT, rhs=qb_[:, ch, e, 0:nq],
                                               start=True, stop=True, skip_group_check=True)
            qk_tok[u] = P.PE.sig(ins)
            last_qk_of_tile[qi] = qk_tok[u]

        em_tok = {}

        def emit_em(u):
            qi, ti, nq, xc, ki, nk, kspec, kh = units[u]
            b = u % 4
            kind, kidx, nkeys, mi = kspec
            sv = PSb(b)[:nkeys, :].rearrange("p (a b) -> p a b", a=4)[:, :, 0:nq]
            pv = pT[b][:nkeys, :, 0:nq]
            te = P.A([qk_tok[u], pfree[b]], "activation", pv, sv, AF.Exp)
            sfree[b] = [te]
            em_tok[u] = P.V([te], "tensor_tensor", pv, pv, maskb[:nkeys, mi, 0:nq].unsqueeze(1).to_broadcast([nkeys, 4, nq]), ALU.mult)

        def emit_pv(u):
            qi, ti, nq, xc, ki, nk, kspec, kh = units[u]
            b = u % 4
            kind, kidx, nkeys, mi = kspec
            deps = [em_tok[u]]
            if ki == 0 and kh == 0:
                deps.append(ofree[0])
            if kind == "c":
                deps.append(cache_ready[kidx][1])
            else:
                deps.append(v_tok)
            P.PE.wait(deps)
            for g4 in range(4):
                h = 4 * kh + g4
                kvh = kh
                ob, osl = divmod(h, 6)
                if kind == "c":
                    rhs = vca[kidx % 4][:, kvh * 80:kvh * 80 + 66]
                else:
                    rhs = vaug[:nkeys, kidx, kvh * 80:kvh * 80 + 66]
                ins = nc.tensor.matmul(PSb(4 + ob)[:nq, osl * 72:osl * 72 + 66], lhsT=pT[b][:nkeys, g4, 0:nq], rhs=rhs,
                                       start=(ki == 0 and h in (0, 6, 12)), stop=(ki == nk - 1), skip_group_check=True)
            tpv = P.PE.sig(ins)
            pfree[b] = [tpv]
            if kind == "c" and kh == 3:
                cfree[kidx % 4] = [tpv]
            return tpv

        att_tok = []
        LOOK = 3
        for n0 in range(4):
            prep_dma(n0)
        for n0 in range(3):
            prep_cache(n0)
        for u in range(min(LOOK, len(units))):
            emit_qk(u)
        emit_em(0)
        for u in range(len(units)):
            if u + LOOK < len(units):
                emit_qk(u + LOOK)
            if u + 1 < len(units):
                emit_em(u + 1)
            tpv = emit_pv(u)
            qi, ti, nq, xc, ki, nk, kspec, kh = units[u]
            if self.upto == f"P4u{u}":
                raise _Stop()
            if kspec[0] == "c" and kh == 3:
                prep_dma(kspec[1] + 4)
                prep_cache(kspec[1] + 3)
            if ki == nk - 1 and kh == 3:
                s = qi % 2
                tn = []
                hb_ = [(0, 6), (6, 6), (12, 4)]
                ovs = [PSb(4 + ob)[:nq, 0:nh_ * 72].rearrange("p (h e) -> p h e", e=72) for ob, (h0, nh_) in enumerate(hb_)]
                tas = [P.V([tpv], "tensor_tensor", den[s][:nq, h0:h0 + nh_], ovs[ob][:, :, 64], esink[:nq, h0:h0 + nh_], ALU.add)
                       for ob, (h0, nh_) in enumerate(hb_)]
                tbs = [P.V([tas[ob]], "reciprocal", den[s][:nq, h0:h0 + nh_], den[s][:nq, h0:h0 + nh_]) for ob, (h0, nh_) in enumerate(hb_)]
                for ob, (h0, nh_) in enumerate(hb_):
                    tn.append(P.V([tbs[ob]], "tensor_tensor", att[s][:nq, h0 * 64:(h0 + nh_) * 64].rearrange("p (h d) -> p h d", d=64),
                                  ovs[ob][:, :, 0:64], den[s][:nq, h0:h0 + nh_].unsqueeze(2).to_broadcast([nq, nh_, 64]), ALU.mult))
                ofree[0] = [tn]
                P.PE.wait(tn, tbfree[0])
                pb = PSb(7).bitcast(BF16).rearrange("p (a b) -> p a b", a=8)
                for c in range(8):
                    ins = nc.tensor.transpose(pb[:, c, 0:nq], att[s][:nq, c * 128:(c + 1) * 128], identb[:nq, :nq])
                tt = P.PE.sig(ins)
                tq = P.A([tt], "copy", mixT[:, 0:8, xc:xc + nq], pb[:, :, 0:nq])
                tbfree[0] = [tq]
                att_tok.append(tq)
                if self.upto == f"P4q{qi}":
                    P.dump("attT", mixT[:, 0:8, :], att_tok)
                    raise _Stop()
        P.dump("attT", mixT[:, 0:8, :], att_tok)
        P.phase_end("P4")
        p4done = [att_tok, ofree[0], tbfree[0], [sfree[b] for b in range(4)], [pfree[b] for b in range(4)], P.new_out_sem()]

        RX.reset()
        xT = RX.carve([128, KC, TX], F32)
        RH.reset()
        xt5 = [RH.carve([128, D], F32) for _ in range(2)]
        xfree = [[p4done, hT_tok], [p4done, hT_tok]]
        bk5 = [[p4done]] * 4
        xT_tok = []
        nb5 = 0
        for ti in range(1, 11):
            src, nr, col = tiles[ti]
            xc = col - NE
            s = ti % 2
            tld = P.load(P.SP, xld[s], xfree[s], xt5[s][:nr], src)
            for g in range(4):
                b = nb5 % 4
                nb5 += 1
                P.PE.wait(tld, bk5[b])
                pv = PSb(b).rearrange("p (a b) -> p a b", a=4)
                for c in range(4):
                    k = 4 * g + c
                    ins = nc.tensor.transpose(pv[:, c, 0:nr], xt5[s][:nr, k * 128:(k + 1) * 128], identf[:nr, :nr])
                tt = P.PE.sig(ins)
                if g % 2 == 0:
                    tc5 = P.A([tt, p4done], "copy", xT[:, 4 * g:4 * g + 4, xc:xc + nr], pv[:, :, 0:nr])
                else:
                    tc5 = P.V([tt, p4done], "tensor_copy", xT[:, 4 * g:4 * g + 4, xc:xc + nr], pv[:, :, 0:nr])
                bk5[b] = [tc5]
                xT_tok.append(tc5)
            xfree[s] = [tt]
        P.dump("xT0", xT, xT_tok)
        P.phase_end("P5")

        self.yset = 0
        self.yfree = [[xT_tok, bk5], [xT_tok, bk5]]

        def proj_add(slab_iter, nk, act, act_tok, cpieces, xoff, scale_col=None, on_done=None):
            last = None
            pending = None
            for (wv, wtok, slot, d0, nd) in slab_iter:
                for dd in range(nd):
                    d = d0 + dd
                    ys = self.yset
                    self.yset ^= 1
                    P.PE.wait(wtok, act_tok, self.yfree[ys])
                    for k in range(nk):
                        for pi, (c0, n) in enumerate(cpieces):
                            ins = nc.tensor.matmul(PSb(3 * ys + pi)[:, 0:n], lhsT=wv[:, k, dd * 128:(dd + 1) * 128],
                                                   rhs=act[:, k, c0:c0 + n], start=(k == 0), stop=(k == nk - 1))
                    tmm = P.PE.sig(ins)
                    tks = []
                    for pi, (c0, n) in enumerate(cpieces):
                        xv = xT[:, d, xoff + c0 - cpieces[0][0]:xoff + c0 - cpieces[0][0] + n]
                        if scale_col is None:
                            tks.append(P.V([tmm, xT_tok], "tensor_tensor", xv, xv, PSb(3 * ys + pi)[:, 0:n], ALU.add))
                        else:
                            tks.append(P.V([tmm, xT_tok], "scalar_tensor_tensor", xv, PSb(3 * ys + pi)[:, 0:n],
                                           vT[:, scale_col + d:scale_col + d + 1], xv, ALU.mult, ALU.add))
                    self.yfree[ys] = [tks]
                    last = tks
                    if on_done is not None:
                        if pending is not None:
                            on_done(*pending)
                        pending = (d, tks)
                P.wrelease(slot, tmm)
            if on_done is not None and pending is not None:
                on_done(*pending)
            return last

        w_out3 = w_out.rearrange("(c p) n -> p c n", p=128)

        def wout_slabs():
            for s8 in range(8):
                wv, wtok, slot = P.wslab(w_out3[:, :, 256 * s8:256 * s8 + 256], KC, 256)
                yield wv, wtok, slot, 2 * s8, 2
        res_tok = proj_add(wout_slabs(), KC, mixT, [att_tok, ln_tok], xpieces, 0)
        res_tok = [res_tok, self.yfree]
        P.dump("xT1", xT, res_tok)
        P.phase_end("P6")

        def rms_stats(x_tok, sqbufs, rs):
            P.PE.wait(self.yfree)
            sfr = [[], []]
            for k in range(KC):
                if k % 2 == 0:
                    tq = P.A([x_tok, sfr[k % 2]], "activation", sqbufs[k % 2], xT[:, k, :], AF.Square)
                else:
                    tq = P.V([x_tok, sfr[k % 2]], "tensor_tensor", sqbufs[k % 2], xT[:, k, :], xT[:, k, :], ALU.mult)
                P.PE.wait(tq)
                for pi, (c0, n) in enumerate(xpieces):
                    ins = nc.tensor.matmul(PSb(pi)[:, 0:n], lhsT=onesb, rhs=sqbufs[k % 2][:, c0:c0 + n], start=(k == 0), stop=(k == KC - 1))
                sfr[k % 2] = [P.PE.sig(ins)]
            tr = []
            for pi, (c0, n) in enumerate(xpieces):
                tr.append(P.V([sfr[1]], "tensor_scalar", rs[:, c0:c0 + n], PSb(pi)[:, 0:n], 1.0, RMS_EPS, ALU.mult, ALU.add))
            tr2 = P.A([tr], "sqrt", rs, rs)
            tr3 = P.V([tr2], "reciprocal", rs, rs)
            self.yfree[0] = [self.yfree[0], tr]
            return tr3

        def ffn(layer, x_tok):
            RH.reset()
            hT2 = RH.carve([128, KC, TH], BF16)
            RA.reset()
            aT = RA.carve([128, 12, TX], BF16)
            sgb = [RA.carve([128, TX], F32) for _ in range(2)]
            sqb = [sgb[0].bitcast(BF16)[:, 0:TX], sgb[1].bitcast(BF16)[:, 0:TX]]
            rs = self.rsF
            t_rs = rms_stats(x_tok, sqb, rs)
            h_tok = []
            for k in range(KC):
                h_tok.append(P.V([t_rs, x_tok], "scalar_tensor_tensor", hT2[:, k, NE:TH], xT[:, k, :],
                                 vT[:, V_NFFN + 16 * layer + k:V_NFFN + 16 * layer + k + 1], rs, ALU.mult, ALU.mult))
            wg3 = w_gate[layer].rearrange("(c p) n -> p c n", p=128)
            wu3 = w_up[layer].rearrange("(c p) n -> p c n", p=128)
            wd3 = w_down[layer].rearrange("(c p) n -> p c n", p=128)
            x0 = 0 if layer == 0 else NH
            cp = [(NE + x0, 512), (NE + x0 + 512, 512), (NE + x0 + 1024, TX - x0 - 1024)]
            dpieces = [(0, 512), (512, 512), (1024, TX - x0 - 1024)]
            sgfree = [[], []]
            a_free = [h_tok]
            last_res = x_tok
            gfree = [self.yfree]
            ufree = [self.yfree]
            for (f0, nf) in QUARTERS:
                a_tok = []
                for sp in range(nf // 2):
                    fs = f0 + 2 * sp
                    gs_ = P.wslab(wg3[:, :, 128 * fs:128 * fs + 256], KC, 256)
                    us_ = P.wslab(wu3[:, :, 128 * fs:128 * fs + 256], KC, 256)
                    for dd in range(2):
                        fi = 2 * sp + dd
                        s = fi % 2
                        P.PE.wait(gs_[1], gfree)
                        for k in range(KC):
                            P.PE.wait(h_tok[k])
                            for pi, (c0, n) in enumerate(cp):
                                ins = nc.tensor.matmul(PSb(pi)[:, 0:n], lhsT=gs_[0][:, k, dd * 128:(dd + 1) * 128], rhs=hT2[:, k, c0:c0 + n],
                                                       start=(k == 0), stop=(k == KC - 1))
                        tg = P.PE.sig(ins)
                        P.PE.wait(us_[1], ufree)
                        for k in range(KC):
                            for pi, (c0, n) in enumerate(cp):
                                ins = nc.tensor.matmul(PSb(3 + pi)[:, 0:n], lhsT=us_[0][:, k, dd * 128:(dd + 1) * 128], rhs=hT2[:, k, c0:c0 + n],
                                                       start=(k == 0), stop=(k == KC - 1))
                        tu = P.PE.sig(ins)
                        tsl = []
                        off = 0
                        for pi, (c0, n) in enumerate(cp):
                            tsl.append(P.A([tg, sgfree[s]], "activation", sgb[s][:, off:off + n], PSb(pi)[:, 0:n], AF.Silu))
                            off += n
                        tml = []
                        off = 0
                        for pi, (c0, n) in enumerate(cp):
                            tml.append(P.V([tu, tsl[pi], a_free], "tensor_tensor", aT[:, fi, off:off + n], sgb[s][:, off:off + n], PSb(3 + pi)[:, 0:n], ALU.mult))
                            off += n
                        sgfree[s] = [tml]
                        gfree = [tsl]
                        ufree = [tml]
                        a_tok.append(tml)
                    P.wrelease(gs_[2], tg)
                    P.wrelease(us_[2], tu)
                self.yfree = [[gfree], [ufree]]

                def wd_slabs():
                    for s8 in range(8):
                        wv, wtok, slot = P.wslab(wd3[:, f0:f0 + nf, 256 * s8:256 * s8 + 256], nf, 256)
                        yield wv, wtok, slot, 2 * s8, 2
                last_res = proj_add(wd_slabs(), nf, aT, a_tok, dpieces, x0,
                                    on_done=(self.out_cb if (layer == 1 and f0 == QUARTERS[-1][0]) else None))
                a_free = [self.yfree]
                gfree = [self.yfree[0]]
                ufree = [self.yfree[1]]
            return [last_res, self.yfree]

        self.RS = Region(P, "RS", TX)
        self.rsF = self.RS.carve([128, TX], F32)
        w_gate = P.din("w_gate", [2, D, DFF])
        w_up = P.din("w_up", [2, D, DFF])
        w_down = P.din("w_down", [2, DFF, D])
        x1_tok = ffn(0, res_tok)
        P.dump("xT2", xT, x1_tok)
        P.phase_end("P8")

        RA.reset()
        sqb = [RA.carve([128, TX], BF16) for _ in range(2)]
        hbp = [RA.carve([128, 1040], F32) for _ in range(2)]
        hbs = [RA.carve([128, NSEQ, 23], F32) for _ in range(2)]
        Sp = [RA.carve([128, 1040], F32) for _ in range(2)]
        Ss = [RA.carve([128, NSEQ, 23], F32) for _ in range(2)]
        spt = [[RA.carve([120, 128], F32) for _ in range(2)] for _ in range(2)]
        hsc = [RA.carve([128, 128], F32) for _ in range(2)]
        npo = [RA.carve([16, 128], F32) for _ in range(2)]
        nso = [RA.carve([128, 128], F32) for _ in range(2)]
        fixb = RA.carve([128, 16], F32)
        RH.reset()
        dpT = RH.carve([128, KC, NM + NS], BF16)
        rs = self.rsF
        pld = [DSem(P, f"pl{i}") for i in range(2)]
        pst = [DSem(P, f"pst{i}") for i in range(2)]
        self.out_sems += pst
        t_rs = rms_stats(x1_tok, sqb, rs)
        t_rs = P.V([t_rs], "tensor_scalar", rs[:, 1:16], rs[:, 1:16], hvalid[:, 0:1], 0.0, ALU.mult, ALU.add)
        hbfree = [[], []]
        sptfree = [[x1_tok], [x1_tok]]
        stgfree = [[], []]
        b6free = [self.yfree]
        b7free = [self.yfree]
        dp_tok = []
        self.yfree = [[self.yfree, b6free, b7free], [self.yfree, b6free, b7free]]
        ppieces = [(0, 512), (512, 512), (1024, 128)]
        pslab = [P.wslab(pool_w[g].rearrange("(c p) n -> p c n", p=128), 4, 512) for g in range(4)]
        pool_last = [None]
        pool_sched = []
        pmm = {}

        def pool_mm(g, e):
            wv, wtok, slot = pslab[g]
            d = 4 * g + e
            ys = self.yset
            self.yset ^= 1
            P.PE.wait(wtok, dp_tok[4 * g:4 * g + 4], self.yfree[ys])
            for cc in range(4):
                for pi, (c0, n) in enumerate(ppieces):
                    ins = nc.tensor.matmul(PSb(3 * ys + pi)[:, 0:n], lhsT=wv[:, cc, e * 128:(e + 1) * 128],
                                           rhs=dpT[:, 4 * g + cc, c0:c0 + n], start=(cc == 0), stop=(cc == 3))
            pmm[d] = (P.PE.sig(ins), ys)
            if e == 3:
                P.wrelease(slot, pmm[d][0])

        def pool_ev(g, e):
            d = 4 * g + e
            tmm, ys = pmm[d]
            tks = []
            for pi, (c0, n) in enumerate(ppieces):
                xv = xT[:, d, NH + c0:NH + c0 + n]
                tks.append(P.V([tmm, x1_tok], "scalar_tensor_tensor", xv, PSb(3 * ys + pi)[:, 0:n],
                               vT[:, V_PSC + d:V_PSC + d + 1], xv, ALU.mult, ALU.add))
            self.yfree[ys] = [tks]
            pool_last[0] = tks

        def pool_steps(g):
            return [lambda: (pool_mm(g, 0), pool_mm(g, 1)),
                    lambda: (pool_ev(g, 0), pool_ev(g, 1), pool_mm(g, 2), pool_mm(g, 3)),
                    lambda: (pool_ev(g, 2), pool_ev(g, 3))]

        def pool_loads(k):
            s = k % 2
            for r in range(2):
                tk = P.load(P.SP, pld[s], sptfree[s] if r == 0 else [], spt[s][r],
                            spool[8 * r:8 * r + 8, :, k * 128:(k + 1) * 128].rearrange("n i c -> (n i) c"))
            return tk

        for k in range(KC):
            s = k % 2
            g = k // 4
            w = (2, 4, 8, 16)[g]
            gcol = vT[:, V_NMIX + 16 + k:V_NMIX + 16 + k + 1]
            if k == 0:
                tsp_next = pool_loads(0)
            tsp = tsp_next
            if k + 1 < KC:
                tsp_next = pool_loads(k + 1)
            t0 = P.V([t_rs, hbfree[s]], "scalar_tensor_tensor", hbp[s][:, 0:1039], xT[:, k, 1:1040], gcol, rs[:, 1:1040], ALU.mult, ALU.mult)
            t0b = t0
            t1_ = P.V([], "scalar_tensor_tensor", hbs[s][:, :, 15:23], xT[:, k, 1040:1168].rearrange("p (a b) -> p a b", a=NSEQ), gcol,
                      rs[:, 1040:1168].rearrange("p (a b) -> p a b", a=NSEQ), ALU.mult, ALU.mult)
            tcs = P.A([t1_, stgfree[s]], "copy", hsc[s].rearrange("p (a b) -> p a b", a=NSEQ), hbs[s][:, :, 15:23])
            P.PE.wait(tsp, b6free)
            for r in range(2):
                ins = nc.tensor.transpose(PSb(6)[:, r * 120:(r + 1) * 120], spt[s][r], identf[0:120, 0:120])
            tt = P.PE.sig(ins)
            sptfree[s] = [tt]
            t2_ = P.A([tt, t1_], "copy", hbs[s][:, :, 0:15], PSb(6)[:, 0:240].rearrange("p (a b) -> p a b", a=NSEQ))
            b6free = [t2_]
            P.PE.wait(t0b, tcs, b7free)
            nc.tensor.transpose(PSb(7)[0:16, 0:128], hbp[s][:, 1023:1039], identf)
            ins = nc.tensor.transpose(PSb(7)[:, 128:256], hsc[s], identf)
            tn7 = P.PE.sig(ins)
            ta7 = P.A([tn7, stgfree[s]], "copy", npo[s], PSb(7)[0:16, 0:128])
            tnp = P.A([], "copy", nso[s], PSb(7)[:, 128:256])
            b7free = [tnp]
            st1 = P.store([ta7, tnp], np_p[:, k * 128:(k + 1) * 128], npo[s][1:16, :], pst[s])
            st2 = P.store([], np_s[:, 7:15, k * 128:(k + 1) * 128], nso[s], pst[s])
            stgfree[s] = [tn7, st2]
            curp, curs = hbp[s], hbs[s]
            tp_, ts_ = [t0b], [t2_, t1_]
            sh = 1
            bi = 0
            while sh < w:
                op_, os_ = Sp[bi % 2], Ss[bi % 2]
                tp_ = [P.V([tp_], "tensor_tensor", op_[:, sh:1039], curp[:, sh:1039], curp[:, 0:1039 - sh], ALU.add)]
                ts_ = [P.V([ts_], "tensor_tensor", os_[:, :, sh:23], curs[:, :, sh:23], curs[:, :, 0:23 - sh], ALU.add)]
                curp, curs = op_, os_
                sh *= 2
                bi += 1
            td = P.V([tp_], "scalar_tensor_tensor", dpT[:, k, 0:NM], curp[:, 15:1039], 1.0 / w, hbp[s][:, 15:1039], ALU.mult, ALU.subtract)
            tf1 = P.V([tp_], "tensor_tensor", fixb, curp[:, 15:31], invc[:, g, :], ALU.mult)
            tf2 = P.V([tf1, td], "tensor_tensor", dpT[:, k, 0:16], fixb, hbp[s][:, 15:31], ALU.subtract)
            te_ = P.V([ts_], "scalar_tensor_tensor", dpT[:, k, NM:NM + NS].rearrange("p (a b) -> p a b", a=NSEQ), curs[:, :, 15:23], 1.0 / w,
                      hbs[s][:, :, 15:23], ALU.mult, ALU.subtract)
            hbfree[s] = [tn7]
            dp_tok.append([tf2, te_, td])
            if pool_sched:
                pool_sched.pop(0)()
            if k % 4 == 3:
                pool_sched += pool_steps(k // 4)
        P.store([], np_s[:, 0:7, :], spool[:, 8:15, :])
        P.dump("dpT", dpT, dp_tok)
        P.phase_end("P9")
        while pool_sched:
            pool_sched.pop(0)()
        last = pool_last[0]
        x2_tok = [last, self.yfree, [(d.sem, d.cnt) for d in pst]]
        P.dump("xT3", xT, x2_tok)
        P.phase_end("P10")
        ostg = self.RS.t[:, 0:1024].rearrange("p (a b) -> p a b", a=8)
        osem = DSem(P, "osem")
        self.out_sems.append(osem)
        ost = {"free": [], "b67": []}
        y_p3 = y_p.rearrange("(t p) c -> p t c", p=128)

        def out_cb(d, tks):
            P.PE.wait(tks, ost["b67"])
            for t in range(8):
                ins = nc.tensor.transpose(PSb(6 + t // 4)[:, (t % 4) * 128:(t % 4) * 128 + 128], xT[:, d, NH + 128 * t:NH + 128 * t + 128], identf)
            tt = P.PE.sig(ins)
            ta = P.A([tt, ost["free"]], "copy", ostg[:, 0:4, :], PSb(6).rearrange("p (a b) -> p a b", a=4))
            tb2 = P.V([tt, ost["free"]], "tensor_copy", ostg[:, 4:8, :], PSb(7).rearrange("p (a b) -> p a b", a=4))
            ost["b67"] = [ta, tb2]
            ost["free"] = [P.store([ta, tb2], y_p3[:, :, d * 128:(d + 1) * 128], ostg, osem)]
        self.out_cb = out_cb
        x3_tok = ffn(1, x2_tok)
        x3_tok = [x3_tok, ost["b67"], ost["free"]]

        RH.reset()
        yt = [RH.carve([128, D], F32) for _ in range(2)]
        ysem = [DSem(P, f"y{i}") for i in range(2)]
        ytfree = [[x3_tok], [x3_tok]]
        bk = [[x3_tok]] * 4
        nb = 0
        for oi in range(8, 9):
            s = oi % 2
            xc = NH + 128 * oi
            dst = y_p[128 * oi:128 * oi + 128, :] if oi < 8 else y_s[:, :]
            tcs = []
            for g in range(4):
                b = nb % 4
                nb += 1
                P.PE.wait(x3_tok, bk[b])
                pv = PSb(b).rearrange("p (a b) -> p a b", a=4)
                for c in range(4):
                    k = 4 * g + c
                    ins = nc.tensor.transpose(pv[:, c, :], xT[:, k, xc:xc + 128], identf)
                tt = P.PE.sig(ins)
                if g % 2 == 0:
                    tc = P.A([tt, ytfree[s]], "copy", yt[s][:, 512 * g:512 * g + 512], PSb(b))
                else:
                    tc = P.V([tt, ytfree[s]], "tensor_copy", yt[s][:, 512 * g:512 * g + 512], PSb(b))
                bk[b] = [tc]
                tcs.append(tc)
            P.SP.wait(tcs)
            tok = ysem[s].add(nc.sync.dma_start(out=dst, in_=yt[s]))
            ytfree[s] = [tok]
        P.SP.wait(ytfree)


def _prep_inputs(inp):
    f = lambda a: np.ascontiguousarray(np.asarray(a, dtype=np.float32))
    x_prompt, x_sample = f(inp["x_prompt"]), f(inp["x_sample"])
    vecs = np.concatenate([f(inp["norm_mix"]).reshape(32, 128), f(inp["norm_ffn"]).reshape(32, 128),
                           f(inp["pool_scale"]).reshape(16, 128), f(inp["conv_b"]).reshape(8, 128),
                           f(inp["conv_ln_g"]).reshape(8, 128), f(inp["conv_ln_b"]).reshape(8, 128)], 0)
    shared = {
        "vecs": f(vecs), "convw": f(inp["conv_w"][0]), "qkg": f(np.stack([inp["q_norm"][0], inp["k_norm"][0]])),
        "sinks": f(inp["sinks"][0]), "w_in": f(inp["w_in"][0]), "w_out": f(inp["w_out"][0]), "pool_w": f(inp["pool_w"][0]),
        "w_gate": f(inp["w_gate"]), "w_up": f(inp["w_up"]), "w_down": f(inp["w_down"]),
    }
    j = np.arange(128)[:, None]
    i = np.arange(128)[None, :]
    mp = (j > i).astype(np.float32)
    mc = (j <= i).astype(np.float32)
    in_maps = []
    for r in range(8):
        b, q = divmod(r, 4)
        c = 1024 * q
        lo = c - (NE + NH)
        xp = np.zeros((NE + NH + NM, D), np.float32)
        s0 = max(lo, 0)
        xp[s0 - lo:] = x_prompt[b, s0:c + NM]
        first = (q == 0)
        masks = np.zeros((128, NMASK, 128), np.float32)
        masks[:, 0], masks[:, 1] = mp, mc
        masks[:, 2] = 0.0 if first else mp
        masks[:, 3] = mc * ((j >= NH) if first else 1.0)
        for n in range(NSEQ):
            masks[:, 4 + n] = ((i // 8) == n) * (j > (i % 8))
        masks[:, 20] = ((j // 8) == (i // 8)) * ((j % 8) <= (i % 8))
        cst = np.zeros((128, 321), np.float32)
        cst[:, 0:128] = np.eye(128)
        cst[:, 128:256] = 1.0
        invc = np.zeros((4, 16), np.float32)
        for g, w in enumerate((2, 4, 8, 16)):
            pos = np.arange(16) + (0 if first else 10 ** 6)
            invc[g] = 1.0 / np.minimum(pos + 1, w)
        cst[:, 256:320] = invc.reshape(1, 64)
        cst[:, 320] = 0.0 if first else 1.0
        m = dict(shared)
        m.update({
            "xp": xp, "xs": f(x_sample[16 * r:16 * r + 16].reshape(128, D)),
            "ck": f(inp["cache_k"][0, 16 * r:16 * r + 16].reshape(16, 128, 256)),
            "cv": f(inp["cache_v"][0, 16 * r:16 * r + 16].reshape(16, 128, 256)),
            "sconv": f(inp["state_conv"][0, 16 * r:16 * r + 16]), "spool": f(inp["state_pool"][0, 16 * r:16 * r + 16]),
            "masks": masks, "cst": cst,
        })
        in_maps.append(m)
    return in_maps


_CACHE = {}


def kernel(**inp):
    debug = tuple(inp.pop("_debug", ()))
    in_maps = _prep_inputs(inp)
    key = debug
    if key not in _CACHE:
        _CACHE[key] = Prog(debug)
    prog = _CACHE[key]
    in_maps = [{k: v for k, v in m.items() if k in prog.in_names} for m in in_maps]
    res = run_bass_kernel_spmd(prog.nc, in_maps, core_ids=list(range(8)))
    R = res.results
    y_prompt = np.zeros((2, 4096, D), np.float32)
    for r in range(8):
        b, q = divmod(r, 4)
        y_prompt[b, 1024 * q:1024 * q + 1024] = R[r]["y_p"]
    y_sample = np.concatenate([R[r]["y_s"].reshape(16, 8, D) for r in range(8)], 0)
    last = [3, 7]
    nkp = np.stack([R[r]["nk_p"].reshape(128, 4, 64) for r in last])[None]
    nvp = np.stack([R[r]["nv_p"].reshape(128, 4, 64) for r in last])[None]
    ncp = np.stack([R[r]["nc_p"] for r in last])[None]
    npp = np.stack([R[r]["np_p"] for r in last])[None]
    nks = np.concatenate([R[r]["nk_s"].reshape(16, 128, 4, 64) for r in range(8)], 0)[None]
    nvs = np.concatenate([R[r]["nv_s"].reshape(16, 128, 4, 64) for r in range(8)], 0)[None]
    ncs = np.concatenate([R[r]["nc_s"] for r in range(8)], 0)[None]
    nps = np.concatenate([R[r]["np_s"] for r in range(8)], 0)[None]
    outs = (y_prompt, y_sample, nkp, nvp, ncp, npp, nks, nvs, ncs, nps)
    outs = tuple(np.ascontiguousarray(o.astype(np.float32)) for o in outs)
    if debug:
        return outs, [{k: R[r]["dbg_" + k] for k in prog.dbg_outs} for r in range(8)]
    return outs
```
